# Optimizing a Trainium2 kernel written in Bass

```python
import math
import jax, jax.numpy as jnp
from jax import lax
import numpy as np

D_MODEL = 4096
BATCH = 4
SEQ = 4096
DEPTH = 1

CHUNK = 64
Q_BLOCK = 128
HEAD_DIM = 128
N_FOX_HEADS = D_MODEL // 256
N_DIFF_HEADS = D_MODEL // 512
FOX_WIDTH = N_FOX_HEADS * HEAD_DIM
DIFF_WIDTH = N_DIFF_HEADS * 2 * HEAD_DIM
D_FF = ((8 * D_MODEL // 3 + 255) // 256) * 256
ROPE_THETA = 500000.0
ROPE_DIM = HEAD_DIM // 4
N_MOD = 9
EPS = 1e-6
IN_SIZES = (FOX_WIDTH, FOX_WIDTH, FOX_WIDTH, N_FOX_HEADS,
            DIFF_WIDTH, DIFF_WIDTH, DIFF_WIDTH, D_MODEL, D_MODEL)
IN_COLS = 3 * FOX_WIDTH + N_FOX_HEADS + 3 * DIFF_WIDTH + 2 * D_MODEL

kernel_name = 'hybrid_fox_diff_macaron_adaln_block'


def rms_norm(x, g):
    xf = x.astype(jnp.float32)
    y = xf * lax.rsqrt(jnp.mean(xf * xf, axis=-1, keepdims=True) + EPS)
    return (y * g.astype(jnp.float32)).astype(x.dtype)


def modulate(h, shift, scale):
    return h * (1 + scale[:, None, :]) + shift[:, None, :]


def swiglu(h, w_in, w_out):
    gate, up = jnp.split(h @ w_in, 2, axis=-1)
    return (jax.nn.silu(gate) * up) @ w_out


def rope_tables(positions):
    inv_freq = ROPE_THETA ** (-jnp.arange(0, ROPE_DIM, 2, dtype=jnp.float32) / ROPE_DIM)
    ang = positions.astype(jnp.float32)[..., None] * inv_freq
    return jnp.cos(ang), jnp.sin(ang)


def apply_partial_rope(t, cos, sin):
    cos = cos[:, :, None, None, :]
    sin = sin[:, :, None, None, :]
    half = ROPE_DIM // 2
    r1 = t[..., :half].astype(jnp.float32)
    r2 = t[..., half:ROPE_DIM].astype(jnp.float32)
    rotated = jnp.concatenate([r1 * cos - r2 * sin, r2 * cos + r1 * sin], axis=-1).astype(t.dtype)
    return jnp.concatenate([rotated, t[..., ROPE_DIM:]], axis=-1)


def fox_attention(q, k, v, log_f_cum):
    S = q.shape[2]
    scale = q.shape[-1] ** -0.5
    outs = []
    for i in range(S // Q_BLOCK):
        q0, q1 = i * Q_BLOCK, (i + 1) * Q_BLOCK
        logits = jnp.einsum('bhqd,bhkd->bhqk', q[:, :, q0:q1], k[:, :, :q1],
                            preferred_element_type=jnp.float32) * scale
        logits = logits + log_f_cum[:, :, q0:q1, None] - log_f_cum[:, :, None, :q1]
        t_idx = jnp.arange(q0, q1)[:, None]
        s_idx = jnp.arange(q1)[None, :]
        logits = jnp.where(s_idx <= t_idx, logits, -jnp.inf)
        p = jax.nn.softmax(logits, axis=-1)
        outs.append(jnp.einsum('bhqk,bhkd->bhqd', p.astype(v.dtype), v[:, :, :q1]))
    return jnp.concatenate(outs, axis=2)


def diff_attention(q, k, v, lam):
    S = q.shape[3]
    scale = q.shape[-1] ** -0.5
    outs = []
    for i in range(S // Q_BLOCK):
        q0, q1 = i * Q_BLOCK, (i + 1) * Q_BLOCK
        logits = jnp.einsum('bhnqd,bhnkd->bhnqk', q[:, :, :, q0:q1], k[:, :, :, :q1],
                            preferred_element_type=jnp.float32) * scale
        t_chunk = jnp.arange(q0, q1)[:, None] // CHUNK
        s_chunk = jnp.arange(q1)[None, :] // CHUNK
        logits = jnp.where(s_chunk <= t_chunk, logits, -jnp.inf)
        p = jax.nn.softmax(logits, axis=-1)
        attn = p[:, :, 0] - lam * p[:, :, 1]
        outs.append(jnp.einsum('bhqk,bhkd->bhqd', attn.astype(v.dtype), v[:, :, :q1]))
    return jnp.concatenate(outs, axis=2)


def setup_inputs(seed: int = 0) -> dict:
    key = jax.random.key(seed)
    ks = jax.random.split(key, 24)
    f32 = jnp.float32

    def w(k, shape, fan_in):
        return jax.random.normal(k, shape, f32) * (fan_in ** -0.5)

    def gain(k, shape):
        return 1.0 + 0.1 * jax.random.normal(k, shape, f32)

    def small(k, shape, s=0.02):
        return s * jax.random.normal(k, shape, f32)

    x = jax.random.normal(ks[0], (BATCH, SEQ, D_MODEL), f32)
    c = jax.random.normal(ks[1], (BATCH, D_MODEL), f32)
    offsets = jax.random.randint(ks[2], (BATCH, 1), 0, 16, dtype=jnp.int32) * CHUNK
    positions = (offsets + jnp.arange(SEQ, dtype=jnp.int32)[None, :]).astype(jnp.int32)
    return {
        'x': x,
        'c': c,
        'positions': positions,
        'ada_w': w(ks[3], (DEPTH, D_MODEL, N_MOD * D_MODEL), D_MODEL),
        'ada_b': small(ks[4], (DEPTH, N_MOD * D_MODEL)),
        'norm_ffn1': gain(ks[5], (DEPTH, D_MODEL)),
        'ffn1_w_in': w(ks[6], (DEPTH, D_MODEL, 2 * D_FF), D_MODEL),
        'ffn1_w_out': w(ks[7], (DEPTH, D_FF, D_MODEL), D_FF),
        'norm_mix': gain(ks[8], (DEPTH, D_MODEL)),
        'w_in': w(ks[9], (DEPTH, D_MODEL, IN_COLS), D_MODEL),
        'b_forget': 2.0 + 0.5 * jax.random.normal(ks[10], (DEPTH, N_FOX_HEADS), f32),
        'b_gate': small(ks[11], (DEPTH, 2 * D_MODEL), 0.1),
        'diff_lambda': 0.1 * jax.random.normal(ks[12], (DEPTH, 4, HEAD_DIM), f32),
        'diff_subln': gain(ks[13], (DEPTH, 2 * HEAD_DIM)),
        'w_o_fox': w(ks[14], (DEPTH, FOX_WIDTH, D_MODEL), FOX_WIDTH),
        'w_o_diff': w(ks[15], (DEPTH, DIFF_WIDTH, D_MODEL), DIFF_WIDTH),
        'w_out': w(ks[16], (DEPTH, D_MODEL, D_MODEL), D_MODEL),
        'norm_ffn2': gain(ks[17], (DEPTH, D_MODEL)),
        'ffn2_w_in': w(ks[18], (DEPTH, D_MODEL, 2 * D_FF), D_MODEL),
        'ffn2_w_out': w(ks[19], (DEPTH, D_FF, D_MODEL), D_FF),
        'final_ada_w': w(ks[20], (D_MODEL, 2 * D_MODEL), D_MODEL),
        'final_ada_b': small(ks[21], (2 * D_MODEL,)),
        'norm_final': gain(ks[22], (D_MODEL,)),
    }


def reference(x, c, positions, ada_w, ada_b, norm_ffn1, ffn1_w_in, ffn1_w_out, norm_mix,
              w_in, b_forget, b_gate, diff_lambda, diff_subln, w_o_fox, w_o_diff, w_out,
              norm_ffn2, ffn2_w_in, ffn2_w_out, final_ada_w, final_ada_b, norm_final):
    B, S, D = x.shape
    sc = jax.nn.silu(c)
    cos, sin = rope_tables(positions)
    split_points = []
    acc = 0
    for size in IN_SIZES[:-1]:
        acc += size
        split_points.append(acc)

    for l in range(DEPTH):
        lambda_init = 0.8 - 0.6 * math.exp(-0.3 * l)
        mod = sc @ ada_w[l] + ada_b[l]
        sh1, sc1, g1, sh2, sc2, g2, sh3, sc3, g3 = jnp.split(mod, N_MOD, axis=-1)

        h = modulate(rms_norm(x, norm_ffn1[l]), sh1, sc1)
        x = x + 0.5 * g1[:, None, :] * swiglu(h, ffn1_w_in[l], ffn1_w_out[l])

        h = modulate(rms_norm(x, norm_mix[l]), sh2, sc2)
        proj = h @ w_in[l]
        fq, fk, fv, ff, dq, dk, dv, ga, gb = jnp.split(proj, split_points, axis=-1)

        qa = fq.reshape(B, S, N_FOX_HEADS, HEAD_DIM).transpose(0, 2, 1, 3)
        ka = fk.reshape(B, S, N_FOX_HEADS, HEAD_DIM).transpose(0, 2, 1, 3)
        va = fv.reshape(B, S, N_FOX_HEADS, HEAD_DIM).transpose(0, 2, 1, 3)
        log_f = jax.nn.log_sigmoid(ff.astype(jnp.float32) + b_forget[l].astype(jnp.float32))
        log_f_cum = jnp.cumsum(log_f, axis=1).transpose(0, 2, 1)
        ya = fox_attention(qa, ka, va, log_f_cum)
        ya = ya.transpose(0, 2, 1, 3).reshape(B, S, FOX_WIDTH)

        qb = apply_partial_rope(dq.reshape(B, S, N_DIFF_HEADS, 2, HEAD_DIM), cos, sin)
        kb = apply_partial_rope(dk.reshape(B, S, N_DIFF_HEADS, 2, HEAD_DIM), cos, sin)
        qb = qb.transpose(0, 2, 3, 1, 4)
        kb = kb.transpose(0, 2, 3, 1, 4)
        vb = dv.reshape(B, S, N_DIFF_HEADS, 2 * HEAD_DIM).transpose(0, 2, 1, 3)
        lam_p = diff_lambda[l].astype(jnp.float32)
        lam = (jnp.exp(jnp.sum(lam_p[0] * lam_p[1])) - jnp.exp(jnp.sum(lam_p[2] * lam_p[3]))
               + lambda_init)
        yb = diff_attention(qb, kb, vb, lam)
        yb = rms_norm(yb, diff_subln[l]) * (1 - lambda_init)
        yb = yb.transpose(0, 2, 1, 3).reshape(B, S, DIFF_WIDTH)

        gate_a, gate_b = jnp.split(jnp.concatenate([ga, gb], axis=-1) + b_gate[l], 2, axis=-1)
        merged = jax.nn.sigmoid(gate_a) * (ya @ w_o_fox[l]) + jax.nn.sigmoid(gate_b) * (yb @ w_o_diff[l])
        x = x + g2[:, None, :] * (merged @ w_out[l])

        h = modulate(rms_norm(x, norm_ffn2[l]), sh3, sc3)
        x = x + 0.5 * g3[:, None, :] * swiglu(h, ffn2_w_in[l], ffn2_w_out[l])

    shf, scf = jnp.split(sc @ final_ada_w + final_ada_b, 2, axis=-1)
    return modulate(rms_norm(x, norm_final), shf, scf)
```

```python
import math
from contextlib import ExitStack
import numpy as np
import ml_dtypes
import concourse.bass as bass
import concourse.mybir as mybir
from concourse.bass_utils import run_bass_kernel_spmd

F32 = mybir.dt.float32
BF16 = mybir.dt.bfloat16
I32 = mybir.dt.int32
AF = mybir.ActivationFunctionType
ALU = mybir.AluOpType
AX = mybir.AxisListType
EPS = 1e-6
ENGS = ("pe", "act", "dve", "pool", "sp")
FULL = dict(D=4096, S=4096, DFF=11008, TB=1024, NG=11, B=4)


class Op:
    __slots__ = ("eng", "fn", "deps", "sig", "sem", "val")


class Prog:
    def __init__(self, nc, stack):
        self.nc = nc
        self.stack = stack
        self.handles = {"pe": nc.tensor, "act": nc.scalar, "dve": nc.vector, "pool": nc.gpsimd, "sp": nc.sync}
        self.esem = {e: stack.enter_context(nc.semaphore("es_" + e)) for e in ENGS}
        self.ecount = {e: 0 for e in ENGS}
        self.dsem = {}
        self.dcnt = {}
        self.dlast = {}
        self.known = {e: {} for e in ENGS}
        self.lastop = {e: None for e in ENGS}
        self.reset()

    def reset(self):
        self.ops = {e: [] for e in ENGS}
        self.lw = {}
        self.rd = {}

    def op(self, eng, fn, r=(), w=(), dma=None):
        o = Op()
        o.eng = eng
        o.fn = fn
        o.sig = False
        o.sem = None
        o.val = 0
        deps = []
        for k in r:
            p = self.lw.get(k)
            if p is not None:
                deps.append(p)
        for k in w:
            p = self.lw.get(k)
            if p is not None:
                deps.append(p)
            deps.extend(self.rd.get(k, ()))
        if dma is not None:
            if dma not in self.dsem:
                self.dsem[dma] = self.stack.enter_context(self.nc.semaphore("ds%d" % len(self.dsem)))
                self.dcnt[dma] = 0
            o.sem = dma
            prev = self.dlast.get(dma)
            if prev is not None:
                deps.append(prev)
            self.dlast[dma] = o
            self.dcnt[dma] += 1
            o.val = 16 * self.dcnt[dma]
        seen = set()
        dd = []
        for d in deps:
            if d is o or id(d) in seen:
                continue
            seen.add(id(d))
            if d.sem is None and d.eng == "pe" and eng == "pe":
                continue
            if d.sem is None:
                d.sig = True
            dd.append(d)
        o.deps = dd
        for k in r:
            self.rd.setdefault(k, []).append(o)
        for k in w:
            self.lw[k] = o
            self.rd[k] = []
        self.ops[eng].append(o)
        self.lastop[eng] = o
        return o

    def dma(self, q, out, in_, r=(), w=(), sem=None, slow=False):
        def fn(h, out=out, in_=in_, slow=slow):
            if slow:
                return h.dma_start(out=out, in_=in_, allow_slow_non_contiguous=True)
            return h.dma_start(out=out, in_=in_)
        return self.op(q, fn, r=r, w=w, dma=sem)

    def barrier(self):
        lasts = [self.lastop[e] for e in ENGS if self.lastop[e] is not None]
        dl = list(self.dlast.values())
        for e in ENGS:
            o = Op()
            o.eng = e
            o.fn = None
            o.sig = False
            o.sem = None
            o.val = 0
            dd = []
            for d in lasts + dl:
                if d.eng == e and d.sem is None:
                    continue
                if d.sem is None:
                    d.sig = True
                dd.append(d)
            o.deps = dd
            self.ops[e].append(o)
        self.lw = {}
        self.rd = {}

    def emit(self):
        for e in ENGS:
            n = self.ecount[e]
            for o in self.ops[e]:
                if o.sem is None and o.sig:
                    n += 1
                    o.val = n
            self.ecount[e] = n
        with self.nc.Block() as block:
            def run(e, h):
                known = self.known[e]
                for o in self.ops[e]:
                    for d in o.deps:
                        if d.sem is not None:
                            sh = self.dsem[d.sem]
                            key = ("d", d.sem)
                        else:
                            sh = self.esem[d.eng]
                            key = ("e", d.eng)
                        if known.get(key, 0) >= d.val:
                            continue
                        h.wait_ge(sh, d.val)
                        known[key] = d.val
                    if o.fn is None:
                        continue
                    inst = o.fn(h)
                    if o.sem is not None:
                        inst.then_inc(self.dsem[o.sem], 16)
                    elif o.sig:
                        inst.then_inc(self.esem[e], 1)

            @block.tensor
            def _(h):
                run("pe", h)

            @block.scalar
            def _(h):
                run("act", h)

            @block.vector
            def _(h):
                run("dve", h)

            @block.gpsimd
            def _(h):
                run("pool", h)

            @block.sync
            def _(h):
                run("sp", h)
        self.reset()


def host_consts():
    p = np.arange(128)
    cb = np.zeros((128, 5 * 128), np.float32)
    cb[:, 0:128] = np.eye(128)
    cb[:, 128:256] = 1.0
    cb[:, 256:384] = (p[:, None] <= p[None, :])
    cb[:, 384:512] = 1.0 - ((p[:, None] >= 64) & (p[None, :] < 64))
    perm = np.zeros((128, 128), np.float32)
    for m in range(32):
        perm[(m + 16) % 32, m] = 1.0
    cb[:, 512:640] = perm
    cf = np.zeros((128, 8 + 256), np.float32)
    cf[:, 136:264] = np.where(p[:, None] <= p[None, :], 0.0, -30000.0)
    invf = np.zeros(128, np.float32)
    invf[:32] = (500000.0 ** (-(np.arange(0, 32, 2, dtype=np.float32)) / 32.0))[np.arange(32) % 16]
    cf[:, 0] = invf
    cf[:16, 1] = -1.0
    cf[16:32, 1] = 1.0
    cf[:, 8:136] = np.eye(128)
    return cb.astype(ml_dtypes.bfloat16), cf


def build(cfg):
    D, S, DFF, TB, NG = cfg["D"], cfg["S"], cfg["DFF"], cfg["TB"], cfg["NG"]
    KC = D // 128
    OWN = S // 2
    HF = D // 256
    HD = D // 512
    FW = HF * 128
    DW = HD * 256
    JC = DFF // 128
    NBLK = S // 128
    OB = OWN // 128
    QSC = 128.0 ** -0.5
    LINIT = 0.8 - 0.6 * math.exp(0.0)
    nc = bass.Bass("TRN2", target_bir_lowering=False)

    def din(name, shape, dt=F32):
        return nc.dram_tensor(name, list(shape), dt, kind="ExternalInput").ap()

    def dscr(name, shape, dt=F32):
        kind = "ExternalOutput" if cfg.get("dbgout") else "Internal"
        return nc.dram_tensor(name, list(shape), dt, kind=kind).ap()

    x_in = din("x", [S, D])
    c_in = din("c", [1, D])
    pos_in = din("pos", [1, S], I32)
    flag_in = din("flag", [128, 1])
    constb_in = din("constb", [128, 640], BF16)
    constf_in = din("constf", [128, 264])
    ada_w = din("ada_w", [D, 9 * D])
    ada_b = din("ada_b", [1, 9 * D])
    fin_w = din("final_ada_w", [D, 2 * D])
    fin_b = din("final_ada_b", [1, 2 * D])
    norms = din("norms", [4, D])
    f1_in = din("ffn1_w_in", [D, 2 * DFF])
    f1_out = din("ffn1_w_out", [DFF, D])
    f2_in = din("ffn2_w_in", [D, 2 * DFF])
    f2_out = din("ffn2_w_out", [DFF, D])
    w_in = din("w_in", [D, 3 * FW + HF + 3 * DW + 2 * D])
    b_forget = din("b_forget", [HF, 1])
    b_gate = din("b_gate", [1, 2 * D])
    dlam = din("diff_lambda", [1, 512])
    subln = din("diff_subln", [1, 256])
    w_of = din("w_o_fox", [FW, D])
    w_od = din("w_o_diff", [DW, D])
    w_o = din("w_out", [D, D])
    out = nc.dram_tensor("out", [OWN, D], F32, kind="ExternalOutput").ap()

    xs = dscr("xs", [S, D])
    modv = dscr("modv", [11, D])
    kfd = dscr("kfd", [FW, S], BF16)
    kdd = dscr("kdd", [DW, S], BF16)
    vfd = dscr("vfd", [S, FW], BF16)
    vdd = dscr("vdd", [S, DW], BF16)
    qfd = dscr("qfd", [FW, OWN], BF16)
    qdd = dscr("qdd", [DW, OWN], BF16)
    gAd = dscr("gAd", [D, OWN], BF16)
    gBd = dscr("gBd", [D, OWN], BF16)
    nld = dscr("nld", [HF, S])
    frd = dscr("frd", [HF, OWN])
    yad = dscr("yad", [FW, OWN], BF16)
    ybd = dscr("ybd", [DW, OWN], BF16)

    groups = []
    j = 0
    while j < JC:
        n = min(NG, JC - j)
        groups.append((j, n))
        j += n

    with ExitStack() as top:
        P = Prog(nc, top)

        uid = [0]

        def sb(st, name, shape, dt):
            uid[0] += 1
            return st.enter_context(nc.sbuf_tensor("%s_%d" % (name, uid[0]), list(shape), dt))

        def ps(st, name, shape, dt=F32):
            uid[0] += 1
            return st.enter_context(nc.psum_tensor("%s_%d" % (name, uid[0]), list(shape), dt))

        cb = sb(top, "cb", [128, 640], BF16)
        cf = sb(top, "cf", [128, 264], F32)
        flagc = sb(top, "flagc", [128, 1], F32)
        cols = sb(top, "cols", [128, 12, KC], F32)
        ident = cb[:, 0:128]
        ones = cb[:, 128:256]
        trim = cb[:, 256:384]
        chkm = cb[:, 384:512]
        perm = cb[:, 512:640]
        invf = cf[:, 0:1]
        sgn = cf[:, 1:2]
        identf = cf[:, 8:136]
        negtri = cf[:, 136:264]
        P.dma("sp", cb[:], constb_in[:, :], w=["cb"], sem="c0")
        P.dma("sp", cf[:], constf_in[:, :], w=["cf"], sem="c1")
        P.dma("sp", flagc[:], flag_in[:, :], w=["flagc"], sem="c2")

        def colview(v):
            return cols[:, v, :]

        with ExitStack() as st:
            ccol = sb(st, "ccol", [128, KC], F32)
            sccol = sb(st, "sccol", [128, KC], BF16)
            Wt = [sb(st, "mW%d" % i, [128, KC, 512], BF16) for i in range(2)]
            brow = sb(st, "brow", [1, D], F32)
            mrow = sb(st, "mrow", [1, D], F32)
            pm = [ps(st, "pm%d" % i, [128, 512]) for i in range(2)]
            P.dma("sp", ccol[:], c_in.rearrange("o (k p) -> p (o k)", p=128), w=["ccol"], sem="m_c", slow=True)
            P.op("act", lambda h: h.activation(out=sccol[:], in_=ccol[:], func=AF.Silu), r=["ccol"], w=["sccol"])
            it = 0
            for v in range(11):
                Wsrc, col0 = (ada_w, v * D) if v < 9 else (fin_w, (v - 9) * D)
                bsrc = ada_b if v < 9 else fin_b
                P.dma("sp", brow[:], bsrc[0:1, col0:col0 + D], w=["brow"], sem="m_b")
                for nt in range(D // 512):
                    s = it % 2
                    it += 1
                    src = Wsrc[:, col0 + nt * 512: col0 + (nt + 1) * 512].rearrange("(k p) n -> p k n", p=128)
                    P.dma("pool", Wt[s][:], src, w=[("mW", s)], sem=("mW", s))

                    def mm(h, s=s):
                        for k in range(KC):
                            i = h.matmul(pm[s][0:1, :], lhsT=sccol[:, k:k + 1], rhs=Wt[s][:, k, :],
                                         start=(k == 0), stop=(k == KC - 1))
                        return i
                    P.op("pe", mm, r=[("mW", s), "sccol"], w=[("pm", s)])
                    P.op("dve", lambda h, s=s, nt=nt: h.tensor_tensor(
                        out=mrow[0:1, nt * 512:(nt + 1) * 512], in0=pm[s][0:1, :],
                        in1=brow[0:1, nt * 512:(nt + 1) * 512], op=ALU.add),
                        r=[("pm", s), "brow"], w=[("mrow", nt)])
                P.dma("sp", modv[v:v + 1, :], mrow[:], r=[("mrow", nt) for nt in range(D // 512)],
                      w=[("modv", v)], sem="m_st")
            for v in range(11):
                P.dma("sp", cols[:, v, :], modv[v:v + 1, :].rearrange("o (k p) -> p (o k)", p=128),
                      r=[("modv", v)], w=[("cols", v)], sem="m_col", slow=True)
            P.barrier()
            P.emit()

        AB = sb(top, "AB", [128, 6, KC], F32)
        gcol = sb(top, "gcol", [128, 4, KC], F32)
        with ExitStack() as st:
            for i in range(4):
                P.dma("sp", gcol[:, i, :], norms[i:i + 1, :].rearrange("o (k p) -> p (o k)", p=128),
                      w=[("gcol", i)], sem="m_col", slow=True)
            for i in range(3):
                P.op("dve", lambda h, i=i: h.scalar_tensor_tensor(
                    out=AB[:, 2 * i, :], in0=cols[:, 3 * i + 1, :], scalar=1.0, in1=gcol[:, i, :],
                    op0=ALU.add, op1=ALU.mult), r=[("gcol", i)], w=[("AB", 2 * i)])
                P.op("dve", lambda h, i=i: h.tensor_copy(out=AB[:, 2 * i + 1, :], in_=cols[:, 3 * i, :]),
                     w=[("AB", 2 * i + 1)])
            P.barrier()
            P.emit()

        def norm_rows(P, src_fn, rkeys_fn, nrb, hT, Acol, Bcol, B):
            xt, ysb, ss, rs, tp = B["xt"], B["ysb"], B["ss"], B["rs"], B["tp"]
            XK = B.get("xk", ["scr0", "scr1"])
            cpb = min(8, KC)
            for rb in range(nrb):
                s = rb % 2
                P.dma("sp", xt, src_fn(rb), r=rkeys_fn(rb), w=XK, sem="xt")
                P.op("dve", lambda h, s=s: h.memset(ss[s][:], 0.0), w=[("ss", s)])
                P.op("act", lambda h, s=s: h.activation(out=ysb[s][:], in_=xt, func=AF.Square,
                                                        accum_out=ss[s][:]),
                     r=XK, w=[("y", s), ("ss", s)])
                P.op("dve", lambda h, s=s: h.tensor_scalar(out=rs[s][:], in0=ss[s][:], scalar1=1.0 / D,
                                                           scalar2=EPS, op0=ALU.mult, op1=ALU.add),
                     r=[("ss", s)], w=[("rs", s)])
                P.op("act", lambda h, s=s: h.sqrt(out=rs[s][:], in_=rs[s][:]), r=[("rs", s)], w=[("rs", s)])
                P.op("dve", lambda h, s=s: h.reciprocal(out=rs[s][:], in_=rs[s][:]), r=[("rs", s)], w=[("rs", s)])
                P.op("act", lambda h, s=s: h.activation(out=ysb[s][:], in_=xt, func=AF.Copy,
                                                        scale=rs[s][:, 0:1]),
                     r=XK + [("rs", s)], w=[("y", s)])
                for c0 in range(0, KC, cpb):
                    t = B["tpc"] % 2
                    B["tpc"] += 1

                    def tr(h, s=s, c0=c0, t=t):
                        for cc in range(cpb):
                            i = h.transpose(out=tp[t][:, cc * 128:(cc + 1) * 128],
                                            in_=ysb[s][:, (c0 + cc) * 128:(c0 + cc + 1) * 128], identity=ident)
                        return i
                    P.op("pe", tr, r=[("y", s)], w=[("tp", t)])

                    def ev(h, c0=c0, t=t, rb=rb):
                        for cc in range(cpb):
                            c = c0 + cc
                            i = h.tensor_scalar(out=hT[:, c, rb * 128:(rb + 1) * 128],
                                                in0=tp[t][:, cc * 128:(cc + 1) * 128],
                                                scalar1=Acol[:, c:c + 1], scalar2=Bcol[:, c:c + 1],
                                                op0=ALU.mult, op1=ALU.add)
                        return i
                    P.op("dve", ev, r=[("tp", t)], w=[("hT", rb)])

        def ffn_stage(P, tag, n_tok, first_src, W1, W2, Acol, Bcol, gvec):
            with ExitStack() as st:
                hT = sb(st, "hT", [128, KC, TB], BF16)
                aT = sb(st, "aT", [128, NG, TB], BF16)
                wi = [sb(st, "wi%d" % i, [128, KC, 128], BF16) for i in range(4)]
                wo = [sb(st, "wo%d" % i, [128, NG, 512], BF16) for i in range(2)]
                scr_full = sb(st, "scr", [128, max(D, 4096)], F32)
                scr = scr_full[:, 0:D]
                ysb = [sb(st, "ysb%d" % i, [128, D], BF16) for i in range(2)]
                gbt = sb(st, "gbt", [128, D], F32)
                sil = [sb(st, "sil%d" % i, [128, 512], F32) for i in range(2)]
                tmp = [sb(st, "tmp%d" % i, [128, 512], F32) for i in range(2)]
                ss = [sb(st, "ss%d" % i, [128, 1], F32) for i in range(2)]
                rs = [sb(st, "rs%d" % i, [128, 1], F32) for i in range(2)]
                pg = [ps(st, "pg%d" % i, [128, 512]) for i in range(2)]
                pu = [ps(st, "pu%d" % i, [128, 512]) for i in range(2)]
                po = [ps(st, "po%d" % i, [128, 512]) for i in range(2)]
                tp = [ps(st, "tp%d" % i, [128, 1024], BF16) for i in range(2)]
                B = dict(xt=scr, ysb=ysb, ss=ss, rs=rs, tp=tp, tpc=0, xk=["scr0", "scr1", "scr2", "scr3"])
                RBG = 2
                NSL = 4
                scrv = [scr_full[:, i * 1024:(i + 1) * 1024].rearrange("p (r n) -> p r n", r=RBG)
                        for i in range(NSL)]
                P.dma("sp", gbt[:], modv[gvec:gvec + 1, :].partition_broadcast(128), w=["gbt"], sem="gbt")
                P.op("act", lambda h: h.mul(out=gbt[:], in_=gbt[:], mul=0.5), r=["gbt"], w=["gbt"])
                cnt = 0
                ocnt = 0
                xcnt = 0
                pcnt = 0
                NTT = TB // 512
                for blk in range(n_tok // TB):
                    tok0 = blk * TB
                    src0 = first_src
                    norm_rows(P, lambda rb: src0[tok0 + rb * 128: tok0 + (rb + 1) * 128, :],
                              lambda rb: [("xs", (tok0 + rb * 128) // 256, n) for n in range(D // 512)],
                              TB // 128, hT, Acol, Bcol, B)
                    for g, (j0, ng) in enumerate(groups):
                        for jl in range(ng):
                            jj = j0 + jl
                            sg = (2 * cnt) % 4
                            su = (2 * cnt + 1) % 4
                            cnt += 1
                            P.dma("pool", wi[sg][:], W1[:, jj * 128:(jj + 1) * 128].rearrange(
                                "(k p) n -> p k n", p=128), w=[("wi", sg)], sem=("wi", sg))
                            P.dma("pool", wi[su][:], W1[:, DFF + jj * 128: DFF + (jj + 1) * 128].rearrange(
                                "(k p) n -> p k n", p=128), w=[("wi", su)], sem=("wi", su))
                            for tt in range(NTT):
                                b2 = tt % 2
                                hk = [("hT", rb) for rb in range(tt * 4, tt * 4 + 4)]

                                def mmg(h, sl=sg, tt=tt, dst=pg[b2]):
                                    for k in range(KC):
                                        i = h.matmul(dst[:], lhsT=wi[sl][:, k, :],
                                                     rhs=hT[:, k, tt * 512:(tt + 1) * 512],
                                                     start=(k == 0), stop=(k == KC - 1))
                                    return i

                                def mmu(h, sl=su, tt=tt, dst=pu[b2]):
                                    for k in range(KC):
                                        i = h.matmul(dst[:], lhsT=wi[sl][:, k, :],
                                                     rhs=hT[:, k, tt * 512:(tt + 1) * 512],
                                                     start=(k == 0), stop=(k == KC - 1))
                                    return i
                                P.op("pe", mmg, r=[("wi", sg)] + hk, w=[("pg", b2)])
                                P.op("pe", mmu, r=[("wi", su)] + hk, w=[("pu", b2)])
                                P.op("act", lambda h, b2=b2: h.activation(out=sil[b2][:], in_=pg[b2][:],
                                                                         func=AF.Silu),
                                     r=[("pg", b2)], w=[("sil", b2)])
                                P.op("dve", lambda h, b2=b2, jl=jl, tt=tt: h.tensor_tensor(
                                    out=aT[:, jl, tt * 512:(tt + 1) * 512], in0=sil[b2][:], in1=pu[b2][:],
                                    op=ALU.mult), r=[("sil", b2), ("pu", b2)], w=[("aT", jl, tt)])
                        base = first_src if g == 0 else xs
                        tasks = [(n, rg) for n in range(D // 512) for rg in range(TB // 128 // RBG)]

                        def xinfo(ti, base=base, tok0=tok0, xb=xcnt, tasks=tasks):
                            n, rg = tasks[ti]
                            r0 = tok0 + rg * RBG * 128
                            return n, rg, r0, (xb + ti) % NSL, ("xs", r0 // 256, n)

                        def xload(ti, base=base):
                            n, rg, r0, xsl, dk = xinfo(ti)
                            P.dma("sp", scrv[xsl], base[r0:r0 + RBG * 128, n * 512:(n + 1) * 512]
                                  .rearrange("(r p) n -> p r n", p=128), r=[dk], w=["scr%d" % xsl],
                                  sem=("scrl", xsl))
                        PRE = 2
                        for ti in range(min(PRE, len(tasks))):
                            xload(ti)
                        cur_n = -1
                        for ti in range(len(tasks)):
                            n, rg, r0, xsl, dk = xinfo(ti)
                            if n != cur_n:
                                cur_n = n
                                s = ocnt % 2
                                ocnt += 1
                                P.dma("pool", wo[s][:, 0:ng, :],
                                      W2[j0 * 128:(j0 + ng) * 128, n * 512:(n + 1) * 512]
                                      .rearrange("(j p) n -> p j n", p=128), w=[("wo", s)], sem=("wo", s))
                            if ti + PRE < len(tasks):
                                xload(ti + PRE)
                            for r4 in range(RBG):
                                rb = rg * RBG + r4
                                pb = pcnt % 2
                                pcnt += 1

                                def mmo(h, rb=rb, s=s, dst=po[pb], ng=ng):
                                    for jl in range(ng):
                                        i = h.matmul(dst[:], lhsT=aT[:, jl, rb * 128:(rb + 1) * 128],
                                                     rhs=wo[s][:, jl, :], start=(jl == 0), stop=(jl == ng - 1))
                                    return i
                                P.op("pe", mmo, r=[("wo", s)] + [("aT", jl, rb // 4) for jl in range(ng)],
                                     w=[("po", pb)])
                                P.op("dve", lambda h, pb=pb, n=n: h.tensor_tensor(
                                    out=tmp[pb][:], in0=po[pb][:], in1=gbt[:, n * 512:(n + 1) * 512],
                                    op=ALU.mult), r=[("po", pb), "gbt"], w=[("tmp", pb)])
                                P.op("dve", lambda h, pb=pb, xsl=xsl, r4=r4: h.tensor_tensor(
                                    out=scrv[xsl][:, r4, :], in0=tmp[pb][:], in1=scrv[xsl][:, r4, :],
                                    op=ALU.add), r=[("tmp", pb), "scr%d" % xsl], w=["scr%d" % xsl])
                            P.dma("sp", xs[r0:r0 + RBG * 128, n * 512:(n + 1) * 512]
                                  .rearrange("(r p) n -> p r n", p=128), scrv[xsl],
                                  r=["scr%d" % xsl], w=[dk], sem=("scrs", xsl))
                        xcnt += len(tasks)
                P.barrier()
                P.emit()

        UPTO = cfg.get("upto", 99)
        if UPTO < 1:
            return nc
        ffn_stage(P, "f1", S, x_in, f1_in, f1_out, AB[:, 0, :], AB[:, 1, :], 2)
        if UPTO < 2:
            return nc

        o_fq, o_fk, o_fv, o_ff = 0, FW, 2 * FW, 3 * FW
        o_dq = 3 * FW + HF
        o_dk, o_dv = o_dq + DW, o_dq + 2 * DW
        o_ga = o_dq + 3 * DW
        o_gb = o_ga + D
        with ExitStack() as st:
            hT = sb(st, "hT", [128, KC, TB], BF16)
            wi = [sb(st, "wi%d" % i, [128, KC, 128], BF16) for i in range(4)]
            wv = [sb(st, "wv%d" % i, [128, KC, 128], BF16) for i in range(2)]
            wf = sb(st, "wf", [128, KC, HF], BF16)
            scr = sb(st, "scr", [128, D], F32)[:, :]
            ysb = [sb(st, "ysb%d" % i, [128, D], BF16) for i in range(2)]
            ss = [sb(st, "ss%d" % i, [128, 1], F32) for i in range(2)]
            rs = [sb(st, "rs%d" % i, [128, 1], F32) for i in range(2)]
            posi = sb(st, "posi", [128, TB], I32)
            ang = sb(st, "ang", [128, TB], F32)
            ctab = sb(st, "ctab", [128, TB], F32)
            stab = sb(st, "stab", [128, TB], F32)
            rq = sb(st, "rq", [128, TB], F32)
            rr = sb(st, "rr", [128, TB], F32)
            rqi = sb(st, "rqi", [128, TB], I32)
            obf = [sb(st, "obf%d" % i, [128, 512], BF16) for i in range(3)]
            qb = [sb(st, "qb%d" % i, [128, 512], BF16) for i in range(2)]
            t1 = [sb(st, "t1%d" % i, [128, 512], F32) for i in range(2)]
            t2 = [sb(st, "t2%d" % i, [128, 512], F32) for i in range(2)]
            bgc = sb(st, "bgc", [128, 2 * KC], F32)
            nbf = sb(st, "nbf", [HF, 1], F32)
            ef = [sb(st, "ef%d" % i, [HF, 512], F32) for i in range(2)]
            vob = [sb(st, "vob%d" % i, [128, 128], BF16) for i in range(2)]
            pA = [ps(st, "pA%d" % i, [128, 512]) for i in range(2)]
            pB = [ps(st, "pB%d" % i, [128, 512]) for i in range(2)]
            pV = [ps(st, "pV%d" % i, [128, 512]) for i in range(2)]
            tp = [ps(st, "tp%d" % i, [128, 1024], BF16) for i in range(2)]
            B = dict(xt=scr, ysb=ysb, ss=ss, rs=rs, tp=tp, tpc=0)
            P.dma("sp", bgc[:], b_gate.rearrange("o (k p) -> p (o k)", p=128), w=["bgc"], sem="m_col", slow=True)
            P.dma("sp", nbf[:], b_forget[:, :], w=["nbf"], sem="m_col")
            P.op("dve", lambda h: h.tensor_scalar(out=nbf[:], in0=nbf[:], scalar1=-1.0, scalar2=None,
                                                  op0=ALU.mult), r=["nbf"], w=["nbf"])
            P.dma("pool", wf[:], w_in[:, o_ff:o_ff + HF].rearrange("(k p) n -> p k n", p=128), w=["wf"],
                  sem="wf", slow=True)
            cnt = 0
            acnt = 0
            ocnt = 0
            vcnt = 0
            NTT = TB // 512
            for blk in range(S // TB):
                tok0 = blk * TB
                own = tok0 < OWN
                norm_rows(P, lambda rb: xs[tok0 + rb * 128: tok0 + (rb + 1) * 128, :],
                          lambda rb: [], TB // 128, hT, AB[:, 2, :], AB[:, 3, :], B)
                P.dma("sp", posi[:], pos_in[0:1, tok0:tok0 + TB].partition_broadcast(128), w=["posi"], sem="posi")
                P.op("dve", lambda h: h.tensor_copy(out=ang[:], in_=posi[:]), r=["posi"], w=["ang"])
                P.op("dve", lambda h: h.tensor_scalar(out=ang[:], in0=ang[:], scalar1=invf, scalar2=None,
                                                      op0=ALU.mult), r=["ang"], w=["ang"])
                for tab, off, tk in ((stab, 0.0, "stab"), (ctab, 0.5 * math.pi, "ctab")):
                    P.op("dve", lambda h, off=off: h.tensor_scalar(out=rq[:], in0=ang[:], scalar1=off,
                                                                   scalar2=1.0 / (2 * math.pi), op0=ALU.add,
                                                                   op1=ALU.mult), r=["ang"], w=["rq"])
                    P.op("dve", lambda h: h.tensor_copy(out=rqi[:], in_=rq[:]), r=["rq"], w=["rqi"])
                    P.op("dve", lambda h: h.tensor_copy(out=rq[:], in_=rqi[:]), r=["rqi"], w=["rq"])
                    P.op("dve", lambda h, off=off: h.tensor_scalar(out=rr[:], in0=ang[:], scalar1=off, scalar2=None,
                                                                   op0=ALU.add), r=["ang"], w=["rr"])
                    P.op("dve", lambda h: h.scalar_tensor_tensor(out=rr[:], in0=rq[:], scalar=-2 * math.pi,
                                                                 in1=rr[:], op0=ALU.mult, op1=ALU.add),
                         r=["rq", "rr"], w=["rr"])
                    P.op("dve", lambda h: h.tensor_scalar(out=rq[:], in0=rr[:], scalar1=-math.pi, scalar2=1.0e6,
                                                          op0=ALU.add, op1=ALU.mult), r=["rr"], w=["rq"])
                    P.op("dve", lambda h: h.tensor_scalar(out=rq[:], in0=rq[:], scalar1=0.0, scalar2=1.0,
                                                          op0=ALU.max, op1=ALU.min), r=["rq"], w=["rq"])
                    P.op("dve", lambda h: h.scalar_tensor_tensor(out=rr[:], in0=rq[:], scalar=-2 * math.pi,
                                                                 in1=rr[:], op0=ALU.mult, op1=ALU.add),
                         r=["rq", "rr"], w=["rr"])
                    P.op("act", lambda h, tab=tab: h.activation(out=tab[:], in_=rr[:], func=AF.Sin),
                         r=["rr"], w=[tk])
                P.op("dve", lambda h: h.tensor_scalar(out=stab[:], in0=stab[:], scalar1=sgn, scalar2=None,
                                                      op0=ALU.mult), r=["stab"], w=["stab"])
                chunks = []
                for i in range(HF):
                    chunks.append(("k", o_fk + i * 128, kfd, i * 128))
                for i in range(2 * HD):
                    chunks.append(("kr", o_dk + i * 128, kdd, i * 128))
                if own:
                    for i in range(HF):
                        chunks.append(("q", o_fq + i * 128, qfd, i * 128))
                    for i in range(2 * HD):
                        chunks.append(("qr", o_dq + i * 128, qdd, i * 128))
                    for i in range(KC):
                        chunks.append(("g", o_ga + i * 128, gAd, i * 128, i))
                    for i in range(KC):
                        chunks.append(("g", o_gb + i * 128, gBd, i * 128, KC + i))
                hk_all = [("hT", rb) for rb in range(TB // 128)]
                SUB = cfg.get("sub", 9)
                if SUB < 1:
                    chunks = []
                if "kinds" in cfg:
                    chunks = [c_ for c_ in chunks if c_[0] in cfg["kinds"]]
                for ch in chunks:
                    kind, co, dst, ro = ch[0], ch[1], ch[2], ch[3]
                    sl = cnt % 4
                    cnt += 1
                    P.dma("pool", wi[sl][:], w_in[:, co:co + 128].rearrange("(k p) n -> p k n", p=128),
                          w=[("wi", sl)], sem=("wi", sl))
                    for tt in range(NTT):
                        a2 = acnt % 2
                        acnt += 1
                        hk = hk_all[tt * 4: tt * 4 + 4]

                        def mm(h, sl=sl, tt=tt, dstp=pA[a2]):
                            for k in range(KC):
                                i = h.matmul(dstp[:], lhsT=wi[sl][:, k, :], rhs=hT[:, k, tt * 512:(tt + 1) * 512],
                                             start=(k == 0), stop=(k == KC - 1))
                            return i
                        P.op("pe", mm, r=[("wi", sl)] + hk, w=[("pA", a2)])
                        ob = ocnt % 3
                        ocnt += 1
                        tsl = slice(tt * 512, (tt + 1) * 512)
                        dcol0 = tok0 + tt * 512
                        if kind == "k":
                            P.op("act", lambda h, a2=a2, ob=ob: h.copy(out=obf[ob][:], in_=pA[a2][:]),
                                 r=[("pA", a2)], w=[("obf", ob)])
                        elif kind == "q":
                            P.op("act", lambda h, a2=a2, ob=ob: h.mul(out=obf[ob][:], in_=pA[a2][:], mul=QSC),
                                 r=[("pA", a2)], w=[("obf", ob)])
                        elif kind == "g":
                            gi = ch[4]
                            P.op("act", lambda h, a2=a2, ob=ob, gi=gi: h.activation(
                                out=obf[ob][:], in_=pA[a2][:], func=AF.Sigmoid, bias=bgc[:, gi:gi + 1]),
                                r=[("pA", a2), "bgc"], w=[("obf", ob)])
                        else:
                            sc_ = QSC if kind == "qr" else 1.0
                            P.op("act", lambda h, a2=a2: h.copy(out=qb[a2][:], in_=pA[a2][:]),
                                 r=[("pA", a2)], w=[("qb", a2)])
                            if cfg.get("dbg", 0) != 1:
                                P.op("pe", lambda h, a2=a2: h.matmul(pB[a2][:], lhsT=perm, rhs=qb[a2][:],
                                                                     start=True, stop=True),
                                     r=[("qb", a2)], w=[("pB", a2)])
                            P.op("dve", lambda h, a2=a2, tsl=tsl: h.tensor_tensor(
                                out=t1[a2][:], in0=pA[a2][:], in1=ctab[:, tsl], op=ALU.mult),
                                r=[("pA", a2), "ctab", ("qb", a2)], w=[("t1", a2)])
                            P.op("dve", lambda h, a2=a2, tsl=tsl: h.tensor_tensor(
                                out=t2[a2][:], in0=pB[a2][:], in1=stab[:, tsl], op=ALU.mult),
                                r=[("pB", a2), "stab"], w=[("t2", a2)])
                            P.op("dve", lambda h, a2=a2: h.tensor_tensor(
                                out=t1[a2][:], in0=t1[a2][:], in1=t2[a2][:], op=ALU.add),
                                r=[("t1", a2), ("t2", a2)], w=[("t1", a2)])
                            P.op("act", lambda h, a2=a2, ob=ob, sc_=sc_: h.activation(
                                out=obf[ob][:], in_=t1[a2][:], func=AF.Copy, scale=sc_),
                                r=[("t1", a2)], w=[("obf", ob)])
                        P.dma("sp", dst[ro:ro + 128, dcol0:dcol0 + 512], obf[ob][:], r=[("obf", ob)],
                              sem=("obs", ob))
                for tt in range(NTT if SUB >= 2 else 0):
                    a2 = acnt % 2
                    acnt += 1

                    def mmf(h, tt=tt, dstp=pA[a2]):
                        for k in range(KC):
                            i = h.matmul(dstp[0:HF, :], lhsT=wf[:, k, :], rhs=hT[:, k, tt * 512:(tt + 1) * 512],
                                         start=(k == 0), stop=(k == KC - 1))
                        return i
                    P.op("pe", mmf, r=["wf"] + hk_all[tt * 4: tt * 4 + 4], w=[("pA", a2)])
                    P.op("act", lambda h, a2=a2: h.activation(out=ef[a2][:], in_=pA[a2][0:HF, :], func=AF.Exp,
                                                              bias=nbf[:, 0:1], scale=-1.0),
                         r=[("pA", a2), "nbf"], w=[("ef", a2)])
                    P.op("act", lambda h, a2=a2: h.activation(out=ef[a2][:], in_=ef[a2][:], func=AF.Ln,
                                                              bias=1.0), r=[("ef", a2)], w=[("ef", a2)])
                    P.dma("sp", nld[:, tok0 + tt * 512: tok0 + (tt + 1) * 512], ef[a2][:], r=[("ef", a2)],
                          sem=("efs", a2))
                for (co, dst, width) in (((o_fv, vfd, FW), (o_dv, vdd, DW)) if SUB >= 3 else ()):
                    for n in range(width // 128):
                        s = vcnt % 2
                        vcnt += 1
                        P.dma("pool", wv[s][:], w_in[:, co + n * 128: co + (n + 1) * 128].rearrange(
                            "(k p) n -> p k n", p=128), w=[("wv", s)], sem=("wv", s))
                        for rb in range(TB // 128):
                            v2 = acnt % 2
                            acnt += 1

                            def mmv(h, s=s, rb=rb, dstp=pV[v2]):
                                for k in range(KC):
                                    i = h.matmul(dstp[:, 0:128], lhsT=hT[:, k, rb * 128:(rb + 1) * 128],
                                                 rhs=wv[s][:, k, :], start=(k == 0), stop=(k == KC - 1))
                                return i
                            P.op("pe", mmv, r=[("wv", s), ("hT", rb)], w=[("pV", v2)])
                            P.op("act", lambda h, v2=v2: h.copy(out=vob[v2][:], in_=pV[v2][:, 0:128]),
                                 r=[("pV", v2)], w=[("vob", v2)])
                            P.dma("sp", dst[tok0 + rb * 128: tok0 + (rb + 1) * 128, n * 128:(n + 1) * 128],
                                  vob[v2][:], r=[("vob", v2)], sem=("vos", v2))
            P.barrier()
            P.emit()

        if UPTO < 3:
            return nc
        nbias = sb(top, "nbias", [128, NBLK * HF], F32)
        with ExitStack() as st:
            fa = sb(st, "fa", [HF, 2, OWN], F32)
            fb = sb(st, "fb", [HF, 2, OWN], F32)
            gtot = sb(st, "gtot", [HF, 1], F32)
            pc = ps(st, "pc", [128, 512])
            P.dma("sp", fa[:], nld.rearrange("h (a t) -> h a t", a=2), w=["fa"], sem="fa")
            cur, nxt, kc_, kn_ = fa, fb, "fa", "fb"
            d = 1
            while d < OWN:
                P.op("dve", lambda h, cur=cur, nxt=nxt, d=d: h.tensor_tensor(
                    out=nxt[:, :, d:], in0=cur[:, :, d:], in1=cur[:, :, 0:OWN - d], op=ALU.add),
                    r=[kc_], w=[kn_])
                P.op("dve", lambda h, cur=cur, nxt=nxt, d=d: h.tensor_copy(out=nxt[:, :, 0:d], in_=cur[:, :, 0:d]),
                     r=[kc_], w=[kn_])
                cur, nxt, kc_, kn_ = nxt, cur, kn_, kc_
                d *= 2
            P.op("dve", lambda h, cur=cur, nxt=nxt: h.tensor_scalar(out=nxt[:, 0, :], in0=cur[:, 0, :], scalar1=-1.0,
                                                                    scalar2=None, op0=ALU.mult), r=[kc_], w=[kn_])
            P.dma("sp", frd[:, :], nxt[:, 0, :], r=[kn_], w=["frd"], sem="frs")
            P.op("dve", lambda h, cur=cur: h.tensor_copy(out=gtot[:], in_=cur[:, 1, OWN - 1:OWN]), r=[kc_],
                 w=["gtot"])
            P.op("dve", lambda h, cur=cur: h.tensor_scalar(out=cur[:, 1, :], in0=cur[:, 1, :], scalar1=gtot[:, 0:1],
                                                           scalar2=flagc[0:HF, 0:1], op0=ALU.subtract,
                                                           op1=ALU.add), r=[kc_, "gtot"], w=[kc_])
            curf = cur.rearrange("h a t -> h (a t)")

            def trf(h, curf=curf):
                for i in range(NBLK):
                    ii = h.matmul(pc[:, i * HF:(i + 1) * HF], lhsT=curf[:, i * 128:(i + 1) * 128],
                                  rhs=identf[0:HF, 0:HF], start=True, stop=True)
                return ii
            P.op("pe", trf, r=[kc_], w=["pc"])
            P.op("dve", lambda h: h.tensor_copy(out=nbias[:], in_=pc[:, 0:NBLK * HF]), r=["pc"], w=["nbias"])
            if cfg.get("dbgout"):
                nbd = dscr("nbd", [128, NBLK * HF])
                P.dma("sp", nbd[:, :], nbias[:], r=["nbias"], sem="nbd")
            P.barrier()
            P.emit()

        if UPTO < 4:
            return nc
        with ExitStack() as st:
            kT = [sb(st, "kT%d" % i, [128, S], BF16) for i in range(2)]
            vS = [sb(st, "vS%d" % i, [128, NBLK, 256], BF16) for i in range(2)]
            qT = [sb(st, "qT%d" % i, [128, 512], BF16) for i in range(2)]
            Fb = [sb(st, "Fb%d" % i, [128, 512], F32) for i in range(2)]
            ein = [sb(st, "ein%d" % i, [128, 512], F32) for i in range(3)]
            Pt = [sb(st, "Pt%d" % i, [128, 512], BF16) for i in range(4)]
            rinv = sb(st, "rinv", [128, 512], F32)
            On = sb(st, "On", [128, 2, 512], F32)
            O1 = sb(st, "O1", [128, 2, OWN], F32)
            pre = sb(st, "pre", [128, 2, 512], F32)
            sq = sb(st, "sq", [128, 2, 512], BF16)
            rst = sb(st, "rst", [128, 512], F32)
            yo = [sb(st, "yo%d" % i, [128, 512], BF16) for i in range(2)]
            lamt = sb(st, "lamt", [128, 512], F32)
            lpr = sb(st, "lpr", [128, 256], F32)
            lsc = sb(st, "lsc", [128, 4], F32)
            subc = sb(st, "subc", [128, 2], F32)
            pS = [ps(st, "pS%d" % i, [128, 512]) for i in range(3)]
            pO = [ps(st, "pO%d" % i, [128, 512]) for i in range(2)]
            pR = ps(st, "pR", [128, 512])
            pN = ps(st, "pN", [128, 512])
            P.dma("sp", lamt[:], dlam[0:1, :].partition_broadcast(128), w=["lamt"], sem="lamt")
            P.op("dve", lambda h: h.tensor_tensor(out=lpr[:, 0:128], in0=lamt[:, 0:128], in1=lamt[:, 128:256],
                                                  op=ALU.mult), r=["lamt"], w=["lpr0"])
            P.op("dve", lambda h: h.tensor_tensor(out=lpr[:, 128:256], in0=lamt[:, 256:384], in1=lamt[:, 384:512],
                                                  op=ALU.mult), r=["lamt"], w=["lpr1"])
            P.op("dve", lambda h: h.reduce_sum(out=lsc[:, 0:1], in_=lpr[:, 0:128], axis=AX.X), r=["lpr0"],
                 w=["lsc0"])
            P.op("dve", lambda h: h.reduce_sum(out=lsc[:, 1:2], in_=lpr[:, 128:256], axis=AX.X), r=["lpr1"],
                 w=["lsc1"])
            P.op("act", lambda h: h.activation(out=lsc[:, 2:4], in_=lsc[:, 0:2], func=AF.Exp),
                 r=["lsc0", "lsc1"], w=["lsc2"])
            P.op("dve", lambda h: h.tensor_tensor(out=lsc[:, 0:1], in0=lsc[:, 3:4], in1=lsc[:, 2:3],
                                                  op=ALU.subtract), r=["lsc2"], w=["lsc0"])
            P.op("dve", lambda h: h.tensor_scalar(out=lsc[:, 0:1], in0=lsc[:, 0:1], scalar1=-LINIT, scalar2=None,
                                                  op0=ALU.add), r=["lsc0"], w=["nlam"])
            P.dma("sp", subc[:], subln.rearrange("o (c p) -> p (o c)", p=128), w=["subc"], sem="m_col", slow=True)
            P.op("dve", lambda h: h.tensor_scalar(out=subc[:], in0=subc[:], scalar1=(1.0 - LINIT), scalar2=None,
                                                  op0=ALU.mult), r=["subc"], w=["subc"])
            state = dict(hc=0, qc=0, sc=0, pc=0, yc=0)

            passes = []
            NQ = OWN // 512
            LA = 2

            def attn_pass(krows, qrows, vsrc, vcol0, nvc, fhead, mask, out_cb):
                passes.append((krows, qrows, vsrc, vcol0, nvc, fhead, mask, out_cb))

            def load_kv(pi):
                krows, qrows, vsrc, vcol0, nvc, fhead, mask, out_cb = passes[pi]
                hs = pi % 2
                P.dma("sp", kT[hs][:], krows, w=[("kT", hs)], sem=("kT", hs))
                P.dma("sp", vS[hs][:, :, 0:nvc * 128],
                      vsrc[:, vcol0:vcol0 + nvc * 128].rearrange("(b p) n -> p b n", p=128),
                      w=[("vS", hs)], sem=("vS", hs))

            def load_q(ti):
                pi, qt = divmod(ti, NQ)
                krows, qrows, vsrc, vcol0, nvc, fhead, mask, out_cb = passes[pi]
                qs = ti % 2
                P.dma("sp", qT[qs][:], qrows[:, qt * 512:(qt + 1) * 512], w=[("qT", qs)], sem=("qT", qs))
                if fhead is not None:
                    P.dma("sp", Fb[qs][:], frd[fhead:fhead + 1, qt * 512:(qt + 1) * 512].partition_broadcast(128),
                          w=[("Fb", qs)], sem=("Fb", qs))

            def run_passes():
                total = len(passes) * NQ
                load_kv(0)
                load_q(0)
                for ti in range(total):
                    pi, qt = divmod(ti, NQ)
                    if qt == 0 and pi + 1 < len(passes):
                        load_kv(pi + 1)
                    if ti + 1 < total:
                        load_q(ti + 1)
                    attn_tile(pi, qt, pi % 2, ti % 2)

            def attn_tile(pi, qt, hs, qs):
                krows, qrows, vsrc, vcol0, nvc, fhead, mask, out_cb = passes[pi]
                if True:
                    tiles = [(OB + i, 0, False, True) for i in range(OB)]
                    for i in range(4 * qt + 4):
                        jd = i - 4 * qt
                        tiles.append((i, 128 * jd if jd > 0 else 0, jd >= 0, False))
                    nt_ = len(tiles)
                    sbank = {}

                    def qk(idx):
                        kt, c0, diag, oth = tiles[idx]
                        sbk = state["sc"] % 3
                        state["sc"] += 1
                        sbank[idx] = sbk
                        P.op("pe", lambda h, kt=kt, c0=c0, sbk=sbk, qs=qs, hs=hs: h.matmul(
                            pS[sbk][:, c0:512], lhsT=kT[hs][:, kt * 128:(kt + 1) * 128], rhs=qT[qs][:, c0:512],
                            start=True, stop=True), r=[("kT", hs), ("qT", qs)], w=[("pS", sbk)])
                    for idx in range(min(LA, nt_)):
                        qk(idx)
                    for idx in range(nt_):
                        if idx + LA < nt_:
                            qk(idx + LA)
                        kt, c0, diag, oth = tiles[idx]
                        sbk = sbank[idx]
                        pi = state["pc"] % 4
                        state["pc"] += 1
                        if fhead is not None:
                            P.op("dve", lambda h, sbk=sbk, c0=c0, qs=qs: h.tensor_tensor(
                                out=ein[sbk][:, c0:512], in0=pS[sbk][:, c0:512], in1=Fb[qs][:, c0:512], op=ALU.add),
                                r=[("pS", sbk), ("Fb", qs)], w=[("ein", sbk)])
                            if diag:
                                P.op("dve", lambda h, sbk=sbk, c0=c0: h.tensor_tensor(
                                    out=ein[sbk][:, c0:c0 + 128], in0=ein[sbk][:, c0:c0 + 128], in1=negtri,
                                    op=ALU.add), r=[("ein", sbk)], w=[("ein", sbk)])
                            bcol = nbias[:, kt * HF + fhead: kt * HF + fhead + 1]
                            P.op("act", lambda h, sbk=sbk, c0=c0, pi=pi, bcol=bcol: h.activation(
                                out=Pt[pi][:, c0:512], in_=ein[sbk][:, c0:512], func=AF.Exp, bias=bcol),
                                r=[("ein", sbk)], w=[("Pt", pi)])
                        else:
                            if oth:
                                P.op("act", lambda h, sbk=sbk, pi=pi: h.activation(
                                    out=Pt[pi][:, :], in_=pS[sbk][:, :], func=AF.Exp, bias=flagc[:, 0:1]),
                                    r=[("pS", sbk)], w=[("Pt", pi)])
                            else:
                                P.op("act", lambda h, sbk=sbk, pi=pi, c0=c0: h.activation(
                                    out=Pt[pi][:, c0:512], in_=pS[sbk][:, c0:512], func=AF.Exp),
                                    r=[("pS", sbk)], w=[("Pt", pi)])
                        if diag and fhead is None:
                            P.op("dve", lambda h, pi=pi, c0=c0: h.tensor_tensor(
                                out=Pt[pi][:, c0:c0 + 128], in0=Pt[pi][:, c0:c0 + 128], in1=mask, op=ALU.mult),
                                r=[("Pt", pi)], w=[("Pt", pi)])
                        first, last = (idx == 0), (idx == nt_ - 1)

                        def pv(h, kt=kt, c0=c0, pi=pi, first=first, last=last, hs=hs, nvc=nvc):
                            for vc in range(nvc):
                                h.matmul(pO[vc][:, c0:512], lhsT=vS[hs][:, kt, vc * 128:(vc + 1) * 128],
                                         rhs=Pt[pi][:, c0:512], start=first, stop=last)
                            return h.matmul(pR[:, c0:512], lhsT=ones, rhs=Pt[pi][:, c0:512], start=first, stop=last)
                        P.op("pe", pv, r=[("Pt", pi), ("vS", hs)], w=["pO", "pR"])
                    P.op("dve", lambda h: h.reciprocal(out=rinv[:], in_=pR[:]), r=["pR"], w=["rinv"])
                    for vc in range(nvc):
                        P.op("dve", lambda h, vc=vc: h.tensor_tensor(out=On[:, vc, :], in0=pO[vc][:], in1=rinv[:],
                                                                     op=ALU.mult),
                             r=["pO", "rinv"], w=[("On", vc)])
                    out_cb(qt)

            for hh in range(HF):
                def cb_f(qt, hh=hh):
                    y = state["yc"] % 2
                    state["yc"] += 1
                    P.op("act", lambda h, y=y: h.copy(out=yo[y][:], in_=On[:, 0, :]), r=[("On", 0)], w=[("yo", y)])
                    P.dma("sp", yad[hh * 128:(hh + 1) * 128, qt * 512:(qt + 1) * 512], yo[y][:], r=[("yo", y)],
                          sem=("yos", y))
                attn_pass(kfd[hh * 128:(hh + 1) * 128, :], qfd[hh * 128:(hh + 1) * 128, :], vfd, hh * 128, 1, hh,
                          trim, cb_f)
            for hh in range(HD):
                def cb_1(qt):
                    for vc in range(2):
                        P.op("act", lambda h, vc=vc, qt=qt: h.copy(out=O1[:, vc, qt * 512:(qt + 1) * 512],
                                                                    in_=On[:, vc, :]),
                             r=[("On", vc)], w=[("O1", qt, vc)])

                def cb_2(qt, hh=hh):
                    for vc in range(2):
                        P.op("dve", lambda h, vc=vc, qt=qt: h.scalar_tensor_tensor(
                            out=pre[:, vc, :], in0=On[:, vc, :], scalar=lsc[:, 0:1],
                            in1=O1[:, vc, qt * 512:(qt + 1) * 512], op0=ALU.mult, op1=ALU.add),
                            r=[("On", vc), ("O1", qt, vc), "nlam"], w=[("pre", vc)])
                        P.op("act", lambda h, vc=vc: h.activation(out=sq[:, vc, :], in_=pre[:, vc, :],
                                                                  func=AF.Square),
                             r=[("pre", vc)], w=[("sq", vc)])

                    def mmn(h):
                        h.matmul(pN[:], lhsT=ones, rhs=sq[:, 0, :], start=True, stop=False)
                        return h.matmul(pN[:], lhsT=ones, rhs=sq[:, 1, :], start=False, stop=True)
                    P.op("pe", mmn, r=[("sq", 0), ("sq", 1)], w=["pN"])
                    P.op("dve", lambda h: h.tensor_scalar(out=rst[:], in0=pN[:], scalar1=1.0 / 256, scalar2=EPS,
                                                          op0=ALU.mult, op1=ALU.add), r=["pN"], w=["rst"])
                    P.op("act", lambda h: h.sqrt(out=rst[:], in_=rst[:]), r=["rst"], w=["rst"])
                    P.op("dve", lambda h: h.reciprocal(out=rst[:], in_=rst[:]), r=["rst"], w=["rst"])
                    for vc in range(2):
                        y = state["yc"] % 2
                        state["yc"] += 1
                        P.op("dve", lambda h, vc=vc: h.tensor_tensor(out=pre[:, vc, :], in0=pre[:, vc, :],
                                                                     in1=rst[:], op=ALU.mult),
                             r=[("pre", vc), "rst"], w=[("pre", vc)])
                        P.op("act", lambda h, vc=vc, y=y: h.activation(out=yo[y][:], in_=pre[:, vc, :], func=AF.Copy,
                                                                       scale=subc[:, vc:vc + 1]),
                             r=[("pre", vc), "subc"], w=[("yo", y)])
                        r0 = hh * 256 + vc * 128
                        P.dma("sp", ybd[r0:r0 + 128, qt * 512:(qt + 1) * 512], yo[y][:], r=[("yo", y)],
                              sem=("yos", y))
                for mp, cbk in ((0, cb_1), (1, cb_2)):
                    r0 = (2 * hh + mp) * 128
                    attn_pass(kdd[r0:r0 + 128, :], qdd[r0:r0 + 128, :], vdd, hh * 256, 2, None, chkm, cbk)
            run_passes()
            P.barrier()
            P.emit()

        if UPTO < 5:
            return nc
        TB5 = 512
        with ExitStack() as st:
            mT = sb(st, "mT", [128, KC, TB5], BF16)
            yaS = sb(st, "yaS", [128, FW // 128, TB5], BF16)
            ybS = sb(st, "ybS", [128, DW // 128, TB5], BF16)
            wa = [sb(st, "wa%d" % i, [128, FW // 128, 128], BF16) for i in range(2)]
            wd = [sb(st, "wd%d" % i, [128, DW // 128, 128], BF16) for i in range(2)]
            wo = [sb(st, "wo%d" % i, [128, KC, 256], BF16) for i in range(2)]
            gA = [sb(st, "gA%d" % i, [128, TB5], BF16) for i in range(2)]
            gB = [sb(st, "gB%d" % i, [128, TB5], BF16) for i in range(2)]
            u1 = [sb(st, "u1%d" % i, [128, TB5], F32) for i in range(2)]
            u2 = [sb(st, "u2%d" % i, [128, TB5], F32) for i in range(2)]
            gbt = sb(st, "gbt", [128, D], F32)
            xo = [sb(st, "xo%d" % i, [128, 4, 256], F32) for i in range(4)]
            tmp = [sb(st, "tmp%d" % i, [128, 256], F32) for i in range(2)]
            pa = [ps(st, "pa%d" % i, [128, 512]) for i in range(2)]
            pb_ = [ps(st, "pb%d" % i, [128, 512]) for i in range(2)]
            po = [ps(st, "po%d" % i, [128, 512]) for i in range(2)]
            P.dma("sp", gbt[:], modv[5:6, :].partition_broadcast(128), w=["gbt"], sem="gbt")
            mc = 0
            oc = 0
            xc = 0
            pcn = 0
            for blk in range(OWN // TB5):
                tok0 = blk * TB5
                P.dma("sp", yaS[:], yad[:, tok0:tok0 + TB5].rearrange("(c p) t -> p c t", p=128), w=["yaS"],
                      sem="yaS")
                P.dma("sp", ybS[:], ybd[:, tok0:tok0 + TB5].rearrange("(c p) t -> p c t", p=128), w=["ybS"],
                      sem="ybS")
                for m in range(KC):
                    s = mc % 2
                    mc += 1
                    P.dma("pool", wa[s][:], w_of[:, m * 128:(m + 1) * 128].rearrange("(c p) n -> p c n", p=128),
                          w=[("wa", s)], sem=("wa", s))
                    P.dma("pool", wd[s][:], w_od[:, m * 128:(m + 1) * 128].rearrange("(c p) n -> p c n", p=128),
                          w=[("wd", s)], sem=("wd", s))
                    P.dma("sp", gA[s][:], gAd[m * 128:(m + 1) * 128, tok0:tok0 + TB5], w=[("gA", s)], sem=("gA", s))
                    P.dma("sp", gB[s][:], gBd[m * 128:(m + 1) * 128, tok0:tok0 + TB5], w=[("gB", s)], sem=("gB", s))

                    def mma(h, s=s):
                        n_ = FW // 128
                        for c_ in range(n_):
                            i = h.matmul(pa[s][:], lhsT=wa[s][:, c_, :], rhs=yaS[:, c_, :], start=(c_ == 0),
                                         stop=(c_ == n_ - 1))
                        return i

                    def mmd(h, s=s):
                        n_ = DW // 128
                        for c_ in range(n_):
                            i = h.matmul(pb_[s][:], lhsT=wd[s][:, c_, :], rhs=ybS[:, c_, :], start=(c_ == 0),
                                         stop=(c_ == n_ - 1))
                        return i
                    P.op("pe", mma, r=[("wa", s), "yaS"], w=[("pa", s)])
                    P.op("pe", mmd, r=[("wd", s), "ybS"], w=[("pb", s)])
                    P.op("dve", lambda h, s=s: h.tensor_tensor(out=u1[s][:], in0=pa[s][:], in1=gA[s][:], op=ALU.mult),
                         r=[("pa", s), ("gA", s)], w=[("u1", s)])
                    P.op("dve", lambda h, s=s: h.tensor_tensor(out=u2[s][:], in0=pb_[s][:], in1=gB[s][:], op=ALU.mult),
                         r=[("pb", s), ("gB", s)], w=[("u2", s)])
                    P.op("dve", lambda h, s=s, m=m: h.tensor_tensor(out=mT[:, m, :], in0=u1[s][:], in1=u2[s][:],
                                                                    op=ALU.add),
                         r=[("u1", s), ("u2", s)], w=[("mT", m)])
                mk = [("mT", m) for m in range(KC)]
                def xload5(n, tok0=tok0, xc=xc):
                    xsl = (xc + n) % 4
                    P.dma("sp", xo[xsl][:], xs[tok0:tok0 + TB5, n * 256:(n + 1) * 256]
                          .rearrange("(r p) n -> p r n", p=128), w=[("xo", xsl)], sem=("xol", xsl))
                NN = D // 256
                for n in range(min(2, NN)):
                    xload5(n)
                for n in range(NN):
                    s = oc % 2
                    oc += 1
                    P.dma("pool", wo[s][:], w_o[:, n * 256:(n + 1) * 256].rearrange("(k p) n -> p k n", p=128),
                          w=[("wo", s)], sem=("wo", s))
                    xsl = (xc + n) % 4
                    dk = ("xs5", blk, n)
                    if n + 2 < NN:
                        xload5(n + 2)
                    for rb in range(TB5 // 128):
                        pb2 = pcn % 2
                        pcn += 1

                        def mmo(h, s=s, rb=rb, dstp=po[pb2]):
                            for k in range(KC):
                                i = h.matmul(dstp[:, 0:256], lhsT=mT[:, k, rb * 128:(rb + 1) * 128], rhs=wo[s][:, k, :],
                                             start=(k == 0), stop=(k == KC - 1))
                            return i
                        P.op("pe", mmo, r=[("wo", s)] + mk, w=[("po", pb2)])
                        P.op("dve", lambda h, pb2=pb2, n=n: h.tensor_tensor(
                            out=tmp[pb2][:], in0=po[pb2][:, 0:256], in1=gbt[:, n * 256:(n + 1) * 256], op=ALU.mult),
                            r=[("po", pb2), "gbt"], w=[("tmp", pb2)])
                        P.op("dve", lambda h, pb2=pb2, xsl=xsl, rb=rb: h.tensor_tensor(
                            out=xo[xsl][:, rb, :], in0=tmp[pb2][:], in1=xo[xsl][:, rb, :], op=ALU.add),
                            r=[("tmp", pb2), ("xo", xsl)], w=[("xo", xsl)])
                    P.dma("sp", xs[tok0:tok0 + TB5, n * 256:(n + 1) * 256].rearrange("(r p) n -> p r n", p=128),
                          xo[xsl][:], r=[("xo", xsl)], w=[dk], sem=("xos", xsl))
                xc += NN
            P.barrier()
            P.emit()

        if UPTO < 6:
            return nc
        ffn_stage(P, "f2", OWN, xs, f2_in, f2_out, AB[:, 4, :], AB[:, 5, :], 8)

        with ExitStack() as st:
            Af = sb(st, "Af", [128, D], F32)
            Bf = sb(st, "Bf", [128, D], F32)
            nf = sb(st, "nf", [128, D], F32)
            xt = [sb(st, "xt%d" % i, [128, D], F32) for i in range(2)]
            junk = sb(st, "junk", [128, D], BF16)
            ss = [sb(st, "ss%d" % i, [128, 1], F32) for i in range(2)]
            rs = [sb(st, "rs%d" % i, [128, 1], F32) for i in range(2)]
            P.dma("sp", Af[:], modv[10:11, :].partition_broadcast(128), w=["Af"], sem="Af")
            P.dma("sp", Bf[:], modv[9:10, :].partition_broadcast(128), w=["Bf"], sem="Bf")
            P.dma("sp", nf[:], norms[3:4, :].partition_broadcast(128), w=["nf"], sem="nf")
            P.op("dve", lambda h: h.scalar_tensor_tensor(out=Af[:], in0=Af[:], scalar=1.0, in1=nf[:], op0=ALU.add,
                                                         op1=ALU.mult), r=["Af", "nf"], w=["Af"])
            for rb in range(OB):
                s = rb % 2
                P.dma("sp", xt[s][:], xs[rb * 128:(rb + 1) * 128, :], w=[("xt", s)], sem=("xtl", s))
                P.op("dve", lambda h, s=s: h.memset(ss[s][:], 0.0), w=[("ss", s)])
                P.op("act", lambda h, s=s: h.activation(out=junk[:], in_=xt[s][:], func=AF.Square, accum_out=ss[s][:]),
                     r=[("xt", s)], w=["junk", ("ss", s)])
                P.op("dve", lambda h, s=s: h.tensor_scalar(out=rs[s][:], in0=ss[s][:], scalar1=1.0 / D, scalar2=EPS,
                                                           op0=ALU.mult, op1=ALU.add), r=[("ss", s)], w=[("rs", s)])
                P.op("act", lambda h, s=s: h.sqrt(out=rs[s][:], in_=rs[s][:]), r=[("rs", s)], w=[("rs", s)])
                P.op("dve", lambda h, s=s: h.reciprocal(out=rs[s][:], in_=rs[s][:]), r=[("rs", s)], w=[("rs", s)])
                P.op("dve", lambda h, s=s: h.scalar_tensor_tensor(out=xt[s][:], in0=xt[s][:], scalar=rs[s][:, 0:1],
                                                                  in1=Af[:], op0=ALU.mult, op1=ALU.mult),
                     r=[("xt", s), ("rs", s), "Af"], w=[("xt", s)])
                P.op("dve", lambda h, s=s: h.tensor_tensor(out=xt[s][:], in0=xt[s][:], in1=Bf[:], op=ALU.add),
                     r=[("xt", s), "Bf"], w=[("xt", s)])
                P.dma("sp", out[rb * 128:(rb + 1) * 128, :], xt[s][:], r=[("xt", s)], sem=("xts", s))
            P.barrier()
            P.emit()
    return nc


def make_in_maps(cfg, inputs):
    D, S, B = cfg["D"], cfg["S"], cfg["B"]
    OWN = S // 2
    g = lambda k: np.asarray(inputs[k])
    cbh, cfh = host_consts()
    shared = {
        "constb": cbh, "constf": cfh,
        "ada_w": g("ada_w")[0], "ada_b": g("ada_b")[0][None, :],
        "final_ada_w": g("final_ada_w"), "final_ada_b": g("final_ada_b")[None, :],
        "norms": np.stack([g("norm_ffn1")[0], g("norm_mix")[0], g("norm_ffn2")[0], g("norm_final")]),
        "ffn1_w_in": g("ffn1_w_in")[0], "ffn1_w_out": g("ffn1_w_out")[0],
        "ffn2_w_in": g("ffn2_w_in")[0], "ffn2_w_out": g("ffn2_w_out")[0],
        "w_in": g("w_in")[0], "b_forget": g("b_forget")[0][:, None], "b_gate": g("b_gate")[0][None, :],
        "diff_lambda": g("diff_lambda")[0].reshape(1, 512), "diff_subln": g("diff_subln")[0][None, :],
        "w_o_fox": g("w_o_fox")[0], "w_o_diff": g("w_o_diff")[0], "w_out": g("w_out")[0],
    }
    shared = {k: np.ascontiguousarray(v) for k, v in shared.items()}
    x, c, pos = g("x"), g("c"), g("positions")
    maps = []
    for core in range(2 * B):
        b, r = core // 2, core % 2
        order = np.concatenate([np.arange(r * OWN, (r + 1) * OWN), np.arange((1 - r) * OWN, (2 - r) * OWN)])
        m = dict(shared)
        m["x"] = np.ascontiguousarray(x[b][order])
        m["c"] = np.ascontiguousarray(c[b][None, :])
        m["pos"] = np.ascontiguousarray(pos[b][order][None, :].astype(np.int32))
        m["flag"] = np.full((128, 1), 0.0 if r == 1 else -30000.0, np.float32)
        maps.append(m)
    return maps


def run(cfg, inputs):
    nc = build(cfg)
    maps = make_in_maps(cfg, inputs)
    ncores = 2 * cfg["B"]
    res = run_bass_kernel_spmd(nc, maps, core_ids=list(range(ncores)))
    if cfg.get("dbgout"):
        return res.results
    D, S, B = cfg["D"], cfg["S"], cfg["B"]
    OWN = S // 2
    outp = np.empty((B, S, D), np.float32)
    for core in range(ncores):
        b, r = core // 2, core % 2
        outp[b, r * OWN:(r + 1) * OWN] = res.results[core]["out"]
    return outp


def kernel(**inputs):
    return run(FULL, inputs)
```

```python
import math
from contextlib import ExitStack
import numpy as np
import ml_dtypes
import concourse.bass as bass
import concourse.mybir as mybir
from concourse.bass_utils import run_bass_kernel_spmd

F32 = mybir.dt.float32
BF16 = mybir.dt.bfloat16
I32 = mybir.dt.int32
AF = mybir.ActivationFunctionType
ALU = mybir.AluOpType
AX = mybir.AxisListType
EPS = 1e-6
ENGS = ("pe", "act", "dve", "pool", "sp")
FULL = dict(D=4096, S=4096, DFF=11008, TB=1024, NG=11, B=4)


class Op:
    __slots__ = ("eng", "fn", "deps", "sig", "sem", "val")


class Prog:
    def __init__(self, nc, stack):
        self.nc = nc
        self.stack = stack
        self.handles = {"pe": nc.tensor, "act": nc.scalar, "dve": nc.vector, "pool": nc.gpsimd, "sp": nc.sync}
        self.esem = {e: stack.enter_context(nc.semaphore("es_" + e)) for e in ENGS}
        self.ecount = {e: 0 for e in ENGS}
        self.dsem = {}
        self.dcnt = {}
        self.dlast = {}
        self.known = {e: {} for e in ENGS}
        self.lastop = {e: None for e in ENGS}
        self.reset()

    def reset(self):
        self.ops = {e: [] for e in ENGS}
        self.lw = {}
        self.rd = {}

    def op(self, eng, fn, r=(), w=(), dma=None):
        o = Op()
        o.eng = eng
        o.fn = fn
        o.sig = False
        o.sem = None
        o.val = 0
        deps = []
        for k in r:
            p = self.lw.get(k)
            if p is not None:
                deps.append(p)
        for k in w:
            p = self.lw.get(k)
            if p is not None:
                deps.append(p)
            deps.extend(self.rd.get(k, ()))
        if dma is not None:
            if dma not in self.dsem:
                self.dsem[dma] = self.stack.enter_context(self.nc.semaphore("ds%d" % len(self.dsem)))
                self.dcnt[dma] = 0
            o.sem = dma
            prev = self.dlast.get(dma)
            if prev is not None:
                deps.append(prev)
            self.dlast[dma] = o
            self.dcnt[dma] += 1
            o.val = 16 * self.dcnt[dma]
        seen = set()
        dd = []
        for d in deps:
            if d is o or id(d) in seen:
                continue
            seen.add(id(d))
            if d.sem is None and d.eng == "pe" and eng == "pe":
                continue
            if d.sem is None:
                d.sig = True
            dd.append(d)
        o.deps = dd
        for k in r:
            self.rd.setdefault(k, []).append(o)
        for k in w:
            self.lw[k] = o
            self.rd[k] = []
        self.ops[eng].append(o)
        self.lastop[eng] = o
        return o

    def dma(self, q, out, in_, r=(), w=(), sem=None, slow=False):
        def fn(h, out=out, in_=in_, slow=slow):
            if slow:
                return h.dma_start(out=out, in_=in_, allow_slow_non_contiguous=True)
            return h.dma_start(out=out, in_=in_)
        return self.op(q, fn, r=r, w=w, dma=sem)

    def barrier(self):
        lasts = [self.lastop[e] for e in ENGS if self.lastop[e] is not None]
        dl = list(self.dlast.values())
        for e in ENGS:
            o = Op()
            o.eng = e
            o.fn = None
            o.sig = False
            o.sem = None
            o.val = 0
            dd = []
            for d in lasts + dl:
                if d.eng == e and d.sem is None:
                    continue
                if d.sem is None:
                    d.sig = True
                dd.append(d)
            o.deps = dd
            self.ops[e].append(o)
        self.lw = {}
        self.rd = {}

    def emit(self):
        for e in ENGS:
            n = self.ecount[e]
            for o in self.ops[e]:
                if o.sem is None and o.sig:
                    n += 1
                    o.val = n
            self.ecount[e] = n
        with self.nc.Block() as block:
            def run(e, h):
                known = self.known[e]
                for o in self.ops[e]:
                    for d in o.deps:
                        if d.sem is not None:
                            sh = self.dsem[d.sem]
                            key = ("d", d.sem)
                        else:
                            sh = self.esem[d.eng]
                            key = ("e", d.eng)
                        if known.get(key, 0) >= d.val:
                            continue
                        h.wait_ge(sh, d.val)
                        known[key] = d.val
                    if o.fn is None:
                        continue
                    inst = o.fn(h)
                    if o.sem is not None:
                        inst.then_inc(self.dsem[o.sem], 16)
                    elif o.sig:
                        inst.then_inc(self.esem[e], 1)

            @block.tensor
            def _(h):
                run("pe", h)

            @block.scalar
            def _(h):
                run("act", h)

            @block.vector
            def _(h):
                run("dve", h)

            @block.gpsimd
            def _(h):
                run("pool", h)

            @block.sync
            def _(h):
                run("sp", h)
        self.reset()


def host_consts():
    p = np.arange(128)
    cb = np.zeros((128, 5 * 128), np.float32)
    cb[:, 0:128] = np.eye(128)
    cb[:, 128:256] = 1.0
    cb[:, 256:384] = (p[:, None] <= p[None, :])
    cb[:, 384:512] = 1.0 - ((p[:, None] >= 64) & (p[None, :] < 64))
    perm = np.zeros((128, 128), np.float32)
    for m in range(32):
        perm[(m + 16) % 32, m] = 1.0
    cb[:, 512:640] = perm
    cf = np.zeros((128, 8 + 256), np.float32)
    cf[:, 136:264] = np.where(p[:, None] <= p[None, :], 0.0, -30000.0)
    invf = np.zeros(128, np.float32)
    invf[:32] = (500000.0 ** (-(np.arange(0, 32, 2, dtype=np.float32)) / 32.0))[np.arange(32) % 16]
    cf[:, 0] = invf
    cf[:16, 1] = -1.0
    cf[16:32, 1] = 1.0
    cf[:, 8:136] = np.eye(128)
    return cb.astype(ml_dtypes.bfloat16), cf


def build(cfg):
    D, S, DFF, TB, NG = cfg["D"], cfg["S"], cfg["DFF"], cfg["TB"], cfg["NG"]
    KC = D // 128
    OWN = S // 2
    HF = D // 256
    HD = D // 512
    FW = HF * 128
    DW = HD * 256
    JC = DFF // 128
    NBLK = S // 128
    OB = OWN // 128
    QSC = 128.0 ** -0.5
    LINIT = 0.8 - 0.6 * math.exp(0.0)
    nc = bass.Bass("TRN2", target_bir_lowering=False)

    def din(name, shape, dt=F32):
        return nc.dram_tensor(name, list(shape), dt, kind="ExternalInput").ap()

    def dscr(name, shape, dt=F32):
        kind = "ExternalOutput" if cfg.get("dbgout") else "Internal"
        return nc.dram_tensor(name, list(shape), dt, kind=kind).ap()

    x_in = din("x", [S, D])
    c_in = din("c", [1, D])
    pos_in = din("pos", [1, S], I32)
    flag_in = din("flag", [128, 1])
    constb_in = din("constb", [128, 640], BF16)
    constf_in = din("constf", [128, 264])
    ada_w = din("ada_w", [D, 9 * D])
    ada_b = din("ada_b", [1, 9 * D])
    fin_w = din("final_ada_w", [D, 2 * D])
    fin_b = din("final_ada_b", [1, 2 * D])
    norms = din("norms", [4, D])
    f1_in = din("ffn1_w_in", [D, 2 * DFF])
    f1_out = din("ffn1_w_out", [DFF, D])
    f2_in = din("ffn2_w_in", [D, 2 * DFF])
    f2_out = din("ffn2_w_out", [DFF, D])
    w_in = din("w_in", [D, 3 * FW + HF + 3 * DW + 2 * D])
    b_forget = din("b_forget", [HF, 1])
    b_gate = din("b_gate", [1, 2 * D])
    dlam = din("diff_lambda", [1, 512])
    subln = din("diff_subln", [1, 256])
    w_of = din("w_o_fox", [FW, D])
    w_od = din("w_o_diff", [DW, D])
    w_o = din("w_out", [D, D])
    out = nc.dram_tensor("out", [OWN, D], F32, kind="ExternalOutput").ap()

    xs = dscr("xs", [S, D])
    modv = dscr("modv", [11, D])
    kfd = dscr("kfd", [FW, S], BF16)
    kdd = dscr("kdd", [DW, S], BF16)
    vfd = dscr("vfd", [S, FW], BF16)
    vdd = dscr("vdd", [S, DW], BF16)
    qfd = dscr("qfd", [FW, OWN], BF16)
    qdd = dscr("qdd", [DW, OWN], BF16)
    gAd = dscr("gAd", [D, OWN], BF16)
    gBd = dscr("gBd", [D, OWN], BF16)
    nld = dscr("nld", [HF, S])
    frd = dscr("frd", [HF, OWN])
    yad = dscr("yad", [FW, OWN], BF16)
    ybd = dscr("ybd", [DW, OWN], BF16)

    groups = []
    j = 0
    while j < JC:
        n = min(NG, JC - j)
        groups.append((j, n))
        j += n

    with ExitStack() as top:
        P = Prog(nc, top)

        uid = [0]

        def sb(st, name, shape, dt):
            uid[0] += 1
            return st.enter_context(nc.sbuf_tensor("%s_%d" % (name, uid[0]), list(shape), dt))

        def ps(st, name, shape, dt=F32):
            uid[0] += 1
            return st.enter_context(nc.psum_tensor("%s_%d" % (name, uid[0]), list(shape), dt))

        cb = sb(top, "cb", [128, 640], BF16)
        cf = sb(top, "cf", [128, 264], F32)
        flagc = sb(top, "flagc", [128, 1], F32)
        cols = sb(top, "cols", [128, 12, KC], F32)
        ident = cb[:, 0:128]
        ones = cb[:, 128:256]
        trim = cb[:, 256:384]
        chkm = cb[:, 384:512]
        perm = cb[:, 512:640]
        invf = cf[:, 0:1]
        sgn = cf[:, 1:2]
        identf = cf[:, 8:136]
        negtri = cf[:, 136:264]
        P.dma("sp", cb[:], constb_in[:, :], w=["cb"], sem="c0")
        P.dma("sp", cf[:], constf_in[:, :], w=["cf"], sem="c1")
        P.dma("sp", flagc[:], flag_in[:, :], w=["flagc"], sem="c2")

        def colview(v):
            return cols[:, v, :]

        with ExitStack() as st:
            ccol = sb(st, "ccol", [128, KC], F32)
            sccol = sb(st, "sccol", [128, KC], BF16)
            Wt = [sb(st, "mW%d" % i, [128, KC, 512], BF16) for i in range(2)]
            brow = sb(st, "brow", [1, D], F32)
            mrow = sb(st, "mrow", [1, D], F32)
            pm = [ps(st, "pm%d" % i, [128, 512]) for i in range(2)]
            P.dma("sp", ccol[:], c_in.rearrange("o (k p) -> p (o k)", p=128), w=["ccol"], sem="m_c", slow=True)
            P.op("act", lambda h: h.activation(out=sccol[:], in_=ccol[:], func=AF.Silu), r=["ccol"], w=["sccol"])
            it = 0
            for v in range(11):
                Wsrc, col0 = (ada_w, v * D) if v < 9 else (fin_w, (v - 9) * D)
                bsrc = ada_b if v < 9 else fin_b
                P.dma("sp", brow[:], bsrc[0:1, col0:col0 + D], w=["brow"], sem="m_b")
                for nt in range(D // 512):
                    s = it % 2
                    it += 1
                    src = Wsrc[:, col0 + nt * 512: col0 + (nt + 1) * 512].rearrange("(k p) n -> p k n", p=128)
                    P.dma("pool", Wt[s][:], src, w=[("mW", s)], sem=("mW", s))

                    def mm(h, s=s):
                        for k in range(KC):
                            i = h.matmul(pm[s][0:1, :], lhsT=sccol[:, k:k + 1], rhs=Wt[s][:, k, :],
                                         start=(k == 0), stop=(k == KC - 1))
                        return i
                    P.op("pe", mm, r=[("mW", s), "sccol"], w=[("pm", s)])
                    P.op("dve", lambda h, s=s, nt=nt: h.tensor_tensor(
                        out=mrow[0:1, nt * 512:(nt + 1) * 512], in0=pm[s][0:1, :],
                        in1=brow[0:1, nt * 512:(nt + 1) * 512], op=ALU.add),
                        r=[("pm", s), "brow"], w=[("mrow", nt)])
                P.dma("sp", modv[v:v + 1, :], mrow[:], r=[("mrow", nt) for nt in range(D // 512)],
                      w=[("modv", v)], sem="m_st")
            for v in range(11):
                P.dma("sp", cols[:, v, :], modv[v:v + 1, :].rearrange("o (k p) -> p (o k)", p=128),
                      r=[("modv", v)], w=[("cols", v)], sem="m_col", slow=True)
            P.barrier()
            P.emit()

        AB = sb(top, "AB", [128, 6, KC], F32)
        gcol = sb(top, "gcol", [128, 4, KC], F32)
        with ExitStack() as st:
            for i in range(4):
                P.dma("sp", gcol[:, i, :], norms[i:i + 1, :].rearrange("o (k p) -> p (o k)", p=128),
                      w=[("gcol", i)], sem="m_col", slow=True)
            for i in range(3):
                P.op("dve", lambda h, i=i: h.scalar_tensor_tensor(
                    out=AB[:, 2 * i, :], in0=cols[:, 3 * i + 1, :], scalar=1.0, in1=gcol[:, i, :],
                    op0=ALU.add, op1=ALU.mult), r=[("gcol", i)], w=[("AB", 2 * i)])
                P.op("dve", lambda h, i=i: h.tensor_copy(out=AB[:, 2 * i + 1, :], in_=cols[:, 3 * i, :]),
                     w=[("AB", 2 * i + 1)])
            P.barrier()
            P.emit()

        def norm_rows(P, src_fn, rkeys_fn, nrb, hT, Acol, Bcol, B):
            xt, ysb, ss, rs, tp = B["xt"], B["ysb"], B["ss"], B["rs"], B["tp"]
            XK = B.get("xk", ["scr0", "scr1"])
            cpb = min(8, KC)
            for rb in range(nrb):
                s = rb % len(ysb)
                P.dma("sp", xt, src_fn(rb), r=rkeys_fn(rb), w=XK, sem="xt")
                P.op("dve", lambda h, s=s: h.memset(ss[s][:], 0.0), w=[("ss", s)])
                P.op("act", lambda h, s=s: h.activation(out=ysb[s][:], in_=xt, func=AF.Square,
                                                        accum_out=ss[s][:]),
                     r=XK, w=[("y", s), ("ss", s)])
                P.op("dve", lambda h, s=s: h.tensor_scalar(out=rs[s][:], in0=ss[s][:], scalar1=1.0 / D,
                                                           scalar2=EPS, op0=ALU.mult, op1=ALU.add),
                     r=[("ss", s)], w=[("rs", s)])
                P.op("act", lambda h, s=s: h.sqrt(out=rs[s][:], in_=rs[s][:]), r=[("rs", s)], w=[("rs", s)])
                P.op("dve", lambda h, s=s: h.reciprocal(out=rs[s][:], in_=rs[s][:]), r=[("rs", s)], w=[("rs", s)])
                P.op("act", lambda h, s=s: h.activation(out=ysb[s][:], in_=xt, func=AF.Copy,
                                                        scale=rs[s][:, 0:1]),
                     r=XK + [("rs", s)], w=[("y", s)])
                for c0 in range(0, KC, cpb):
                    t = B["tpc"] % 2
                    B["tpc"] += 1

                    def tr(h, s=s, c0=c0, t=t):
                        for cc in range(cpb):
                            i = h.transpose(out=tp[t][:, cc * 128:(cc + 1) * 128],
                                            in_=ysb[s][:, (c0 + cc) * 128:(c0 + cc + 1) * 128], identity=ident)
                        return i
                    P.op("pe", tr, r=[("y", s)], w=[("tp", t)])

                    def ev(h, c0=c0, t=t, rb=rb):
                        for cc in range(cpb):
                            c = c0 + cc
                            i = h.tensor_scalar(out=hT[:, c, rb * 128:(rb + 1) * 128],
                                                in0=tp[t][:, cc * 128:(cc + 1) * 128],
                                                scalar1=Acol[:, c:c + 1], scalar2=Bcol[:, c:c + 1],
                                                op0=ALU.mult, op1=ALU.add)
                        return i
                    P.op("dve", ev, r=[("tp", t)], w=[("hT", rb)])

        def ffn_stage(P, tag, n_tok, first_src, W1, W2, Acol, Bcol, gvec):
            with ExitStack() as st:
                hT = sb(st, "hT", [128, KC, TB], BF16)
                aT = sb(st, "aT", [128, NG, TB], BF16)
                wi = [sb(st, "wi%d" % i, [128, KC, 128], BF16) for i in range(4)]
                wo = [sb(st, "wo%d" % i, [128, NG, 512], BF16) for i in range(2)]
                scr_full = sb(st, "scr", [128, max(D, 4096)], F32)
                scr = scr_full[:, 0:D]
                ysb = [sb(st, "ysb%d" % i, [128, D], BF16) for i in range(2)]
                gbt = sb(st, "gbt", [128, D], F32)
                sil = [sb(st, "sil%d" % i, [128, 512], F32) for i in range(2)]
                tmp = [sb(st, "tmp%d" % i, [128, 512], F32) for i in range(2)]
                ss = [sb(st, "ss%d" % i, [128, 1], F32) for i in range(2)]
                rs = [sb(st, "rs%d" % i, [128, 1], F32) for i in range(2)]
                pg = [ps(st, "pg%d" % i, [128, 512]) for i in range(2)]
                pu = [ps(st, "pu%d" % i, [128, 512]) for i in range(2)]
                po = [ps(st, "po%d" % i, [128, 512]) for i in range(2)]
                tp = [ps(st, "tp%d" % i, [128, 1024], BF16) for i in range(2)]
                B = dict(xt=scr, ysb=ysb, ss=ss, rs=rs, tp=tp, tpc=0, xk=["scr0", "scr1", "scr2", "scr3"])
                RBG = 2
                NSL = 4
                scrv = [scr_full[:, i * 1024:(i + 1) * 1024].rearrange("p (r n) -> p r n", r=RBG)
                        for i in range(NSL)]
                P.dma("sp", gbt[:], modv[gvec:gvec + 1, :].partition_broadcast(128), w=["gbt"], sem="gbt")
                P.op("act", lambda h: h.mul(out=gbt[:], in_=gbt[:], mul=0.5), r=["gbt"], w=["gbt"])
                cnt = 0
                ocnt = 0
                xcnt = 0
                pcnt = 0
                NTT = TB // 512
                for blk in range(n_tok // TB):
                    tok0 = blk * TB
                    src0 = first_src
                    norm_rows(P, lambda rb: src0[tok0 + rb * 128: tok0 + (rb + 1) * 128, :],
                              lambda rb: [("xs", (tok0 + rb * 128) // 256, n) for n in range(D // 512)],
                              TB // 128, hT, Acol, Bcol, B)
                    for g, (j0, ng) in enumerate(groups):
                        for jl in range(ng):
                            jj = j0 + jl
                            sg = (2 * cnt) % 4
                            su = (2 * cnt + 1) % 4
                            cnt += 1
                            P.dma("pool", wi[sg][:], W1[:, jj * 128:(jj + 1) * 128].rearrange(
                                "(k p) n -> p k n", p=128), w=[("wi", sg)], sem=("wi", sg))
                            P.dma("pool", wi[su][:], W1[:, DFF + jj * 128: DFF + (jj + 1) * 128].rearrange(
                                "(k p) n -> p k n", p=128), w=[("wi", su)], sem=("wi", su))
                            for tt in range(NTT):
                                b2 = tt % 2
                                hk = [("hT", rb) for rb in range(tt * 4, tt * 4 + 4)]

                                def mmg(h, sl=sg, tt=tt, dst=pg[b2]):
                                    for k in range(KC):
                                        i = h.matmul(dst[:], lhsT=wi[sl][:, k, :],
                                                     rhs=hT[:, k, tt * 512:(tt + 1) * 512],
                                                     start=(k == 0), stop=(k == KC - 1))
                                    return i

                                def mmu(h, sl=su, tt=tt, dst=pu[b2]):
                                    for k in range(KC):
                                        i = h.matmul(dst[:], lhsT=wi[sl][:, k, :],
                                                     rhs=hT[:, k, tt * 512:(tt + 1) * 512],
                                                     start=(k == 0), stop=(k == KC - 1))
                                    return i
                                P.op("pe", mmg, r=[("wi", sg)] + hk, w=[("pg", b2)])
                                P.op("pe", mmu, r=[("wi", su)] + hk, w=[("pu", b2)])
                                P.op("act", lambda h, b2=b2: h.activation(out=sil[b2][:], in_=pg[b2][:],
                                                                         func=AF.Silu),
                                     r=[("pg", b2)], w=[("sil", b2)])
                                P.op("dve", lambda h, b2=b2, jl=jl, tt=tt: h.tensor_tensor(
                                    out=aT[:, jl, tt * 512:(tt + 1) * 512], in0=sil[b2][:], in1=pu[b2][:],
                                    op=ALU.mult), r=[("sil", b2), ("pu", b2)], w=[("aT", jl, tt)])
                        base = first_src if g == 0 else xs
                        tasks = [(n, rg) for n in range(D // 512) for rg in range(TB // 128 // RBG)]

                        def xinfo(ti, base=base, tok0=tok0, xb=xcnt, tasks=tasks):
                            n, rg = tasks[ti]
                            r0 = tok0 + rg * RBG * 128
                            return n, rg, r0, (xb + ti) % NSL, ("xs", r0 // 256, n)

                        def xload(ti, base=base):
                            n, rg, r0, xsl, dk = xinfo(ti)
                            P.dma("sp", scrv[xsl], base[r0:r0 + RBG * 128, n * 512:(n + 1) * 512]
                                  .rearrange("(r p) n -> p r n", p=128), r=[dk], w=["scr%d" % xsl],
                                  sem=("scrl", xsl))
                        PRE = 2
                        for ti in range(min(PRE, len(tasks))):
                            xload(ti)
                        cur_n = -1
                        for ti in range(len(tasks)):
                            n, rg, r0, xsl, dk = xinfo(ti)
                            if n != cur_n:
                                cur_n = n
                                s = ocnt % 2
                                ocnt += 1
                                P.dma("pool", wo[s][:, 0:ng, :],
                                      W2[j0 * 128:(j0 + ng) * 128, n * 512:(n + 1) * 512]
                                      .rearrange("(j p) n -> p j n", p=128), w=[("wo", s)], sem=("wo", s))
                            if ti + PRE < len(tasks):
                                xload(ti + PRE)
                            for r4 in range(RBG):
                                rb = rg * RBG + r4
                                pb = pcnt % 2
                                pcnt += 1

                                def mmo(h, rb=rb, s=s, dst=po[pb], ng=ng):
                                    for jl in range(ng):
                                        i = h.matmul(dst[:], lhsT=aT[:, jl, rb * 128:(rb + 1) * 128],
                                                     rhs=wo[s][:, jl, :], start=(jl == 0), stop=(jl == ng - 1))
                                    return i
                                P.op("pe", mmo, r=[("wo", s)] + [("aT", jl, rb // 4) for jl in range(ng)],
                                     w=[("po", pb)])
                                P.op("dve", lambda h, pb=pb, n=n: h.tensor_tensor(
                                    out=tmp[pb][:], in0=po[pb][:], in1=gbt[:, n * 512:(n + 1) * 512],
                                    op=ALU.mult), r=[("po", pb), "gbt"], w=[("tmp", pb)])
                                P.op("dve", lambda h, pb=pb, xsl=xsl, r4=r4: h.tensor_tensor(
                                    out=scrv[xsl][:, r4, :], in0=tmp[pb][:], in1=scrv[xsl][:, r4, :],
                                    op=ALU.add), r=[("tmp", pb), "scr%d" % xsl], w=["scr%d" % xsl])
                            P.dma("sp", xs[r0:r0 + RBG * 128, n * 512:(n + 1) * 512]
                                  .rearrange("(r p) n -> p r n", p=128), scrv[xsl],
                                  r=["scr%d" % xsl], w=[dk], sem=("scrs", xsl))
                        xcnt += len(tasks)
                P.barrier()
                P.emit()

        UPTO = cfg.get("upto", 99)
        if UPTO < 1:
            return nc
        ffn_stage(P, "f1", S, x_in, f1_in, f1_out, AB[:, 0, :], AB[:, 1, :], 2)
        if UPTO < 2:
            return nc

        o_fq, o_fk, o_fv, o_ff = 0, FW, 2 * FW, 3 * FW
        o_dq = 3 * FW + HF
        o_dk, o_dv = o_dq + DW, o_dq + 2 * DW
        o_ga = o_dq + 3 * DW
        o_gb = o_ga + D
        with ExitStack() as st:
            hT = sb(st, "hT", [128, KC, TB], BF16)
            wi = [sb(st, "wi%d" % i, [128, KC, 128], BF16) for i in range(4)]
            wv = [sb(st, "wv%d" % i, [128, KC, 256], BF16) for i in range(2)]
            wf = sb(st, "wf", [128, KC, HF], BF16)
            scr = sb(st, "scr", [128, D], F32)[:, :]
            ysb = [sb(st, "ysb0", [128, D], BF16)]
            ss = [sb(st, "ss%d" % i, [128, 1], F32) for i in range(2)]
            rs = [sb(st, "rs%d" % i, [128, 1], F32) for i in range(2)]
            posi = sb(st, "posi", [128, TB], I32)
            ang = sb(st, "ang", [128, TB], F32)
            ctab = sb(st, "ctab", [128, TB], F32)
            stab = sb(st, "stab", [128, TB], F32)
            rq = sb(st, "rq", [128, TB], F32)
            rr = sb(st, "rr", [128, TB], F32)
            obf = [sb(st, "obf%d" % i, [128, 512], BF16) for i in range(3)]
            qb = [sb(st, "qb%d" % i, [128, 512], BF16) for i in range(2)]
            t1 = [sb(st, "t1%d" % i, [128, 512], F32) for i in range(2)]
            t2 = [sb(st, "t2%d" % i, [128, 512], F32) for i in range(2)]
            bgc = sb(st, "bgc", [128, 2 * KC], F32)
            nbf = sb(st, "nbf", [HF, 1], F32)
            ef = [sb(st, "ef%d" % i, [HF, 512], F32) for i in range(2)]
            vob = [sb(st, "vob%d" % i, [128, 256], BF16) for i in range(2)]
            pA = [ps(st, "pA%d" % i, [128, 512]) for i in range(2)]
            pB = [ps(st, "pB%d" % i, [128, 512]) for i in range(2)]
            pV = [ps(st, "pV%d" % i, [128, 512]) for i in range(2)]
            tp = [ps(st, "tp%d" % i, [128, 1024], BF16) for i in range(2)]
            B = dict(xt=scr, ysb=ysb, ss=ss, rs=rs, tp=tp, tpc=0)
            P.dma("sp", bgc[:], b_gate.rearrange("o (k p) -> p (o k)", p=128), w=["bgc"], sem="m_col", slow=True)
            P.dma("sp", nbf[:], b_forget[:, :], w=["nbf"], sem="m_col")
            P.op("dve", lambda h: h.tensor_scalar(out=nbf[:], in0=nbf[:], scalar1=-1.0, scalar2=None,
                                                  op0=ALU.mult), r=["nbf"], w=["nbf"])
            P.dma("pool", wf[:], w_in[:, o_ff:o_ff + HF].rearrange("(k p) n -> p k n", p=128), w=["wf"],
                  sem="wf", slow=True)
            cnt = 0
            acnt = 0
            ocnt = 0
            vcnt = 0
            NTT = TB // 512
            for blk in range(S // TB):
                tok0 = blk * TB
                own = tok0 < OWN
                norm_rows(P, lambda rb: xs[tok0 + rb * 128: tok0 + (rb + 1) * 128, :],
                          lambda rb: [], TB // 128, hT, AB[:, 2, :], AB[:, 3, :], B)
                P.dma("sp", posi[:], pos_in[0:1, tok0:tok0 + TB].partition_broadcast(128), w=["posi"], sem="posi")
                P.op("dve", lambda h: h.tensor_copy(out=ang[:], in_=posi[:]), r=["posi"], w=["ang"])
                P.op("dve", lambda h: h.tensor_scalar(out=ang[:], in0=ang[:], scalar1=invf, scalar2=None,
                                                      op0=ALU.mult), r=["ang"], w=["ang"])
                for tab, off, tk in ((stab, 0.0, "stab"), (ctab, 0.5 * math.pi, "ctab")):
                    P.op("dve", lambda h, off=off: h.tensor_scalar(out=rq[:], in0=ang[:], scalar1=off,
                                                                   scalar2=1.0 / (2 * math.pi), op0=ALU.add,
                                                                   op1=ALU.mult), r=["ang"], w=["rq"])
                    P.op("dve", lambda h: h.tensor_copy(out=posi[:], in_=rq[:]), r=["rq"], w=["posi"])
                    P.op("dve", lambda h: h.tensor_copy(out=rq[:], in_=posi[:]), r=["posi"], w=["rq"])
                    P.op("dve", lambda h, off=off: h.tensor_scalar(out=rr[:], in0=ang[:], scalar1=off, scalar2=None,
                                                                   op0=ALU.add), r=["ang"], w=["rr"])
                    P.op("dve", lambda h: h.scalar_tensor_tensor(out=rr[:], in0=rq[:], scalar=-2 * math.pi,
                                                                 in1=rr[:], op0=ALU.mult, op1=ALU.add),
                         r=["rq", "rr"], w=["rr"])
                    P.op("dve", lambda h: h.tensor_scalar(out=rq[:], in0=rr[:], scalar1=-math.pi, scalar2=1.0e6,
                                                          op0=ALU.add, op1=ALU.mult), r=["rr"], w=["rq"])
                    P.op("dve", lambda h: h.tensor_scalar(out=rq[:], in0=rq[:], scalar1=0.0, scalar2=1.0,
                                                          op0=ALU.max, op1=ALU.min), r=["rq"], w=["rq"])
                    P.op("dve", lambda h: h.scalar_tensor_tensor(out=rr[:], in0=rq[:], scalar=-2 * math.pi,
                                                                 in1=rr[:], op0=ALU.mult, op1=ALU.add),
                         r=["rq", "rr"], w=["rr"])
                    P.op("act", lambda h, tab=tab: h.activation(out=tab[:], in_=rr[:], func=AF.Sin),
                         r=["rr"], w=[tk])
                P.op("dve", lambda h: h.tensor_scalar(out=stab[:], in0=stab[:], scalar1=sgn, scalar2=None,
                                                      op0=ALU.mult), r=["stab"], w=["stab"])
                chunks = []
                for i in range(HF):
                    chunks.append(("k", o_fk + i * 128, kfd, i * 128))
                for i in range(2 * HD):
                    chunks.append(("kr", o_dk + i * 128, kdd, i * 128))
                if own:
                    for i in range(HF):
                        chunks.append(("q", o_fq + i * 128, qfd, i * 128))
                    for i in range(2 * HD):
                        chunks.append(("qr", o_dq + i * 128, qdd, i * 128))
                    for i in range(KC):
                        chunks.append(("g", o_ga + i * 128, gAd, i * 128, i))
                    for i in range(KC):
                        chunks.append(("g", o_gb + i * 128, gBd, i * 128, KC + i))
                hk_all = [("hT", rb) for rb in range(TB // 128)]
                SUB = cfg.get("sub", 9)
                if SUB < 1:
                    chunks = []
                if "kinds" in cfg:
                    chunks = [c_ for c_ in chunks if c_[0] in cfg["kinds"]]
                for ch in chunks:
                    kind, co, dst, ro = ch[0], ch[1], ch[2], ch[3]
                    sl = cnt % 4
                    cnt += 1
                    P.dma("pool", wi[sl][:], w_in[:, co:co + 128].rearrange("(k p) n -> p k n", p=128),
                          w=[("wi", sl)], sem=("wi", sl))
                    for tt in range(NTT):
                        a2 = acnt % 2
                        acnt += 1
                        hk = hk_all[tt * 4: tt * 4 + 4]

                        def mm(h, sl=sl, tt=tt, dstp=pA[a2]):
                            for k in range(KC):
                                i = h.matmul(dstp[:], lhsT=wi[sl][:, k, :], rhs=hT[:, k, tt * 512:(tt + 1) * 512],
                                             start=(k == 0), stop=(k == KC - 1))
                            return i
                        P.op("pe", mm, r=[("wi", sl)] + hk, w=[("pA", a2)])
                        ob = ocnt % 3
                        ocnt += 1
                        tsl = slice(tt * 512, (tt + 1) * 512)
                        dcol0 = tok0 + tt * 512
                        if kind == "k":
                            P.op("act", lambda h, a2=a2, ob=ob: h.copy(out=obf[ob][:], in_=pA[a2][:]),
                                 r=[("pA", a2)], w=[("obf", ob)])
                        elif kind == "q":
                            P.op("act", lambda h, a2=a2, ob=ob: h.mul(out=obf[ob][:], in_=pA[a2][:], mul=QSC),
                                 r=[("pA", a2)], w=[("obf", ob)])
                        elif kind == "g":
                            gi = ch[4]
                            P.op("act", lambda h, a2=a2, ob=ob, gi=gi: h.activation(
                                out=obf[ob][:], in_=pA[a2][:], func=AF.Sigmoid, bias=bgc[:, gi:gi + 1]),
                                r=[("pA", a2), "bgc"], w=[("obf", ob)])
                        else:
                            sc_ = QSC if kind == "qr" else 1.0
                            P.op("act", lambda h, a2=a2: h.copy(out=qb[a2][:], in_=pA[a2][:]),
                                 r=[("pA", a2)], w=[("qb", a2)])
                            if cfg.get("dbg", 0) != 1:
                                P.op("pe", lambda h, a2=a2: h.matmul(pB[a2][:], lhsT=perm, rhs=qb[a2][:],
                                                                     start=True, stop=True),
                                     r=[("qb", a2)], w=[("pB", a2)])
                            P.op("dve", lambda h, a2=a2, tsl=tsl: h.tensor_tensor(
                                out=t1[a2][:], in0=pA[a2][:], in1=ctab[:, tsl], op=ALU.mult),
                                r=[("pA", a2), "ctab", ("qb", a2)], w=[("t1", a2)])
                            P.op("dve", lambda h, a2=a2, tsl=tsl: h.tensor_tensor(
                                out=t2[a2][:], in0=pB[a2][:], in1=stab[:, tsl], op=ALU.mult),
                                r=[("pB", a2), "stab"], w=[("t2", a2)])
                            P.op("dve", lambda h, a2=a2: h.tensor_tensor(
                                out=t1[a2][:], in0=t1[a2][:], in1=t2[a2][:], op=ALU.add),
                                r=[("t1", a2), ("t2", a2)], w=[("t1", a2)])
                            P.op("act", lambda h, a2=a2, ob=ob, sc_=sc_: h.activation(
                                out=obf[ob][:], in_=t1[a2][:], func=AF.Copy, scale=sc_),
                                r=[("t1", a2)], w=[("obf", ob)])
                        P.dma("sp", dst[ro:ro + 128, dcol0:dcol0 + 512], obf[ob][:], r=[("obf", ob)],
                              sem=("obs", ob))
                for tt in range(NTT if SUB >= 2 else 0):
                    a2 = acnt % 2
                    acnt += 1

                    def mmf(h, tt=tt, dstp=pA[a2]):
                        for k in range(KC):
                            i = h.matmul(dstp[0:HF, :], lhsT=wf[:, k, :], rhs=hT[:, k, tt * 512:(tt + 1) * 512],
                                         start=(k == 0), stop=(k == KC - 1))
                        return i
                    P.op("pe", mmf, r=["wf"] + hk_all[tt * 4: tt * 4 + 4], w=[("pA", a2)])
                    P.op("act", lambda h, a2=a2: h.activation(out=ef[a2][:], in_=pA[a2][0:HF, :], func=AF.Exp,
                                                              bias=nbf[:, 0:1], scale=-1.0),
                         r=[("pA", a2), "nbf"], w=[("ef", a2)])
                    P.op("act", lambda h, a2=a2: h.activation(out=ef[a2][:], in_=ef[a2][:], func=AF.Ln,
                                                              bias=1.0), r=[("ef", a2)], w=[("ef", a2)])
                    P.dma("sp", nld[:, tok0 + tt * 512: tok0 + (tt + 1) * 512], ef[a2][:], r=[("ef", a2)],
                          sem=("efs", a2))
                for (co, dst, width) in (((o_fv, vfd, FW), (o_dv, vdd, DW)) if SUB >= 3 else ()):
                    for n in range(width // 256):
                        s = vcnt % 2
                        vcnt += 1
                        P.dma("pool", wv[s][:], w_in[:, co + n * 256: co + (n + 1) * 256].rearrange(
                            "(k p) n -> p k n", p=128), w=[("wv", s)], sem=("wv", s))
                        for rb in range(TB // 128):
                            v2 = acnt % 2
                            acnt += 1

                            def mmv(h, s=s, rb=rb, dstp=pV[v2]):
                                for k in range(KC):
                                    i = h.matmul(dstp[:, 0:256], lhsT=hT[:, k, rb * 128:(rb + 1) * 128],
                                                 rhs=wv[s][:, k, :], start=(k == 0), stop=(k == KC - 1))
                                return i
                            P.op("pe", mmv, r=[("wv", s), ("hT", rb)], w=[("pV", v2)])
                            P.op("act", lambda h, v2=v2: h.copy(out=vob[v2][:], in_=pV[v2][:, 0:256]),
                                 r=[("pV", v2)], w=[("vob", v2)])
                            P.dma("sp", dst[tok0 + rb * 128: tok0 + (rb + 1) * 128, n * 256:(n + 1) * 256],
                                  vob[v2][:], r=[("vob", v2)], sem=("vos", v2))
            P.barrier()
            P.emit()

        if UPTO < 3:
            return nc
        nbias = sb(top, "nbias", [128, NBLK * HF], F32)
        with ExitStack() as st:
            fa = sb(st, "fa", [HF, 2, OWN], F32)
            fb = sb(st, "fb", [HF, 2, OWN], F32)
            gtot = sb(st, "gtot", [HF, 1], F32)
            pc = ps(st, "pc", [128, 512])
            P.dma("sp", fa[:], nld.rearrange("h (a t) -> h a t", a=2), w=["fa"], sem="fa")
            cur, nxt, kc_, kn_ = fa, fb, "fa", "fb"
            d = 1
            while d < OWN:
                P.op("dve", lambda h, cur=cur, nxt=nxt, d=d: h.tensor_tensor(
                    out=nxt[:, :, d:], in0=cur[:, :, d:], in1=cur[:, :, 0:OWN - d], op=ALU.add),
                    r=[kc_], w=[kn_])
                P.op("dve", lambda h, cur=cur, nxt=nxt, d=d: h.tensor_copy(out=nxt[:, :, 0:d], in_=cur[:, :, 0:d]),
                     r=[kc_], w=[kn_])
                cur, nxt, kc_, kn_ = nxt, cur, kn_, kc_
                d *= 2
            P.op("dve", lambda h, cur=cur, nxt=nxt: h.tensor_scalar(out=nxt[:, 0, :], in0=cur[:, 0, :], scalar1=-1.0,
                                                                    scalar2=None, op0=ALU.mult), r=[kc_], w=[kn_])
            P.dma("sp", frd[:, :], nxt[:, 0, :], r=[kn_], w=["frd"], sem="frs")
            P.op("dve", lambda h, cur=cur: h.tensor_copy(out=gtot[:], in_=cur[:, 1, OWN - 1:OWN]), r=[kc_],
                 w=["gtot"])
            P.op("dve", lambda h, cur=cur: h.tensor_scalar(out=cur[:, 1, :], in0=cur[:, 1, :], scalar1=gtot[:, 0:1],
                                                           scalar2=flagc[0:HF, 0:1], op0=ALU.subtract,
                                                           op1=ALU.add), r=[kc_, "gtot"], w=[kc_])
            curf = cur.rearrange("h a t -> h (a t)")

            def trf(h, curf=curf):
                for i in range(NBLK):
                    ii = h.matmul(pc[:, i * HF:(i + 1) * HF], lhsT=curf[:, i * 128:(i + 1) * 128],
                                  rhs=identf[0:HF, 0:HF], start=True, stop=True)
                return ii
            P.op("pe", trf, r=[kc_], w=["pc"])
            P.op("dve", lambda h: h.tensor_copy(out=nbias[:], in_=pc[:, 0:NBLK * HF]), r=["pc"], w=["nbias"])
            if cfg.get("dbgout"):
                nbd = dscr("nbd", [128, NBLK * HF])
                P.dma("sp", nbd[:, :], nbias[:], r=["nbias"], sem="nbd")
            P.barrier()
            P.emit()

        if UPTO < 4:
            return nc
        with ExitStack() as st:
            kT = [sb(st, "kT%d" % i, [128, S], BF16) for i in range(2)]
            vS = [sb(st, "vS%d" % i, [128, NBLK, 256], BF16) for i in range(2)]
            qT = [sb(st, "qT%d" % i, [128, 512], BF16) for i in range(2)]
            Fb = [sb(st, "Fb%d" % i, [128, 512], F32) for i in range(2)]
            ein = [sb(st, "ein%d" % i, [128, 512], F32) for i in range(3)]
            Pt = [sb(st, "Pt%d" % i, [128, 512], BF16) for i in range(4)]
            rinv = sb(st, "rinv", [128, 512], F32)
            On = sb(st, "On", [128, 2, 512], F32)
            O1 = sb(st, "O1", [128, 2, OWN], F32)
            pre = sb(st, "pre", [128, 2, 512], F32)
            sq = sb(st, "sq", [128, 2, 512], BF16)
            rst = sb(st, "rst", [128, 512], F32)
            yo = [sb(st, "yo%d" % i, [128, 512], BF16) for i in range(2)]
            lamt = sb(st, "lamt", [128, 512], F32)
            lpr = sb(st, "lpr", [128, 256], F32)
            lsc = sb(st, "lsc", [128, 4], F32)
            subc = sb(st, "subc", [128, 2], F32)
            pS = [ps(st, "pS%d" % i, [128, 512]) for i in range(3)]
            pO = [ps(st, "pO%d" % i, [128, 512]) for i in range(2)]
            pR = [ps(st, "pR%d" % i, [128, 512]) for i in range(2)]
            pN = ps(st, "pN", [128, 512])
            lnr = sb(st, "lnr", [128, 512], F32)
            P.dma("sp", lamt[:], dlam[0:1, :].partition_broadcast(128), w=["lamt"], sem="lamt")
            P.op("dve", lambda h: h.tensor_tensor(out=lpr[:, 0:128], in0=lamt[:, 0:128], in1=lamt[:, 128:256],
                                                  op=ALU.mult), r=["lamt"], w=["lpr0"])
            P.op("dve", lambda h: h.tensor_tensor(out=lpr[:, 128:256], in0=lamt[:, 256:384], in1=lamt[:, 384:512],
                                                  op=ALU.mult), r=["lamt"], w=["lpr1"])
            P.op("dve", lambda h: h.reduce_sum(out=lsc[:, 0:1], in_=lpr[:, 0:128], axis=AX.X), r=["lpr0"],
                 w=["lsc0"])
            P.op("dve", lambda h: h.reduce_sum(out=lsc[:, 1:2], in_=lpr[:, 128:256], axis=AX.X), r=["lpr1"],
                 w=["lsc1"])
            P.op("act", lambda h: h.activation(out=lsc[:, 2:4], in_=lsc[:, 0:2], func=AF.Exp),
                 r=["lsc0", "lsc1"], w=["lsc2"])
            P.op("dve", lambda h: h.tensor_tensor(out=lsc[:, 0:1], in0=lsc[:, 3:4], in1=lsc[:, 2:3],
                                                  op=ALU.subtract), r=["lsc2"], w=["lsc0"])
            P.op("dve", lambda h: h.tensor_scalar(out=lsc[:, 0:1], in0=lsc[:, 0:1], scalar1=-LINIT, scalar2=None,
                                                  op0=ALU.add), r=["lsc0"], w=["nlam"])
            P.dma("sp", subc[:], subln.rearrange("o (c p) -> p (o c)", p=128), w=["subc"], sem="m_col", slow=True)
            P.op("dve", lambda h: h.tensor_scalar(out=subc[:], in0=subc[:], scalar1=(1.0 - LINIT), scalar2=None,
                                                  op0=ALU.mult), r=["subc"], w=["subc"])
            state = dict(hc=0, qc=0, sc=0, pc=0, yc=0, fc=0)

            passes = []
            NQ = OWN // 512
            LA = 2

            def attn_pass(krows, qrows, vsrc, vcol0, nvc, fhead, mask, out_cb):
                passes.append((krows, qrows, vsrc, vcol0, nvc, fhead, mask, out_cb))

            def load_kv(pi):
                krows, qrows, vsrc, vcol0, nvc, fhead, mask, out_cb = passes[pi]
                hs = pi % 2
                P.dma("sp", kT[hs][:], krows, w=[("kT", hs)], sem=("kT", hs))
                P.dma("sp", vS[hs][:, :, 0:nvc * 128],
                      vsrc[:, vcol0:vcol0 + nvc * 128].rearrange("(b p) n -> p b n", p=128),
                      w=[("vS", hs)], sem=("vS", hs))

            def load_q(ti):
                pi, qt = divmod(ti, NQ)
                krows, qrows, vsrc, vcol0, nvc, fhead, mask, out_cb = passes[pi]
                qs = ti % 2
                P.dma("sp", qT[qs][:], qrows[:, qt * 512:(qt + 1) * 512], w=[("qT", qs)], sem=("qT", qs))
                if fhead is not None:
                    P.dma("sp", Fb[qs][:], frd[fhead:fhead + 1, qt * 512:(qt + 1) * 512].partition_broadcast(128),
                          w=[("Fb", qs)], sem=("Fb", qs))

            def run_passes():
                total = len(passes) * NQ
                load_kv(0)
                load_q(0)
                for ti in range(total):
                    pi, qt = divmod(ti, NQ)
                    if qt == 0 and pi + 1 < len(passes):
                        load_kv(pi + 1)
                    if ti + 1 < total:
                        load_q(ti + 1)
                    attn_tile(pi, qt, pi % 2, ti % 2)

            def attn_tile(pi, qt, hs, qs):
                krows, qrows, vsrc, vcol0, nvc, fhead, mask, out_cb = passes[pi]
                fb = state["fc"] % 2
                state["fc"] += 1
                pOb = [pO[fb]] if nvc == 1 else [pO[0], pO[1]]
                pOk = [("pO", fb)] if nvc == 1 else [("pO", 0), ("pO", 1)]
                pRb = pR[fb]
                pRk = ("pR", fb)
                if True:
                    tiles = [(OB + i, 0, False, True) for i in range(OB)]
                    for i in range(4 * qt + 4):
                        jd = i - 4 * qt
                        tiles.append((i, 128 * jd if jd > 0 else 0, jd >= 0, False))
                    nt_ = len(tiles)
                    sbank = {}

                    def qk(idx):
                        kt, c0, diag, oth = tiles[idx]
                        sbk = state["sc"] % 3
                        state["sc"] += 1
                        sbank[idx] = sbk
                        P.op("pe", lambda h, kt=kt, c0=c0, sbk=sbk, qs=qs, hs=hs: h.matmul(
                            pS[sbk][:, c0:512], lhsT=kT[hs][:, kt * 128:(kt + 1) * 128], rhs=qT[qs][:, c0:512],
                            start=True, stop=True), r=[("kT", hs), ("qT", qs)], w=[("pS", sbk)])
                    for idx in range(min(LA, nt_)):
                        qk(idx)
                    for idx in range(nt_):
                        if idx + LA < nt_:
                            qk(idx + LA)
                        kt, c0, diag, oth = tiles[idx]
                        sbk = sbank[idx]
                        pi = state["pc"] % 4
                        state["pc"] += 1
                        if fhead is not None:
                            P.op("dve", lambda h, sbk=sbk, c0=c0, qs=qs: h.tensor_tensor(
                                out=ein[sbk][:, c0:512], in0=pS[sbk][:, c0:512], in1=Fb[qs][:, c0:512], op=ALU.add),
                                r=[("pS", sbk), ("Fb", qs)], w=[("ein", sbk)])
                            if diag:
                                P.op("dve", lambda h, sbk=sbk, c0=c0: h.tensor_tensor(
                                    out=ein[sbk][:, c0:c0 + 128], in0=ein[sbk][:, c0:c0 + 128], in1=negtri,
                                    op=ALU.add), r=[("ein", sbk)], w=[("ein", sbk)])
                            bcol = nbias[:, kt * HF + fhead: kt * HF + fhead + 1]
                            P.op("act", lambda h, sbk=sbk, c0=c0, pi=pi, bcol=bcol: h.activation(
                                out=Pt[pi][:, c0:512], in_=ein[sbk][:, c0:512], func=AF.Exp, bias=bcol),
                                r=[("ein", sbk)], w=[("Pt", pi)])
                        else:
                            if oth:
                                P.op("act", lambda h, sbk=sbk, pi=pi: h.activation(
                                    out=Pt[pi][:, :], in_=pS[sbk][:, :], func=AF.Exp, bias=flagc[:, 0:1]),
                                    r=[("pS", sbk)], w=[("Pt", pi)])
                            else:
                                P.op("act", lambda h, sbk=sbk, pi=pi, c0=c0: h.activation(
                                    out=Pt[pi][:, c0:512], in_=pS[sbk][:, c0:512], func=AF.Exp),
                                    r=[("pS", sbk)], w=[("Pt", pi)])
                        if diag and fhead is None:
                            P.op("dve", lambda h, pi=pi, c0=c0: h.tensor_tensor(
                                out=Pt[pi][:, c0:c0 + 128], in0=Pt[pi][:, c0:c0 + 128], in1=mask, op=ALU.mult),
                                r=[("Pt", pi)], w=[("Pt", pi)])
                        first, last = (idx == 0), (idx == nt_ - 1)

                        def pv(h, kt=kt, c0=c0, pi=pi, first=first, last=last, hs=hs, nvc=nvc, pOb=pOb, pRb=pRb):
                            for vc in range(nvc):
                                h.matmul(pOb[vc][:, c0:512], lhsT=vS[hs][:, kt, vc * 128:(vc + 1) * 128],
                                         rhs=Pt[pi][:, c0:512], start=first, stop=last)
                            return h.matmul(pRb[:, c0:512], lhsT=ones, rhs=Pt[pi][:, c0:512], start=first, stop=last)
                        P.op("pe", pv, r=[("Pt", pi), ("vS", hs)], w=pOk + [pRk])
                    P.op("act", lambda h, pRb=pRb: h.activation(out=lnr[:], in_=pRb[:], func=AF.Ln), r=[pRk], w=["lnr"])
                    P.op("act", lambda h: h.activation(out=rinv[:], in_=lnr[:], func=AF.Exp, scale=-1.0),
                         r=["lnr"], w=["rinv"])
                    for vc in range(nvc):
                        P.op("dve", lambda h, vc=vc, pOb=pOb: h.tensor_tensor(out=On[:, vc, :], in0=pOb[vc][:],
                                                                             in1=rinv[:], op=ALU.mult),
                             r=[pOk[vc], "rinv"], w=[("On", vc)])
                    out_cb(qt)

            for hh in range(HF):
                def cb_f(qt, hh=hh):
                    y = state["yc"] % 2
                    state["yc"] += 1
                    P.op("act", lambda h, y=y: h.copy(out=yo[y][:], in_=On[:, 0, :]), r=[("On", 0)], w=[("yo", y)])
                    P.dma("sp", yad[hh * 128:(hh + 1) * 128, qt * 512:(qt + 1) * 512], yo[y][:], r=[("yo", y)],
                          sem=("yos", y))
                attn_pass(kfd[hh * 128:(hh + 1) * 128, :], qfd[hh * 128:(hh + 1) * 128, :], vfd, hh * 128, 1, hh,
                          trim, cb_f)
            for hh in range(HD):
                def cb_1(qt):
                    for vc in range(2):
                        P.op("act", lambda h, vc=vc, qt=qt: h.copy(out=O1[:, vc, qt * 512:(qt + 1) * 512],
                                                                    in_=On[:, vc, :]),
                             r=[("On", vc)], w=[("O1", qt, vc)])

                def cb_2(qt, hh=hh):
                    for vc in range(2):
                        P.op("dve", lambda h, vc=vc, qt=qt: h.scalar_tensor_tensor(
                            out=pre[:, vc, :], in0=On[:, vc, :], scalar=lsc[:, 0:1],
                            in1=O1[:, vc, qt * 512:(qt + 1) * 512], op0=ALU.mult, op1=ALU.add),
                            r=[("On", vc), ("O1", qt, vc), "nlam"], w=[("pre", vc)])
                        P.op("act", lambda h, vc=vc: h.activation(out=sq[:, vc, :], in_=pre[:, vc, :],
                                                                  func=AF.Square),
                             r=[("pre", vc)], w=[("sq", vc)])

                    def mmn(h):
                        h.matmul(pN[:], lhsT=ones, rhs=sq[:, 0, :], start=True, stop=False)
                        return h.matmul(pN[:], lhsT=ones, rhs=sq[:, 1, :], start=False, stop=True)
                    P.op("pe", mmn, r=[("sq", 0), ("sq", 1)], w=["pN"])
                    P.op("dve", lambda h: h.tensor_scalar(out=rst[:], in0=pN[:], scalar1=1.0 / 256, scalar2=EPS,
                                                          op0=ALU.mult, op1=ALU.add), r=["pN"], w=["rst"])
                    P.op("act", lambda h: h.sqrt(out=rst[:], in_=rst[:]), r=["rst"], w=["rst"])
                    P.op("dve", lambda h: h.reciprocal(out=rst[:], in_=rst[:]), r=["rst"], w=["rst"])
                    for vc in range(2):
                        y = state["yc"] % 2
                        state["yc"] += 1
                        P.op("dve", lambda h, vc=vc: h.tensor_tensor(out=pre[:, vc, :], in0=pre[:, vc, :],
                                                                     in1=rst[:], op=ALU.mult),
                             r=[("pre", vc), "rst"], w=[("pre", vc)])
                        P.op("act", lambda h, vc=vc, y=y: h.activation(out=yo[y][:], in_=pre[:, vc, :], func=AF.Copy,
                                                                       scale=subc[:, vc:vc + 1]),
                             r=[("pre", vc), "subc"], w=[("yo", y)])
                        r0 = hh * 256 + vc * 128
                        P.dma("sp", ybd[r0:r0 + 128, qt * 512:(qt + 1) * 512], yo[y][:], r=[("yo", y)],
                              sem=("yos", y))
                for mp, cbk in ((0, cb_1), (1, cb_2)):
                    r0 = (2 * hh + mp) * 128
                    attn_pass(kdd[r0:r0 + 128, :], qdd[r0:r0 + 128, :], vdd, hh * 256, 2, None, chkm, cbk)
            run_passes()
            P.barrier()
            P.emit()

        if UPTO < 5:
            return nc
        TB5 = 512
        with ExitStack() as st:
            mT = sb(st, "mT", [128, KC, TB5], BF16)
            yaS = sb(st, "yaS", [128, FW // 128, TB5], BF16)
            ybS = sb(st, "ybS", [128, DW // 128, TB5], BF16)
            wa = [sb(st, "wa%d" % i, [128, FW // 128, 128], BF16) for i in range(2)]
            wd = [sb(st, "wd%d" % i, [128, DW // 128, 128], BF16) for i in range(2)]
            wo = [sb(st, "wo%d" % i, [128, KC, 256], BF16) for i in range(2)]
            gA = [sb(st, "gA%d" % i, [128, TB5], BF16) for i in range(2)]
            gB = [sb(st, "gB%d" % i, [128, TB5], BF16) for i in range(2)]
            u1 = [sb(st, "u1%d" % i, [128, TB5], F32) for i in range(2)]
            u2 = [sb(st, "u2%d" % i, [128, TB5], F32) for i in range(2)]
            gbt = sb(st, "gbt", [128, D], F32)
            xo = [sb(st, "xo%d" % i, [128, 4, 256], F32) for i in range(4)]
            tmp = [sb(st, "tmp%d" % i, [128, 256], F32) for i in range(2)]
            pa = [ps(st, "pa%d" % i, [128, 512]) for i in range(2)]
            pb_ = [ps(st, "pb%d" % i, [128, 512]) for i in range(2)]
            po = [ps(st, "po%d" % i, [128, 512]) for i in range(2)]
            P.dma("sp", gbt[:], modv[5:6, :].partition_broadcast(128), w=["gbt"], sem="gbt")
            mc = 0
            oc = 0
            xc = 0
            pcn = 0
            for blk in range(OWN // TB5):
                tok0 = blk * TB5
                P.dma("sp", yaS[:], yad[:, tok0:tok0 + TB5].rearrange("(c p) t -> p c t", p=128), w=["yaS"],
                      sem="yaS")
                P.dma("sp", ybS[:], ybd[:, tok0:tok0 + TB5].rearrange("(c p) t -> p c t", p=128), w=["ybS"],
                      sem="ybS")
                for m in range(KC):
                    s = mc % 2
                    mc += 1
                    P.dma("pool", wa[s][:], w_of[:, m * 128:(m + 1) * 128].rearrange("(c p) n -> p c n", p=128),
                          w=[("wa", s)], sem=("wa", s))
                    P.dma("pool", wd[s][:], w_od[:, m * 128:(m + 1) * 128].rearrange("(c p) n -> p c n", p=128),
                          w=[("wd", s)], sem=("wd", s))
                    P.dma("sp", gA[s][:], gAd[m * 128:(m + 1) * 128, tok0:tok0 + TB5], w=[("gA", s)], sem=("gA", s))
                    P.dma("sp", gB[s][:], gBd[m * 128:(m + 1) * 128, tok0:tok0 + TB5], w=[("gB", s)], sem=("gB", s))

                    def mma(h, s=s):
                        n_ = FW // 128
                        for c_ in range(n_):
                            i = h.matmul(pa[s][:], lhsT=wa[s][:, c_, :], rhs=yaS[:, c_, :], start=(c_ == 0),
                                         stop=(c_ == n_ - 1))
                        return i

                    def mmd(h, s=s):
                        n_ = DW // 128
                        for c_ in range(n_):
                            i = h.matmul(pb_[s][:], lhsT=wd[s][:, c_, :], rhs=ybS[:, c_, :], start=(c_ == 0),
                                         stop=(c_ == n_ - 1))
                        return i
                    P.op("pe", mma, r=[("wa", s), "yaS"], w=[("pa", s)])
                    P.op("pe", mmd, r=[("wd", s), "ybS"], w=[("pb", s)])
                    P.op("dve", lambda h, s=s: h.tensor_tensor(out=u1[s][:], in0=pa[s][:], in1=gA[s][:], op=ALU.mult),
                         r=[("pa", s), ("gA", s)], w=[("u1", s)])
                    P.op("dve", lambda h, s=s: h.tensor_tensor(out=u2[s][:], in0=pb_[s][:], in1=gB[s][:], op=ALU.mult),
                         r=[("pb", s), ("gB", s)], w=[("u2", s)])
                    P.op("dve", lambda h, s=s, m=m: h.tensor_tensor(out=mT[:, m, :], in0=u1[s][:], in1=u2[s][:],
                                                                    op=ALU.add),
                         r=[("u1", s), ("u2", s)], w=[("mT", m)])
                mk = [("mT", m) for m in range(KC)]
                def xload5(n, tok0=tok0, xc=xc):
                    xsl = (xc + n) % 4
                    P.dma("sp", xo[xsl][:], xs[tok0:tok0 + TB5, n * 256:(n + 1) * 256]
                          .rearrange("(r p) n -> p r n", p=128), w=[("xo", xsl)], sem=("xol", xsl))
                NN = D // 256
                for n in range(min(2, NN)):
                    xload5(n)
                for n in range(NN):
                    s = oc % 2
                    oc += 1
                    P.dma("pool", wo[s][:], w_o[:, n * 256:(n + 1) * 256].rearrange("(k p) n -> p k n", p=128),
                          w=[("wo", s)], sem=("wo", s))
                    xsl = (xc + n) % 4
                    dk = ("xs5", blk, n)
                    if n + 2 < NN:
                        xload5(n + 2)
                    for rb in range(TB5 // 128):
                        pb2 = pcn % 2
                        pcn += 1

                        def mmo(h, s=s, rb=rb, dstp=po[pb2]):
                            for k in range(KC):
                                i = h.matmul(dstp[:, 0:256], lhsT=mT[:, k, rb * 128:(rb + 1) * 128], rhs=wo[s][:, k, :],
                                             start=(k == 0), stop=(k == KC - 1))
                            return i
                        P.op("pe", mmo, r=[("wo", s)] + mk, w=[("po", pb2)])
                        P.op("dve", lambda h, pb2=pb2, n=n: h.tensor_tensor(
                            out=tmp[pb2][:], in0=po[pb2][:, 0:256], in1=gbt[:, n * 256:(n + 1) * 256], op=ALU.mult),
                            r=[("po", pb2), "gbt"], w=[("tmp", pb2)])
                        P.op("dve", lambda h, pb2=pb2, xsl=xsl, rb=rb: h.tensor_tensor(
                            out=xo[xsl][:, rb, :], in0=tmp[pb2][:], in1=xo[xsl][:, rb, :], op=ALU.add),
                            r=[("tmp", pb2), ("xo", xsl)], w=[("xo", xsl)])
                    P.dma("sp", xs[tok0:tok0 + TB5, n * 256:(n + 1) * 256].rearrange("(r p) n -> p r n", p=128),
                          xo[xsl][:], r=[("xo", xsl)], w=[dk], sem=("xos", xsl))
                xc += NN
            P.barrier()
            P.emit()

        if UPTO < 6:
            return nc
        ffn_stage(P, "f2", OWN, xs, f2_in, f2_out, AB[:, 4, :], AB[:, 5, :], 8)

        with ExitStack() as st:
            Af = sb(st, "Af", [128, D], F32)
            Bf = sb(st, "Bf", [128, D], F32)
            nf = sb(st, "nf", [128, D], F32)
            xt = [sb(st, "xt%d" % i, [128, D], F32) for i in range(2)]
            junk = sb(st, "junk", [128, D], BF16)
            ss = [sb(st, "ss%d" % i, [128, 1], F32) for i in range(2)]
            rs = [sb(st, "rs%d" % i, [128, 1], F32) for i in range(2)]
            P.dma("sp", Af[:], modv[10:11, :].partition_broadcast(128), w=["Af"], sem="Af")
            P.dma("sp", Bf[:], modv[9:10, :].partition_broadcast(128), w=["Bf"], sem="Bf")
            P.dma("sp", nf[:], norms[3:4, :].partition_broadcast(128), w=["nf"], sem="nf")
            P.op("dve", lambda h: h.scalar_tensor_tensor(out=Af[:], in0=Af[:], scalar=1.0, in1=nf[:], op0=ALU.add,
                                                         op1=ALU.mult), r=["Af", "nf"], w=["Af"])
            for rb in range(OB):
                s = rb % 2
                P.dma("sp", xt[s][:], xs[rb * 128:(rb + 1) * 128, :], w=[("xt", s)], sem=("xtl", s))
                P.op("dve", lambda h, s=s: h.memset(ss[s][:], 0.0), w=[("ss", s)])
                P.op("act", lambda h, s=s: h.activation(out=junk[:], in_=xt[s][:], func=AF.Square, accum_out=ss[s][:]),
                     r=[("xt", s)], w=["junk", ("ss", s)])
                P.op("dve", lambda h, s=s: h.tensor_scalar(out=rs[s][:], in0=ss[s][:], scalar1=1.0 / D, scalar2=EPS,
                                                           op0=ALU.mult, op1=ALU.add), r=[("ss", s)], w=[("rs", s)])
                P.op("act", lambda h, s=s: h.sqrt(out=rs[s][:], in_=rs[s][:]), r=[("rs", s)], w=[("rs", s)])
                P.op("dve", lambda h, s=s: h.reciprocal(out=rs[s][:], in_=rs[s][:]), r=[("rs", s)], w=[("rs", s)])
                P.op("dve", lambda h, s=s: h.scalar_tensor_tensor(out=xt[s][:], in0=xt[s][:], scalar=rs[s][:, 0:1],
                                                                  in1=Af[:], op0=ALU.mult, op1=ALU.mult),
                     r=[("xt", s), ("rs", s), "Af"], w=[("xt", s)])
                P.op("dve", lambda h, s=s: h.tensor_tensor(out=xt[s][:], in0=xt[s][:], in1=Bf[:], op=ALU.add),
                     r=[("xt", s), "Bf"], w=[("xt", s)])
                P.dma("sp", out[rb * 128:(rb + 1) * 128, :], xt[s][:], r=[("xt", s)], sem=("xts", s))
            P.barrier()
            P.emit()
    return nc


def make_in_maps(cfg, inputs):
    D, S, B = cfg["D"], cfg["S"], cfg["B"]
    OWN = S // 2
    g = lambda k: np.asarray(inputs[k])
    cbh, cfh = host_consts()
    shared = {
        "constb": cbh, "constf": cfh,
        "ada_w": g("ada_w")[0], "ada_b": g("ada_b")[0][None, :],
        "final_ada_w": g("final_ada_w"), "final_ada_b": g("final_ada_b")[None, :],
        "norms": np.stack([g("norm_ffn1")[0], g("norm_mix")[0], g("norm_ffn2")[0], g("norm_final")]),
        "ffn1_w_in": g("ffn1_w_in")[0], "ffn1_w_out": g("ffn1_w_out")[0],
        "ffn2_w_in": g("ffn2_w_in")[0], "ffn2_w_out": g("ffn2_w_out")[0],
        "w_in": g("w_in")[0], "b_forget": g("b_forget")[0][:, None], "b_gate": g("b_gate")[0][None, :],
        "diff_lambda": g("diff_lambda")[0].reshape(1, 512), "diff_subln": g("diff_subln")[0][None, :],
        "w_o_fox": g("w_o_fox")[0], "w_o_diff": g("w_o_diff")[0], "w_out": g("w_out")[0],
    }
    shared = {k: np.ascontiguousarray(v) for k, v in shared.items()}
    x, c, pos = g("x"), g("c"), g("positions")
    maps = []
    for core in range(2 * B):
        b, r = core // 2, core % 2
        order = np.concatenate([np.arange(r * OWN, (r + 1) * OWN), np.arange((1 - r) * OWN, (2 - r) * OWN)])
        m = dict(shared)
        m["x"] = np.ascontiguousarray(x[b][order])
        m["c"] = np.ascontiguousarray(c[b][None, :])
        m["pos"] = np.ascontiguousarray(pos[b][order][None, :].astype(np.int32))
        m["flag"] = np.full((128, 1), 0.0 if r == 1 else -30000.0, np.float32)
        maps.append(m)
    return maps


def run(cfg, inputs):
    nc = build(cfg)
    maps = make_in_maps(cfg, inputs)
    ncores = 2 * cfg["B"]
    res = run_bass_kernel_spmd(nc, maps, core_ids=list(range(ncores)))
    if cfg.get("dbgout"):
        return res.results
    D, S, B = cfg["D"], cfg["S"], cfg["B"]
    OWN = S // 2
    outp = np.empty((B, S, D), np.float32)
    for core in range(ncores):
        b, r = core // 2, core % 2
        outp[b, r * OWN:(r + 1) * OWN] = res.results[core]["out"]
    return outp


def kernel(**inputs):
    return run(FULL, inputs)
```

```python
import math
from contextlib import ExitStack
import numpy as np
import ml_dtypes
import concourse.bass as bass
import concourse.mybir as mybir
from concourse.bass_utils import run_bass_kernel_spmd

F32 = mybir.dt.float32
BF16 = mybir.dt.bfloat16
I32 = mybir.dt.int32
AF = mybir.ActivationFunctionType
ALU = mybir.AluOpType
AX = mybir.AxisListType
EPS = 1e-6
ENGS = ("pe", "act", "dve", "pool", "sp")
FULL = dict(D=4096, S=4096, DFF=11008, TB=1024, NG=11, B=4)


class Op:
    __slots__ = ("eng", "fn", "deps", "sig", "sem", "val")


class Prog:
    def __init__(self, nc, stack):
        self.nc = nc
        self.stack = stack
        self.handles = {"pe": nc.tensor, "act": nc.scalar, "dve": nc.vector, "pool": nc.gpsimd, "sp": nc.sync}
        self.esem = {e: stack.enter_context(nc.semaphore("es_" + e)) for e in ENGS}
        self.ecount = {e: 0 for e in ENGS}
        self.dsem = {}
        self.dcnt = {}
        self.dlast = {}
        self.known = {e: {} for e in ENGS}
        self.lastop = {e: None for e in ENGS}
        self.reset()

    def reset(self):
        self.ops = {e: [] for e in ENGS}
        self.lw = {}
        self.rd = {}

    def op(self, eng, fn, r=(), w=(), dma=None):
        o = Op()
        o.eng = eng
        o.fn = fn
        o.sig = False
        o.sem = None
        o.val = 0
        deps = []
        for k in r:
            p = self.lw.get(k)
            if p is not None:
                deps.append(p)
        for k in w:
            p = self.lw.get(k)
            if p is not None:
                deps.append(p)
            deps.extend(self.rd.get(k, ()))
        if dma is not None:
            if dma not in self.dsem:
                self.dsem[dma] = self.stack.enter_context(self.nc.semaphore("ds%d" % len(self.dsem)))
                self.dcnt[dma] = 0
            o.sem = dma
            prev = self.dlast.get(dma)
            if prev is not None:
                deps.append(prev)
            self.dlast[dma] = o
            self.dcnt[dma] += 1
            o.val = 16 * self.dcnt[dma]
        seen = set()
        dd = []
        for d in deps:
            if d is o or id(d) in seen:
                continue
            seen.add(id(d))
            if d.sem is None and d.eng == "pe" and eng == "pe":
                continue
            if d.sem is None:
                d.sig = True
            dd.append(d)
        o.deps = dd
        for k in r:
            self.rd.setdefault(k, []).append(o)
        for k in w:
            self.lw[k] = o
            self.rd[k] = []
        self.ops[eng].append(o)
        self.lastop[eng] = o
        return o

    def dma(self, q, out, in_, r=(), w=(), sem=None, slow=False):
        def fn(h, out=out, in_=in_, slow=slow):
            if slow:
                return h.dma_start(out=out, in_=in_, allow_slow_non_contiguous=True)
            return h.dma_start(out=out, in_=in_)
        return self.op(q, fn, r=r, w=w, dma=sem)

    def barrier(self):
        lasts = [self.lastop[e] for e in ENGS if self.lastop[e] is not None]
        dl = list(self.dlast.values())
        for e in ENGS:
            o = Op()
            o.eng = e
            o.fn = None
            o.sig = False
            o.sem = None
            o.val = 0
            dd = []
            for d in lasts + dl:
                if d.eng == e and d.sem is None:
                    continue
                if d.sem is None:
                    d.sig = True
                dd.append(d)
            o.deps = dd
            self.ops[e].append(o)
        self.lw = {}
        self.rd = {}

    def emit(self):
        for e in ENGS:
            n = self.ecount[e]
            for o in self.ops[e]:
                if o.sem is None and o.sig:
                    n += 1
                    o.val = n
            self.ecount[e] = n
        with self.nc.Block() as block:
            def run(e, h):
                known = self.known[e]
                for o in self.ops[e]:
                    for d in o.deps:
                        if d.sem is not None:
                            sh = self.dsem[d.sem]
                            key = ("d", d.sem)
                        else:
                            sh = self.esem[d.eng]
                            key = ("e", d.eng)
                        if known.get(key, 0) >= d.val:
                            continue
                        h.wait_ge(sh, d.val)
                        known[key] = d.val
                    if o.fn is None:
                        continue
                    inst = o.fn(h)
                    if o.sem is not None:
                        inst.then_inc(self.dsem[o.sem], 16)
                    elif o.sig:
                        inst.then_inc(self.esem[e], 1)

            @block.tensor
            def _(h):
                run("pe", h)

            @block.scalar
            def _(h):
                run("act", h)

            @block.vector
            def _(h):
                run("dve", h)

            @block.gpsimd
            def _(h):
                run("pool", h)

            @block.sync
            def _(h):
                run("sp", h)
        self.reset()


def host_consts():
    p = np.arange(128)
    cb = np.zeros((128, 5 * 128), np.float32)
    cb[:, 0:128] = np.eye(128)
    cb[:, 128:256] = 1.0
    cb[:, 256:384] = (p[:, None] <= p[None, :])
    cb[:, 384:512] = 1.0 - ((p[:, None] >= 64) & (p[None, :] < 64))
    perm = np.zeros((128, 128), np.float32)
    for m in range(32):
        perm[(m + 16) % 32, m] = 1.0
    cb[:, 512:640] = perm
    cf = np.zeros((128, 8 + 256), np.float32)
    cf[:, 136:264] = np.where(p[:, None] <= p[None, :], 0.0, -30000.0)
    invf = np.zeros(128, np.float32)
    invf[:32] = (500000.0 ** (-(np.arange(0, 32, 2, dtype=np.float32)) / 32.0))[np.arange(32) % 16]
    cf[:, 0] = invf
    cf[:16, 1] = -1.0
    cf[16:32, 1] = 1.0
    cf[:, 8:136] = np.eye(128)
    return cb.astype(ml_dtypes.bfloat16), cf


def build(cfg):
    D, S, DFF, TB, NG = cfg["D"], cfg["S"], cfg["DFF"], cfg["TB"], cfg["NG"]
    KC = D // 128
    OWN = S // 2
    HF = D // 256
    HD = D // 512
    FW = HF * 128
    DW = HD * 256
    JC = DFF // 128
    NBLK = S // 128
    OB = OWN // 128
    QSC = 128.0 ** -0.5
    LINIT = 0.8 - 0.6 * math.exp(0.0)
    nc = bass.Bass("TRN2", target_bir_lowering=False)

    def din(name, shape, dt=F32):
        return nc.dram_tensor(name, list(shape), dt, kind="ExternalInput").ap()

    def dscr(name, shape, dt=F32):
        kind = "ExternalOutput" if cfg.get("dbgout") else "Internal"
        return nc.dram_tensor(name, list(shape), dt, kind=kind).ap()

    x_in = din("x", [S, D])
    c_in = din("c", [1, D])
    pos_in = din("pos", [1, S], I32)
    flag_in = din("flag", [128, 1])
    constb_in = din("constb", [128, 640], BF16)
    constf_in = din("constf", [128, 264])
    ada_w = din("ada_w", [D, 9 * D])
    ada_b = din("ada_b", [1, 9 * D])
    fin_w = din("final_ada_w", [D, 2 * D])
    fin_b = din("final_ada_b", [1, 2 * D])
    norms = din("norms", [4, D])
    f1_in = din("ffn1_w_in", [D, 2 * DFF])
    f1_out = din("ffn1_w_out", [DFF, D])
    f2_in = din("ffn2_w_in", [D, 2 * DFF])
    f2_out = din("ffn2_w_out", [DFF, D])
    w_in = din("w_in", [D, 3 * FW + HF + 3 * DW + 2 * D])
    b_forget = din("b_forget", [HF, 1])
    b_gate = din("b_gate", [1, 2 * D])
    dlam = din("diff_lambda", [1, 512])
    subln = din("diff_subln", [1, 256])
    w_of = din("w_o_fox", [FW, D])
    w_od = din("w_o_diff", [DW, D])
    w_o = din("w_out", [D, D])
    out = nc.dram_tensor("out", [OWN, D], F32, kind="ExternalOutput").ap()

    xs = dscr("xs", [S, D])
    modv = dscr("modv", [11, D])
    kfd = dscr("kfd", [FW, S], BF16)
    kdd = dscr("kdd", [DW, S], BF16)
    vfd = dscr("vfd", [S, FW], BF16)
    vdd = dscr("vdd", [S, DW], BF16)
    qfd = dscr("qfd", [FW, OWN], BF16)
    qdd = dscr("qdd", [DW, OWN], BF16)
    gAd = dscr("gAd", [D, OWN], BF16)
    gBd = dscr("gBd", [D, OWN], BF16)
    nld = dscr("nld", [HF, S])
    frd = dscr("frd", [HF, OWN])
    yad = dscr("yad", [FW, OWN], BF16)
    ybd = dscr("ybd", [DW, OWN], BF16)

    groups = []
    j = 0
    while j < JC:
        n = min(NG, JC - j)
        groups.append((j, n))
        j += n

    with ExitStack() as top:
        P = Prog(nc, top)

        uid = [0]

        def sb(st, name, shape, dt):
            uid[0] += 1
            return st.enter_context(nc.sbuf_tensor("%s_%d" % (name, uid[0]), list(shape), dt))

        def ps(st, name, shape, dt=F32):
            uid[0] += 1
            return st.enter_context(nc.psum_tensor("%s_%d" % (name, uid[0]), list(shape), dt))

        cb = sb(top, "cb", [128, 640], BF16)
        cf = sb(top, "cf", [128, 264], F32)
        flagc = sb(top, "flagc", [128, 1], F32)
        cols = sb(top, "cols", [128, 12, KC], F32)
        ident = cb[:, 0:128]
        ones = cb[:, 128:256]
        trim = cb[:, 256:384]
        chkm = cb[:, 384:512]
        perm = cb[:, 512:640]
        invf = cf[:, 0:1]
        sgn = cf[:, 1:2]
        identf = cf[:, 8:136]
        negtri = cf[:, 136:264]
        P.dma("sp", cb[:], constb_in[:, :], w=["cb"], sem="c0")
        P.dma("sp", cf[:], constf_in[:, :], w=["cf"], sem="c1")
        P.dma("sp", flagc[:], flag_in[:, :], w=["flagc"], sem="c2")

        def colview(v):
            return cols[:, v, :]

        ccol = sb(top, "ccol", [128, KC], F32)
        sccol = sb(top, "sccol", [128, KC], BF16)
        NT0 = D // 512

        def mod_units(vs, Wt, bt, mt, pm, pmk):
            units = []
            for v in vs:
                Wsrc, col0 = (ada_w, v * D) if v < 9 else (fin_w, (v - 9) * D)
                bsrc = ada_b if v < 9 else fin_b
                for nt in range(NT0):
                    s = len(units) % 2
                    c0 = col0 + nt * 512

                    def dma_fn(s=s, c0=c0, Wsrc=Wsrc, bsrc=bsrc):
                        P.dma("sp", bt[s][:], bsrc[0:1, c0:c0 + 512], w=[("bt", s)], sem=("bt", s))
                        P.dma("pool", Wt[s][:], Wsrc[:, c0:c0 + 512].rearrange("(k p) n -> p k n", p=128),
                              w=[("mW", s)], sem=("mW", s))

                    def comp_fn(s=s, v=v, nt=nt):
                        def mm(h, s=s):
                            for k in range(KC):
                                i = h.matmul(pm[s][0:1, :], lhsT=sccol[:, k:k + 1], rhs=Wt[s][:, k, :],
                                             start=(k == 0), stop=(k == KC - 1))
                            return i
                        P.op("pe", mm, r=[("mW", s), "sccol"], w=[pmk[s]])
                        P.op("dve", lambda h, s=s: h.tensor_tensor(out=mt[s][:], in0=pm[s][0:1, :], in1=bt[s][:],
                                                                  op=ALU.add),
                             r=[pmk[s], ("bt", s)], w=[("mt", s)])
                        P.dma("sp", modv[v:v + 1, nt * 512:(nt + 1) * 512], mt[s][:], r=[("mt", s)],
                              w=[("modv", v, nt)], sem=("mts", s))
                        if nt == NT0 - 1:
                            P.dma("sp", cols[:, v, :], modv[v:v + 1, :].rearrange("o (k p) -> p (o k)", p=128),
                                  r=[("modv", v, n_) for n_ in range(NT0)], w=[("cols", v)], sem="m_col", slow=True)
                    units.append((dma_fn, comp_fn))
            return units

        def run_units(units):
            if units:
                units[0][0]()
            for u in range(len(units)):
                if u + 1 < len(units):
                    units[u + 1][0]()
                units[u][1]()

        with ExitStack() as st:
            Wt = [sb(st, "mW%d" % i, [128, KC, 512], BF16) for i in range(2)]
            bt = [sb(st, "bt%d" % i, [1, 512], F32) for i in range(2)]
            mt = [sb(st, "mt%d" % i, [1, 512], F32) for i in range(2)]
            pm = [ps(st, "pm%d" % i, [128, 512]) for i in range(2)]
            P.dma("sp", ccol[:], c_in.rearrange("o (k p) -> p (o k)", p=128), w=["ccol"], sem="m_c", slow=True)
            P.op("act", lambda h: h.activation(out=sccol[:], in_=ccol[:], func=AF.Silu), r=["ccol"], w=["sccol"])
            run_units(mod_units(range(0, 5), Wt, bt, mt, pm, [("pm", 0), ("pm", 1)]))
            P.barrier()
            P.emit()

        AB = sb(top, "AB", [128, 6, KC], F32)
        gcol = sb(top, "gcol", [128, 4, KC], F32)

        def make_AB(i):
            P.op("dve", lambda h, i=i: h.scalar_tensor_tensor(
                out=AB[:, 2 * i, :], in0=cols[:, 3 * i + 1, :], scalar=1.0, in1=gcol[:, i, :],
                op0=ALU.add, op1=ALU.mult), r=[("gcol", i)], w=[("AB", 2 * i)])
            P.op("dve", lambda h, i=i: h.tensor_copy(out=AB[:, 2 * i + 1, :], in_=cols[:, 3 * i, :]),
                 w=[("AB", 2 * i + 1)])
        with ExitStack() as st:
            for i in range(4):
                P.dma("sp", gcol[:, i, :], norms[i:i + 1, :].rearrange("o (k p) -> p (o k)", p=128),
                      w=[("gcol", i)], sem="m_col", slow=True)
            for i in range(2):
                make_AB(i)
            P.barrier()
            P.emit()

        def norm_rows(P, src_fn, rkeys_fn, nrb, hT, Acol, Bcol, B):
            xt, ysb, ss, rs, tp = B["xt"], B["ysb"], B["ss"], B["rs"], B["tp"]
            XK = B.get("xk", ["scr0", "scr1"])
            cpb = min(8, KC)
            for rb in range(nrb):
                s = rb % len(ysb)
                P.dma("sp", xt, src_fn(rb), r=rkeys_fn(rb), w=XK, sem="xt")
                P.op("dve", lambda h, s=s: h.memset(ss[s][:], 0.0), w=[("ss", s)])
                P.op("act", lambda h, s=s: h.activation(out=ysb[s][:], in_=xt, func=AF.Square,
                                                        accum_out=ss[s][:]),
                     r=XK, w=[("y", s), ("ss", s)])
                P.op("dve", lambda h, s=s: h.tensor_scalar(out=rs[s][:], in0=ss[s][:], scalar1=1.0 / D,
                                                           scalar2=EPS, op0=ALU.mult, op1=ALU.add),
                     r=[("ss", s)], w=[("rs", s)])
                P.op("act", lambda h, s=s: h.sqrt(out=rs[s][:], in_=rs[s][:]), r=[("rs", s)], w=[("rs", s)])
                P.op("dve", lambda h, s=s: h.reciprocal(out=rs[s][:], in_=rs[s][:]), r=[("rs", s)], w=[("rs", s)])
                P.op("act", lambda h, s=s: h.activation(out=ysb[s][:], in_=xt, func=AF.Copy,
                                                        scale=rs[s][:, 0:1]),
                     r=XK + [("rs", s)], w=[("y", s)])
                for c0 in range(0, KC, cpb):
                    t = B["tpc"] % 2
                    B["tpc"] += 1

                    def tr(h, s=s, c0=c0, t=t):
                        for cc in range(cpb):
                            i = h.transpose(out=tp[t][:, cc * 128:(cc + 1) * 128],
                                            in_=ysb[s][:, (c0 + cc) * 128:(c0 + cc + 1) * 128], identity=ident)
                        return i
                    P.op("pe", tr, r=[("y", s)], w=[("tp", t)])

                    def ev(h, c0=c0, t=t, rb=rb):
                        for cc in range(cpb):
                            c = c0 + cc
                            i = h.tensor_scalar(out=hT[:, c, rb * 128:(rb + 1) * 128],
                                                in0=tp[t][:, cc * 128:(cc + 1) * 128],
                                                scalar1=Acol[:, c:c + 1], scalar2=Bcol[:, c:c + 1],
                                                op0=ALU.mult, op1=ALU.add)
                        return i
                    P.op("dve", ev, r=[("tp", t)], w=[("hT", rb)])

        def ffn_stage(P, tag, n_tok, first_src, W1, W2, Acol, Bcol, gvec):
            with ExitStack() as st:
                hT = sb(st, "hT", [128, KC, TB], BF16)
                aT = sb(st, "aT", [128, NG, TB], BF16)
                wi = [sb(st, "wi%d" % i, [128, KC, 128], BF16) for i in range(4)]
                wo = [sb(st, "wo%d" % i, [128, NG, 512], BF16) for i in range(2)]
                scr_full = sb(st, "scr", [128, max(D, 4096)], F32)
                scr = scr_full[:, 0:D]
                ysb = [sb(st, "ysb%d" % i, [128, D], BF16) for i in range(2)]
                gbt = sb(st, "gbt", [128, D], F32)
                sil = [sb(st, "sil%d" % i, [128, 512], F32) for i in range(2)]
                tmp = [sb(st, "tmp%d" % i, [128, 512], F32) for i in range(2)]
                ss = [sb(st, "ss%d" % i, [128, 1], F32) for i in range(2)]
                rs = [sb(st, "rs%d" % i, [128, 1], F32) for i in range(2)]
                pg = [ps(st, "pg%d" % i, [128, 512]) for i in range(2)]
                pu = [ps(st, "pu%d" % i, [128, 512]) for i in range(2)]
                po = [ps(st, "po%d" % i, [128, 512]) for i in range(2)]
                tp = [ps(st, "tp%d" % i, [128, 1024], BF16) for i in range(2)]
                B = dict(xt=scr, ysb=ysb, ss=ss, rs=rs, tp=tp, tpc=0, xk=["scr0", "scr1", "scr2", "scr3"])
                RBG = 2
                NSL = 4
                scrv = [scr_full[:, i * 1024:(i + 1) * 1024].rearrange("p (r n) -> p r n", r=RBG)
                        for i in range(NSL)]
                P.dma("sp", gbt[:], modv[gvec:gvec + 1, :].partition_broadcast(128), w=["gbt"], sem="gbt")
                P.op("act", lambda h: h.mul(out=gbt[:], in_=gbt[:], mul=0.5), r=["gbt"], w=["gbt"])
                cnt = 0
                ocnt = 0
                xcnt = 0
                pcnt = 0
                NTT = TB // 512
                for blk in range(n_tok // TB):
                    tok0 = blk * TB
                    src0 = first_src
                    norm_rows(P, lambda rb: src0[tok0 + rb * 128: tok0 + (rb + 1) * 128, :],
                              lambda rb: [("xs", (tok0 + rb * 128) // 256, n) for n in range(D // 512)],
                              TB // 128, hT, Acol, Bcol, B)
                    for g, (j0, ng) in enumerate(groups):
                        for jl in range(ng):
                            jj = j0 + jl
                            sg = (2 * cnt) % 4
                            su = (2 * cnt + 1) % 4
                            cnt += 1
                            P.dma("pool", wi[sg][:], W1[:, jj * 128:(jj + 1) * 128].rearrange(
                                "(k p) n -> p k n", p=128), w=[("wi", sg)], sem=("wi", sg))
                            P.dma("pool", wi[su][:], W1[:, DFF + jj * 128: DFF + (jj + 1) * 128].rearrange(
                                "(k p) n -> p k n", p=128), w=[("wi", su)], sem=("wi", su))
                            for tt in range(NTT):
                                b2 = tt % 2
                                hk = [("hT", rb) for rb in range(tt * 4, tt * 4 + 4)]

                                def mmg(h, sl=sg, tt=tt, dst=pg[b2]):
                                    for k in range(KC):
                                        i = h.matmul(dst[:], lhsT=wi[sl][:, k, :],
                                                     rhs=hT[:, k, tt * 512:(tt + 1) * 512],
                                                     start=(k == 0), stop=(k == KC - 1))
                                    return i

                                def mmu(h, sl=su, tt=tt, dst=pu[b2]):
                                    for k in range(KC):
                                        i = h.matmul(dst[:], lhsT=wi[sl][:, k, :],
                                                     rhs=hT[:, k, tt * 512:(tt + 1) * 512],
                                                     start=(k == 0), stop=(k == KC - 1))
                                    return i
                                P.op("pe", mmg, r=[("wi", sg)] + hk, w=[("pg", b2)])
                                P.op("pe", mmu, r=[("wi", su)] + hk, w=[("pu", b2)])
                                P.op("act", lambda h, b2=b2: h.activation(out=sil[b2][:], in_=pg[b2][:],
                                                                         func=AF.Silu),
                                     r=[("pg", b2)], w=[("sil", b2)])
                                P.op("dve", lambda h, b2=b2, jl=jl, tt=tt: h.tensor_tensor(
                                    out=aT[:, jl, tt * 512:(tt + 1) * 512], in0=sil[b2][:], in1=pu[b2][:],
                                    op=ALU.mult), r=[("sil", b2), ("pu", b2)], w=[("aT", jl, tt)])
                        base = first_src if g == 0 else xs
                        tasks = [(n, rg) for n in range(D // 512) for rg in range(TB // 128 // RBG)]

                        def xinfo(ti, base=base, tok0=tok0, xb=xcnt, tasks=tasks):
                            n, rg = tasks[ti]
                            r0 = tok0 + rg * RBG * 128
                            return n, rg, r0, (xb + ti) % NSL, ("xs", r0 // 256, n)

                        def xload(ti, base=base):
                            n, rg, r0, xsl, dk = xinfo(ti)
                            P.dma("sp", scrv[xsl], base[r0:r0 + RBG * 128, n * 512:(n + 1) * 512]
                                  .rearrange("(r p) n -> p r n", p=128), r=[dk], w=["scr%d" % xsl],
                                  sem=("scrl", xsl))
                        PRE = 2
                        for ti in range(min(PRE, len(tasks))):
                            xload(ti)
                        cur_n = -1
                        for ti in range(len(tasks)):
                            n, rg, r0, xsl, dk = xinfo(ti)
                            if n != cur_n:
                                cur_n = n
                                s = ocnt % 2
                                ocnt += 1
                                P.dma("pool", wo[s][:, 0:ng, :],
                                      W2[j0 * 128:(j0 + ng) * 128, n * 512:(n + 1) * 512]
                                      .rearrange("(j p) n -> p j n", p=128), w=[("wo", s)], sem=("wo", s))
                            if ti + PRE < len(tasks):
                                xload(ti + PRE)
                            for r4 in range(RBG):
                                rb = rg * RBG + r4
                                pb = pcnt % 2
                                pcnt += 1

                                def mmo(h, rb=rb, s=s, dst=po[pb], ng=ng):
                                    for jl in range(ng):
                                        i = h.matmul(dst[:], lhsT=aT[:, jl, rb * 128:(rb + 1) * 128],
                                                     rhs=wo[s][:, jl, :], start=(jl == 0), stop=(jl == ng - 1))
                                    return i
                                P.op("pe", mmo, r=[("wo", s)] + [("aT", jl, rb // 4) for jl in range(ng)],
                                     w=[("po", pb)])
                                P.op("dve", lambda h, pb=pb, n=n: h.tensor_tensor(
                                    out=tmp[pb][:], in0=po[pb][:], in1=gbt[:, n * 512:(n + 1) * 512],
                                    op=ALU.mult), r=[("po", pb), "gbt"], w=[("tmp", pb)])
                                P.op("dve", lambda h, pb=pb, xsl=xsl, r4=r4: h.tensor_tensor(
                                    out=scrv[xsl][:, r4, :], in0=tmp[pb][:], in1=scrv[xsl][:, r4, :],
                                    op=ALU.add), r=[("tmp", pb), "scr%d" % xsl], w=["scr%d" % xsl])
                            P.dma("sp", xs[r0:r0 + RBG * 128, n * 512:(n + 1) * 512]
                                  .rearrange("(r p) n -> p r n", p=128), scrv[xsl],
                                  r=["scr%d" % xsl], w=[dk], sem=("scrs", xsl))
                        xcnt += len(tasks)
                P.barrier()
                P.emit()

        UPTO = cfg.get("upto", 99)
        if UPTO < 1:
            return nc
        ffn_stage(P, "f1", S, x_in, f1_in, f1_out, AB[:, 0, :], AB[:, 1, :], 2)
        if UPTO < 2:
            return nc

        o_fq, o_fk, o_fv, o_ff = 0, FW, 2 * FW, 3 * FW
        o_dq = 3 * FW + HF
        o_dk, o_dv = o_dq + DW, o_dq + 2 * DW
        o_ga = o_dq + 3 * DW
        o_gb = o_ga + D
        with ExitStack() as st:
            hT = sb(st, "hT", [128, KC, TB], BF16)
            wi = [sb(st, "wi%d" % i, [128, KC, 128], BF16) for i in range(4)]
            wv = [sb(st, "wv%d" % i, [128, KC, 256], BF16) for i in range(2)]
            wf = sb(st, "wf", [128, KC, HF], BF16)
            scr = sb(st, "scr", [128, D], F32)[:, :]
            ysb = [sb(st, "ysb0", [128, D], BF16)]
            ss = [sb(st, "ss%d" % i, [128, 1], F32) for i in range(2)]
            rs = [sb(st, "rs%d" % i, [128, 1], F32) for i in range(2)]
            posi = sb(st, "posi", [128, TB], I32)
            ang = sb(st, "ang", [128, TB], F32)
            ctab = sb(st, "ctab", [128, TB], F32)
            stab = sb(st, "stab", [128, TB], F32)
            rq = sb(st, "rq", [128, TB], F32)
            rr = sb(st, "rr", [128, TB], F32)
            obf = [sb(st, "obf%d" % i, [128, 512], BF16) for i in range(3)]
            qb = [sb(st, "qb%d" % i, [128, 512], BF16) for i in range(2)]
            t1 = [sb(st, "t1%d" % i, [128, 512], F32) for i in range(2)]
            t2 = [sb(st, "t2%d" % i, [128, 512], F32) for i in range(2)]
            bgc = sb(st, "bgc", [128, 2 * KC], F32)
            nbf = sb(st, "nbf", [HF, 1], F32)
            ef = [sb(st, "ef%d" % i, [HF, 512], F32) for i in range(2)]
            vob = [sb(st, "vob%d" % i, [128, 256], BF16) for i in range(2)]
            pA = [ps(st, "pA%d" % i, [128, 512]) for i in range(2)]
            pB = [ps(st, "pB%d" % i, [128, 512]) for i in range(2)]
            pV = [ps(st, "pV%d" % i, [128, 512]) for i in range(2)]
            tp = [ps(st, "tp%d" % i, [128, 1024], BF16) for i in range(2)]
            B = dict(xt=scr, ysb=ysb, ss=ss, rs=rs, tp=tp, tpc=0)
            P.dma("sp", bgc[:], b_gate.rearrange("o (k p) -> p (o k)", p=128), w=["bgc"], sem="m_col", slow=True)
            P.dma("sp", nbf[:], b_forget[:, :], w=["nbf"], sem="m_col")
            P.op("dve", lambda h: h.tensor_scalar(out=nbf[:], in0=nbf[:], scalar1=-1.0, scalar2=None,
                                                  op0=ALU.mult), r=["nbf"], w=["nbf"])
            P.dma("pool", wf[:], w_in[:, o_ff:o_ff + HF].rearrange("(k p) n -> p k n", p=128), w=["wf"],
                  sem="wf", slow=True)
            cnt = 0
            acnt = 0
            ocnt = 0
            vcnt = 0
            NTT = TB // 512
            for blk in range(S // TB):
                tok0 = blk * TB
                own = tok0 < OWN
                norm_rows(P, lambda rb: xs[tok0 + rb * 128: tok0 + (rb + 1) * 128, :],
                          lambda rb: [], TB // 128, hT, AB[:, 2, :], AB[:, 3, :], B)
                P.dma("sp", posi[:], pos_in[0:1, tok0:tok0 + TB].partition_broadcast(128), w=["posi"], sem="posi")
                P.op("dve", lambda h: h.tensor_copy(out=ang[:], in_=posi[:]), r=["posi"], w=["ang"])
                P.op("dve", lambda h: h.tensor_scalar(out=ang[:], in0=ang[:], scalar1=invf, scalar2=None,
                                                      op0=ALU.mult), r=["ang"], w=["ang"])
                for tab, off, tk in ((stab, 0.0, "stab"), (ctab, 0.5 * math.pi, "ctab")):
                    P.op("dve", lambda h, off=off: h.tensor_scalar(out=rq[:], in0=ang[:], scalar1=off,
                                                                   scalar2=1.0 / (2 * math.pi), op0=ALU.add,
                                                                   op1=ALU.mult), r=["ang"], w=["rq"])
                    P.op("dve", lambda h: h.tensor_copy(out=posi[:], in_=rq[:]), r=["rq"], w=["posi"])
                    P.op("dve", lambda h: h.tensor_copy(out=rq[:], in_=posi[:]), r=["posi"], w=["rq"])
                    P.op("dve", lambda h, off=off: h.tensor_scalar(out=rr[:], in0=ang[:], scalar1=off, scalar2=None,
                                                                   op0=ALU.add), r=["ang"], w=["rr"])
                    P.op("dve", lambda h: h.scalar_tensor_tensor(out=rr[:], in0=rq[:], scalar=-2 * math.pi,
                                                                 in1=rr[:], op0=ALU.mult, op1=ALU.add),
                         r=["rq", "rr"], w=["rr"])
                    P.op("dve", lambda h: h.tensor_scalar(out=rq[:], in0=rr[:], scalar1=-math.pi, scalar2=1.0e6,
                                                          op0=ALU.add, op1=ALU.mult), r=["rr"], w=["rq"])
                    P.op("dve", lambda h: h.tensor_scalar(out=rq[:], in0=rq[:], scalar1=0.0, scalar2=1.0,
                                                          op0=ALU.max, op1=ALU.min), r=["rq"], w=["rq"])
                    P.op("dve", lambda h: h.scalar_tensor_tensor(out=rr[:], in0=rq[:], scalar=-2 * math.pi,
                                                                 in1=rr[:], op0=ALU.mult, op1=ALU.add),
                         r=["rq", "rr"], w=["rr"])
                    P.op("act", lambda h, tab=tab: h.activation(out=tab[:], in_=rr[:], func=AF.Sin),
                         r=["rr"], w=[tk])
                P.op("dve", lambda h: h.tensor_scalar(out=stab[:], in0=stab[:], scalar1=sgn, scalar2=None,
                                                      op0=ALU.mult), r=["stab"], w=["stab"])
                chunks = []
                for i in range(HF):
                    chunks.append(("k", o_fk + i * 128, kfd, i * 128))
                for i in range(2 * HD):
                    chunks.append(("kr", o_dk + i * 128, kdd, i * 128))
                if own:
                    for i in range(HF):
                        chunks.append(("q", o_fq + i * 128, qfd, i * 128))
                    for i in range(2 * HD):
                        chunks.append(("qr", o_dq + i * 128, qdd, i * 128))
                    for i in range(KC):
                        chunks.append(("g", o_ga + i * 128, gAd, i * 128, i))
                    for i in range(KC):
                        chunks.append(("g", o_gb + i * 128, gBd, i * 128, KC + i))
                hk_all = [("hT", rb) for rb in range(TB // 128)]
                SUB = cfg.get("sub", 9)
                if SUB < 1:
                    chunks = []
                if "kinds" in cfg:
                    chunks = [c_ for c_ in chunks if c_[0] in cfg["kinds"]]
                for ch in chunks:
                    kind, co, dst, ro = ch[0], ch[1], ch[2], ch[3]
                    sl = cnt % 4
                    cnt += 1
                    P.dma("pool", wi[sl][:], w_in[:, co:co + 128].rearrange("(k p) n -> p k n", p=128),
                          w=[("wi", sl)], sem=("wi", sl))
                    for tt in range(NTT):
                        a2 = acnt % 2
                        acnt += 1
                        hk = hk_all[tt * 4: tt * 4 + 4]

                        def mm(h, sl=sl, tt=tt, dstp=pA[a2]):
                            for k in range(KC):
                                i = h.matmul(dstp[:], lhsT=wi[sl][:, k, :], rhs=hT[:, k, tt * 512:(tt + 1) * 512],
                                             start=(k == 0), stop=(k == KC - 1))
                            return i
                        P.op("pe", mm, r=[("wi", sl)] + hk, w=[("pA", a2)])
                        ob = ocnt % 3
                        ocnt += 1
                        tsl = slice(tt * 512, (tt + 1) * 512)
                        dcol0 = tok0 + tt * 512
                        if kind == "k":
                            P.op("act", lambda h, a2=a2, ob=ob: h.copy(out=obf[ob][:], in_=pA[a2][:]),
                                 r=[("pA", a2)], w=[("obf", ob)])
                        elif kind == "q":
                            P.op("act", lambda h, a2=a2, ob=ob: h.mul(out=obf[ob][:], in_=pA[a2][:], mul=QSC),
                                 r=[("pA", a2)], w=[("obf", ob)])
                        elif kind == "g":
                            gi = ch[4]
                            P.op("act", lambda h, a2=a2, ob=ob, gi=gi: h.activation(
                                out=obf[ob][:], in_=pA[a2][:], func=AF.Sigmoid, bias=bgc[:, gi:gi + 1]),
                                r=[("pA", a2), "bgc"], w=[("obf", ob)])
                        else:
                            sc_ = QSC if kind == "qr" else 1.0
                            P.op("act", lambda h, a2=a2: h.copy(out=qb[a2][:], in_=pA[a2][:]),
                                 r=[("pA", a2)], w=[("qb", a2)])
                            if cfg.get("dbg", 0) != 1:
                                P.op("pe", lambda h, a2=a2: h.matmul(pB[a2][:], lhsT=perm, rhs=qb[a2][:],
                                                                     start=True, stop=True),
                                     r=[("qb", a2)], w=[("pB", a2)])
                            P.op("dve", lambda h, a2=a2, tsl=tsl: h.tensor_tensor(
                                out=t1[a2][:], in0=pA[a2][:], in1=ctab[:, tsl], op=ALU.mult),
                                r=[("pA", a2), "ctab", ("qb", a2)], w=[("t1", a2)])
                            P.op("dve", lambda h, a2=a2, tsl=tsl: h.tensor_tensor(
                                out=t2[a2][:], in0=pB[a2][:], in1=stab[:, tsl], op=ALU.mult),
                                r=[("pB", a2), "stab"], w=[("t2", a2)])
                            P.op("dve", lambda h, a2=a2: h.tensor_tensor(
                                out=t1[a2][:], in0=t1[a2][:], in1=t2[a2][:], op=ALU.add),
                                r=[("t1", a2), ("t2", a2)], w=[("t1", a2)])
                            P.op("act", lambda h, a2=a2, ob=ob, sc_=sc_: h.activation(
                                out=obf[ob][:], in_=t1[a2][:], func=AF.Copy, scale=sc_),
                                r=[("t1", a2)], w=[("obf", ob)])
                        P.dma("sp", dst[ro:ro + 128, dcol0:dcol0 + 512], obf[ob][:], r=[("obf", ob)],
                              sem=("obs", ob))
                for tt in range(NTT if SUB >= 2 else 0):
                    a2 = acnt % 2
                    acnt += 1

                    def mmf(h, tt=tt, dstp=pA[a2]):
                        for k in range(KC):
                            i = h.matmul(dstp[0:HF, :], lhsT=wf[:, k, :], rhs=hT[:, k, tt * 512:(tt + 1) * 512],
                                         start=(k == 0), stop=(k == KC - 1))
                        return i
                    P.op("pe", mmf, r=["wf"] + hk_all[tt * 4: tt * 4 + 4], w=[("pA", a2)])
                    P.op("act", lambda h, a2=a2: h.activation(out=ef[a2][:], in_=pA[a2][0:HF, :], func=AF.Exp,
                                                              bias=nbf[:, 0:1], scale=-1.0),
                         r=[("pA", a2), "nbf"], w=[("ef", a2)])
                    P.op("act", lambda h, a2=a2: h.activation(out=ef[a2][:], in_=ef[a2][:], func=AF.Ln,
                                                              bias=1.0), r=[("ef", a2)], w=[("ef", a2)])
                    P.dma("sp", nld[:, tok0 + tt * 512: tok0 + (tt + 1) * 512], ef[a2][:], r=[("ef", a2)],
                          sem=("efs", a2))
                for (co, dst, width) in (((o_fv, vfd, FW), (o_dv, vdd, DW)) if SUB >= 3 else ()):
                    for n in range(width // 256):
                        s = vcnt % 2
                        vcnt += 1
                        P.dma("pool", wv[s][:], w_in[:, co + n * 256: co + (n + 1) * 256].rearrange(
                            "(k p) n -> p k n", p=128), w=[("wv", s)], sem=("wv", s))
                        for rb in range(TB // 128):
                            v2 = acnt % 2
                            acnt += 1

                            def mmv(h, s=s, rb=rb, dstp=pV[v2]):
                                for k in range(KC):
                                    i = h.matmul(dstp[:, 0:256], lhsT=hT[:, k, rb * 128:(rb + 1) * 128],
                                                 rhs=wv[s][:, k, :], start=(k == 0), stop=(k == KC - 1))
                                return i
                            P.op("pe", mmv, r=[("wv", s), ("hT", rb)], w=[("pV", v2)])
                            P.op("act", lambda h, v2=v2: h.copy(out=vob[v2][:], in_=pV[v2][:, 0:256]),
                                 r=[("pV", v2)], w=[("vob", v2)])
                            P.dma("sp", dst[tok0 + rb * 128: tok0 + (rb + 1) * 128, n * 256:(n + 1) * 256],
                                  vob[v2][:], r=[("vob", v2)], sem=("vos", v2))
            P.barrier()
            P.emit()

        if UPTO < 3:
            return nc
        nbias = sb(top, "nbias", [128, NBLK * HF], F32)
        with ExitStack() as st:
            fa = sb(st, "fa", [HF, 2, OWN], F32)
            fb = sb(st, "fb", [HF, 2, OWN], F32)
            gtot = sb(st, "gtot", [HF, 1], F32)
            pc = ps(st, "pc", [128, 512])
            P.dma("sp", fa[:], nld.rearrange("h (a t) -> h a t", a=2), w=["fa"], sem="fa")
            cur, nxt, kc_, kn_ = fa, fb, "fa", "fb"
            d = 1
            while d < OWN:
                P.op("dve", lambda h, cur=cur, nxt=nxt, d=d: h.tensor_tensor(
                    out=nxt[:, :, d:], in0=cur[:, :, d:], in1=cur[:, :, 0:OWN - d], op=ALU.add),
                    r=[kc_], w=[kn_])
                P.op("dve", lambda h, cur=cur, nxt=nxt, d=d: h.tensor_copy(out=nxt[:, :, 0:d], in_=cur[:, :, 0:d]),
                     r=[kc_], w=[kn_])
                cur, nxt, kc_, kn_ = nxt, cur, kn_, kc_
                d *= 2
            P.op("dve", lambda h, cur=cur, nxt=nxt: h.tensor_scalar(out=nxt[:, 0, :], in0=cur[:, 0, :], scalar1=-1.0,
                                                                    scalar2=None, op0=ALU.mult), r=[kc_], w=[kn_])
            P.dma("sp", frd[:, :], nxt[:, 0, :], r=[kn_], w=["frd"], sem="frs")
            P.op("dve", lambda h, cur=cur: h.tensor_copy(out=gtot[:], in_=cur[:, 1, OWN - 1:OWN]), r=[kc_],
                 w=["gtot"])
            P.op("dve", lambda h, cur=cur: h.tensor_scalar(out=cur[:, 1, :], in0=cur[:, 1, :], scalar1=gtot[:, 0:1],
                                                           scalar2=flagc[0:HF, 0:1], op0=ALU.subtract,
                                                           op1=ALU.add), r=[kc_, "gtot"], w=[kc_])
            curf = cur.rearrange("h a t -> h (a t)")

            def trf(h, curf=curf):
                for i in range(NBLK):
                    ii = h.matmul(pc[:, i * HF:(i + 1) * HF], lhsT=curf[:, i * 128:(i + 1) * 128],
                                  rhs=identf[0:HF, 0:HF], start=True, stop=True)
                return ii
            P.op("pe", trf, r=[kc_], w=["pc"])
            P.op("dve", lambda h: h.tensor_copy(out=nbias[:], in_=pc[:, 0:NBLK * HF]), r=["pc"], w=["nbias"])
            if cfg.get("dbgout"):
                nbd = dscr("nbd", [128, NBLK * HF])
                P.dma("sp", nbd[:, :], nbias[:], r=["nbias"], sem="nbd")
            P.barrier()
            P.emit()

        if UPTO < 4:
            return nc
        with ExitStack() as st:
            kT = [sb(st, "kT%d" % i, [128, S], BF16) for i in range(2)]
            vS = [sb(st, "vS%d" % i, [128, NBLK, 256], BF16) for i in range(2)]
            qT = [sb(st, "qT%d" % i, [128, 512], BF16) for i in range(2)]
            Fb = [sb(st, "Fb%d" % i, [128, 512], F32) for i in range(2)]
            ein = [sb(st, "ein%d" % i, [128, 512], F32) for i in range(3)]
            Pt = [sb(st, "Pt%d" % i, [128, 512], BF16) for i in range(4)]
            rinv = sb(st, "rinv", [128, 512], F32)
            On = sb(st, "On", [128, 2, 512], F32)
            O1 = sb(st, "O1", [128, 2, OWN], F32)
            pre = sb(st, "pre", [128, 2, 512], F32)
            sq = sb(st, "sq", [128, 2, 512], BF16)
            rst = sb(st, "rst", [128, 512], F32)
            yo = [sb(st, "yo%d" % i, [128, 512], BF16) for i in range(2)]
            lamt = sb(st, "lamt", [128, 512], F32)
            lpr = sb(st, "lpr", [128, 256], F32)
            lsc = sb(st, "lsc", [128, 4], F32)
            subc = sb(st, "subc", [128, 2], F32)
            pS = [ps(st, "pS%d" % i, [128, 512]) for i in range(3)]
            pO = [ps(st, "pO%d" % i, [128, 512]) for i in range(2)]
            pR = [ps(st, "pR%d" % i, [128, 512]) for i in range(2)]
            pN = ps(st, "pN", [128, 512])
            lnr = sb(st, "lnr", [128, 512], F32)
            Wt4 = [sb(st, "mW4%d" % i, [128, KC, 512], BF16) for i in range(2)]
            bt4 = [sb(st, "bt4%d" % i, [1, 512], F32) for i in range(2)]
            mt4 = [sb(st, "mt4%d" % i, [1, 512], F32) for i in range(2)]
            dunits = mod_units(range(5, 11), Wt4, bt4, mt4, [pN, pN], ["pN", "pN"])
            P.dma("sp", lamt[:], dlam[0:1, :].partition_broadcast(128), w=["lamt"], sem="lamt")
            P.op("dve", lambda h: h.tensor_tensor(out=lpr[:, 0:128], in0=lamt[:, 0:128], in1=lamt[:, 128:256],
                                                  op=ALU.mult), r=["lamt"], w=["lpr0"])
            P.op("dve", lambda h: h.tensor_tensor(out=lpr[:, 128:256], in0=lamt[:, 256:384], in1=lamt[:, 384:512],
                                                  op=ALU.mult), r=["lamt"], w=["lpr1"])
            P.op("dve", lambda h: h.reduce_sum(out=lsc[:, 0:1], in_=lpr[:, 0:128], axis=AX.X), r=["lpr0"],
                 w=["lsc0"])
            P.op("dve", lambda h: h.reduce_sum(out=lsc[:, 1:2], in_=lpr[:, 128:256], axis=AX.X), r=["lpr1"],
                 w=["lsc1"])
            P.op("act", lambda h: h.activation(out=lsc[:, 2:4], in_=lsc[:, 0:2], func=AF.Exp),
                 r=["lsc0", "lsc1"], w=["lsc2"])
            P.op("dve", lambda h: h.tensor_tensor(out=lsc[:, 0:1], in0=lsc[:, 3:4], in1=lsc[:, 2:3],
                                                  op=ALU.subtract), r=["lsc2"], w=["lsc0"])
            P.op("dve", lambda h: h.tensor_scalar(out=lsc[:, 0:1], in0=lsc[:, 0:1], scalar1=-LINIT, scalar2=None,
                                                  op0=ALU.add), r=["lsc0"], w=["nlam"])
            P.dma("sp", subc[:], subln.rearrange("o (c p) -> p (o c)", p=128), w=["subc"], sem="m_col", slow=True)
            P.op("dve", lambda h: h.tensor_scalar(out=subc[:], in0=subc[:], scalar1=(1.0 - LINIT), scalar2=None,
                                                  op0=ALU.mult), r=["subc"], w=["subc"])
            state = dict(hc=0, qc=0, sc=0, pc=0, yc=0, fc=0)

            passes = []
            NQ = OWN // 512
            LA = 2

            def attn_pass(krows, qrows, vsrc, vcol0, nvc, fhead, mask, out_cb):
                passes.append((krows, qrows, vsrc, vcol0, nvc, fhead, mask, out_cb))

            def load_kv(pi):
                krows, qrows, vsrc, vcol0, nvc, fhead, mask, out_cb = passes[pi]
                hs = pi % 2
                P.dma("sp", kT[hs][:], krows, w=[("kT", hs)], sem=("kT", hs))
                P.dma("sp", vS[hs][:, :, 0:nvc * 128],
                      vsrc[:, vcol0:vcol0 + nvc * 128].rearrange("(b p) n -> p b n", p=128),
                      w=[("vS", hs)], sem=("vS", hs))

            def load_q(ti):
                pi, qt = divmod(ti, NQ)
                krows, qrows, vsrc, vcol0, nvc, fhead, mask, out_cb = passes[pi]
                qs = ti % 2
                P.dma("sp", qT[qs][:], qrows[:, qt * 512:(qt + 1) * 512], w=[("qT", qs)], sem=("qT", qs))
                if fhead is not None:
                    P.dma("sp", Fb[qs][:], frd[fhead:fhead + 1, qt * 512:(qt + 1) * 512].partition_broadcast(128),
                          w=[("Fb", qs)], sem=("Fb", qs))

            def run_passes():
                total = len(passes) * NQ
                load_kv(0)
                load_q(0)
                ui = 0
                if dunits:
                    dunits[0][0]()
                for ti in range(total):
                    pi, qt = divmod(ti, NQ)
                    if qt == 0 and pi + 1 < len(passes):
                        load_kv(pi + 1)
                    if ti + 1 < total:
                        load_q(ti + 1)
                    if ui < len(dunits):
                        if ui + 1 < len(dunits):
                            dunits[ui + 1][0]()
                        dunits[ui][1]()
                        ui += 1
                    attn_tile(pi, qt, pi % 2, ti % 2)
                while ui < len(dunits):
                    if ui + 1 < len(dunits):
                        dunits[ui + 1][0]()
                    dunits[ui][1]()
                    ui += 1

            def attn_tile(pi, qt, hs, qs):
                krows, qrows, vsrc, vcol0, nvc, fhead, mask, out_cb = passes[pi]
                fb = state["fc"] % 2
                state["fc"] += 1
                pOb = [pO[fb]] if nvc == 1 else [pO[0], pO[1]]
                pOk = [("pO", fb)] if nvc == 1 else [("pO", 0), ("pO", 1)]
                pRb = pR[fb]
                pRk = ("pR", fb)
                if True:
                    tiles = [(OB + i, 0, False, True) for i in range(OB)]
                    for i in range(4 * qt + 4):
                        jd = i - 4 * qt
                        tiles.append((i, 128 * jd if jd > 0 else 0, jd >= 0, False))
                    nt_ = len(tiles)
                    sbank = {}

                    def qk(idx):
                        kt, c0, diag, oth = tiles[idx]
                        sbk = state["sc"] % 3
                        state["sc"] += 1
                        sbank[idx] = sbk
                        P.op("pe", lambda h, kt=kt, c0=c0, sbk=sbk, qs=qs, hs=hs: h.matmul(
                            pS[sbk][:, c0:512], lhsT=kT[hs][:, kt * 128:(kt + 1) * 128], rhs=qT[qs][:, c0:512],
                            start=True, stop=True), r=[("kT", hs), ("qT", qs)], w=[("pS", sbk)])
                    for idx in range(min(LA, nt_)):
                        qk(idx)
                    for idx in range(nt_):
                        if idx + LA < nt_:
                            qk(idx + LA)
                        kt, c0, diag, oth = tiles[idx]
                        sbk = sbank[idx]
                        pi = state["pc"] % 4
                        state["pc"] += 1
                        if fhead is not None:
                            P.op("dve", lambda h, sbk=sbk, c0=c0, qs=qs: h.tensor_tensor(
                                out=ein[sbk][:, c0:512], in0=pS[sbk][:, c0:512], in1=Fb[qs][:, c0:512], op=ALU.add),
                                r=[("pS", sbk), ("Fb", qs)], w=[("ein", sbk)])
                            if diag:
                                P.op("dve", lambda h, sbk=sbk, c0=c0: h.tensor_tensor(
                                    out=ein[sbk][:, c0:c0 + 128], in0=ein[sbk][:, c0:c0 + 128], in1=negtri,
                                    op=ALU.add), r=[("ein", sbk)], w=[("ein", sbk)])
                            bcol = nbias[:, kt * HF + fhead: kt * HF + fhead + 1]
                            P.op("act", lambda h, sbk=sbk, c0=c0, pi=pi, bcol=bcol: h.activation(
                                out=Pt[pi][:, c0:512], in_=ein[sbk][:, c0:512], func=AF.Exp, bias=bcol),
                                r=[("ein", sbk)], w=[("Pt", pi)])
                        else:
                            if oth:
                                P.op("act", lambda h, sbk=sbk, pi=pi: h.activation(
                                    out=Pt[pi][:, :], in_=pS[sbk][:, :], func=AF.Exp, bias=flagc[:, 0:1]),
                                    r=[("pS", sbk)], w=[("Pt", pi)])
                            else:
                                P.op("act", lambda h, sbk=sbk, pi=pi, c0=c0: h.activation(
                                    out=Pt[pi][:, c0:512], in_=pS[sbk][:, c0:512], func=AF.Exp),
                                    r=[("pS", sbk)], w=[("Pt", pi)])
                        if diag and fhead is None:
                            P.op("dve", lambda h, pi=pi, c0=c0: h.tensor_tensor(
                                out=Pt[pi][:, c0:c0 + 128], in0=Pt[pi][:, c0:c0 + 128], in1=mask, op=ALU.mult),
                                r=[("Pt", pi)], w=[("Pt", pi)])
                        first, last = (idx == 0), (idx == nt_ - 1)

                        def pv(h, kt=kt, c0=c0, pi=pi, first=first, last=last, hs=hs, nvc=nvc, pOb=pOb, pRb=pRb):
                            for vc in range(nvc):
                                h.matmul(pOb[vc][:, c0:512], lhsT=vS[hs][:, kt, vc * 128:(vc + 1) * 128],
                                         rhs=Pt[pi][:, c0:512], start=first, stop=last)
                            return h.matmul(pRb[:, c0:512], lhsT=ones, rhs=Pt[pi][:, c0:512], start=first, stop=last)
                        P.op("pe", pv, r=[("Pt", pi), ("vS", hs)], w=pOk + [pRk])
                    P.op("act", lambda h, pRb=pRb: h.activation(out=lnr[:], in_=pRb[:], func=AF.Ln), r=[pRk], w=["lnr"])
                    P.op("act", lambda h: h.activation(out=rinv[:], in_=lnr[:], func=AF.Exp, scale=-1.0),
                         r=["lnr"], w=["rinv"])
                    for vc in range(nvc):
                        P.op("dve", lambda h, vc=vc, pOb=pOb: h.tensor_tensor(out=On[:, vc, :], in0=pOb[vc][:],
                                                                             in1=rinv[:], op=ALU.mult),
                             r=[pOk[vc], "rinv"], w=[("On", vc)])
                    out_cb(qt)

            for hh in range(HF):
                def cb_f(qt, hh=hh):
                    y = state["yc"] % 2
                    state["yc"] += 1
                    P.op("act", lambda h, y=y: h.copy(out=yo[y][:], in_=On[:, 0, :]), r=[("On", 0)], w=[("yo", y)])
                    P.dma("sp", yad[hh * 128:(hh + 1) * 128, qt * 512:(qt + 1) * 512], yo[y][:], r=[("yo", y)],
                          sem=("yos", y))
                attn_pass(kfd[hh * 128:(hh + 1) * 128, :], qfd[hh * 128:(hh + 1) * 128, :], vfd, hh * 128, 1, hh,
                          trim, cb_f)
            for hh in range(HD):
                def cb_1(qt):
                    for vc in range(2):
                        P.op("act", lambda h, vc=vc, qt=qt: h.copy(out=O1[:, vc, qt * 512:(qt + 1) * 512],
                                                                    in_=On[:, vc, :]),
                             r=[("On", vc)], w=[("O1", qt, vc)])

                def cb_2(qt, hh=hh):
                    for vc in range(2):
                        P.op("dve", lambda h, vc=vc, qt=qt: h.scalar_tensor_tensor(
                            out=pre[:, vc, :], in0=On[:, vc, :], scalar=lsc[:, 0:1],
                            in1=O1[:, vc, qt * 512:(qt + 1) * 512], op0=ALU.mult, op1=ALU.add),
                            r=[("On", vc), ("O1", qt, vc), "nlam"], w=[("pre", vc)])
                        P.op("act", lambda h, vc=vc: h.activation(out=sq[:, vc, :], in_=pre[:, vc, :],
                                                                  func=AF.Square),
                             r=[("pre", vc)], w=[("sq", vc)])

                    def mmn(h):
                        h.matmul(pN[:], lhsT=ones, rhs=sq[:, 0, :], start=True, stop=False)
                        return h.matmul(pN[:], lhsT=ones, rhs=sq[:, 1, :], start=False, stop=True)
                    P.op("pe", mmn, r=[("sq", 0), ("sq", 1)], w=["pN"])
                    P.op("dve", lambda h: h.tensor_scalar(out=rst[:], in0=pN[:], scalar1=1.0 / 256, scalar2=EPS,
                                                          op0=ALU.mult, op1=ALU.add), r=["pN"], w=["rst"])
                    P.op("act", lambda h: h.sqrt(out=rst[:], in_=rst[:]), r=["rst"], w=["rst"])
                    P.op("dve", lambda h: h.reciprocal(out=rst[:], in_=rst[:]), r=["rst"], w=["rst"])
                    for vc in range(2):
                        y = state["yc"] % 2
                        state["yc"] += 1
                        P.op("dve", lambda h, vc=vc: h.tensor_tensor(out=pre[:, vc, :], in0=pre[:, vc, :],
                                                                     in1=rst[:], op=ALU.mult),
                             r=[("pre", vc), "rst"], w=[("pre", vc)])
                        P.op("act", lambda h, vc=vc, y=y: h.activation(out=yo[y][:], in_=pre[:, vc, :], func=AF.Copy,
                                                                       scale=subc[:, vc:vc + 1]),
                             r=[("pre", vc), "subc"], w=[("yo", y)])
                        r0 = hh * 256 + vc * 128
                        P.dma("sp", ybd[r0:r0 + 128, qt * 512:(qt + 1) * 512], yo[y][:], r=[("yo", y)],
                              sem=("yos", y))
                for mp, cbk in ((0, cb_1), (1, cb_2)):
                    r0 = (2 * hh + mp) * 128
                    attn_pass(kdd[r0:r0 + 128, :], qdd[r0:r0 + 128, :], vdd, hh * 256, 2, None, chkm, cbk)
            run_passes()
            P.barrier()
            P.emit()
        with ExitStack() as st:
            make_AB(2)
            P.barrier()
            P.emit()

        if UPTO < 5:
            return nc
        TB5 = 512
        with ExitStack() as st:
            mT = sb(st, "mT", [128, KC, TB5], BF16)
            yaS = sb(st, "yaS", [128, FW // 128, TB5], BF16)
            ybS = sb(st, "ybS", [128, DW // 128, TB5], BF16)
            wa = [sb(st, "wa%d" % i, [128, FW // 128, 128], BF16) for i in range(2)]
            wd = [sb(st, "wd%d" % i, [128, DW // 128, 128], BF16) for i in range(2)]
            wo = [sb(st, "wo%d" % i, [128, KC, 256], BF16) for i in range(2)]
            gA = [sb(st, "gA%d" % i, [128, TB5], BF16) for i in range(2)]
            gB = [sb(st, "gB%d" % i, [128, TB5], BF16) for i in range(2)]
            u1 = [sb(st, "u1%d" % i, [128, TB5], F32) for i in range(2)]
            u2 = [sb(st, "u2%d" % i, [128, TB5], F32) for i in range(2)]
            gbt = sb(st, "gbt", [128, D], F32)
            xo = [sb(st, "xo%d" % i, [128, 4, 256], F32) for i in range(4)]
            tmp = [sb(st, "tmp%d" % i, [128, 256], F32) for i in range(2)]
            pa = [ps(st, "pa%d" % i, [128, 512]) for i in range(2)]
            pb_ = [ps(st, "pb%d" % i, [128, 512]) for i in range(2)]
            po = [ps(st, "po%d" % i, [128, 512]) for i in range(2)]
            P.dma("sp", gbt[:], modv[5:6, :].partition_broadcast(128), w=["gbt"], sem="gbt")
            mc = 0
            oc = 0
            xc = 0
            pcn = 0
            for blk in range(OWN // TB5):
                tok0 = blk * TB5
                P.dma("sp", yaS[:], yad[:, tok0:tok0 + TB5].rearrange("(c p) t -> p c t", p=128), w=["yaS"],
                      sem="yaS")
                P.dma("sp", ybS[:], ybd[:, tok0:tok0 + TB5].rearrange("(c p) t -> p c t", p=128), w=["ybS"],
                      sem="ybS")
                for m in range(KC):
                    s = mc % 2
                    mc += 1
                    P.dma("pool", wa[s][:], w_of[:, m * 128:(m + 1) * 128].rearrange("(c p) n -> p c n", p=128),
                          w=[("wa", s)], sem=("wa", s))
                    P.dma("pool", wd[s][:], w_od[:, m * 128:(m + 1) * 128].rearrange("(c p) n -> p c n", p=128),
                          w=[("wd", s)], sem=("wd", s))
                    P.dma("sp", gA[s][:], gAd[m * 128:(m + 1) * 128, tok0:tok0 + TB5], w=[("gA", s)], sem=("gA", s))
                    P.dma("sp", gB[s][:], gBd[m * 128:(m + 1) * 128, tok0:tok0 + TB5], w=[("gB", s)], sem=("gB", s))

                    def mma(h, s=s):
                        n_ = FW // 128
                        for c_ in range(n_):
                            i = h.matmul(pa[s][:], lhsT=wa[s][:, c_, :], rhs=yaS[:, c_, :], start=(c_ == 0),
                                         stop=(c_ == n_ - 1))
                        return i

                    def mmd(h, s=s):
                        n_ = DW // 128
                        for c_ in range(n_):
                            i = h.matmul(pb_[s][:], lhsT=wd[s][:, c_, :], rhs=ybS[:, c_, :], start=(c_ == 0),
                                         stop=(c_ == n_ - 1))
                        return i
                    P.op("pe", mma, r=[("wa", s), "yaS"], w=[("pa", s)])
                    P.op("pe", mmd, r=[("wd", s), "ybS"], w=[("pb", s)])
                    P.op("dve", lambda h, s=s: h.tensor_tensor(out=u1[s][:], in0=pa[s][:], in1=gA[s][:], op=ALU.mult),
                         r=[("pa", s), ("gA", s)], w=[("u1", s)])
                    P.op("dve", lambda h, s=s: h.tensor_tensor(out=u2[s][:], in0=pb_[s][:], in1=gB[s][:], op=ALU.mult),
                         r=[("pb", s), ("gB", s)], w=[("u2", s)])
                    P.op("dve", lambda h, s=s, m=m: h.tensor_tensor(out=mT[:, m, :], in0=u1[s][:], in1=u2[s][:],
                                                                    op=ALU.add),
                         r=[("u1", s), ("u2", s)], w=[("mT", m)])
                mk = [("mT", m) for m in range(KC)]
                def xload5(n, tok0=tok0, xc=xc):
                    xsl = (xc + n) % 4
                    P.dma("sp", xo[xsl][:], xs[tok0:tok0 + TB5, n * 256:(n + 1) * 256]
                          .rearrange("(r p) n -> p r n", p=128), w=[("xo", xsl)], sem=("xol", xsl))
                NN = D // 256
                for n in range(min(2, NN)):
                    xload5(n)
                for n in range(NN):
                    s = oc % 2
                    oc += 1
                    P.dma("pool", wo[s][:], w_o[:, n * 256:(n + 1) * 256].rearrange("(k p) n -> p k n", p=128),
                          w=[("wo", s)], sem=("wo", s))
                    xsl = (xc + n) % 4
                    dk = ("xs5", blk, n)
                    if n + 2 < NN:
                        xload5(n + 2)
                    for rb in range(TB5 // 128):
                        pb2 = pcn % 2
                        pcn += 1

                        def mmo(h, s=s, rb=rb, dstp=po[pb2]):
                            for k in range(KC):
                                i = h.matmul(dstp[:, 0:256], lhsT=mT[:, k, rb * 128:(rb + 1) * 128], rhs=wo[s][:, k, :],
                                             start=(k == 0), stop=(k == KC - 1))
                            return i
                        P.op("pe", mmo, r=[("wo", s)] + mk, w=[("po", pb2)])
                        P.op("dve", lambda h, pb2=pb2, n=n: h.tensor_tensor(
                            out=tmp[pb2][:], in0=po[pb2][:, 0:256], in1=gbt[:, n * 256:(n + 1) * 256], op=ALU.mult),
                            r=[("po", pb2), "gbt"], w=[("tmp", pb2)])
                        P.op("dve", lambda h, pb2=pb2, xsl=xsl, rb=rb: h.tensor_tensor(
                            out=xo[xsl][:, rb, :], in0=tmp[pb2][:], in1=xo[xsl][:, rb, :], op=ALU.add),
                            r=[("tmp", pb2), ("xo", xsl)], w=[("xo", xsl)])
                    P.dma("sp", xs[tok0:tok0 + TB5, n * 256:(n + 1) * 256].rearrange("(r p) n -> p r n", p=128),
                          xo[xsl][:], r=[("xo", xsl)], w=[dk], sem=("xos", xsl))
                xc += NN
            P.barrier()
            P.emit()

        if UPTO < 6:
            return nc
        ffn_stage(P, "f2", OWN, xs, f2_in, f2_out, AB[:, 4, :], AB[:, 5, :], 8)

        with ExitStack() as st:
            Af = sb(st, "Af", [128, D], F32)
            Bf = sb(st, "Bf", [128, D], F32)
            nf = sb(st, "nf", [128, D], F32)
            xt = [sb(st, "xt%d" % i, [128, D], F32) for i in range(2)]
            junk = sb(st, "junk", [128, D], BF16)
            ss = [sb(st, "ss%d" % i, [128, 1], F32) for i in range(2)]
            rs = [sb(st, "rs%d" % i, [128, 1], F32) for i in range(2)]
            P.dma("sp", Af[:], modv[10:11, :].partition_broadcast(128), w=["Af"], sem="Af")
            P.dma("sp", Bf[:], modv[9:10, :].partition_broadcast(128), w=["Bf"], sem="Bf")
            P.dma("sp", nf[:], norms[3:4, :].partition_broadcast(128), w=["nf"], sem="nf")
            P.op("dve", lambda h: h.scalar_tensor_tensor(out=Af[:], in0=Af[:], scalar=1.0, in1=nf[:], op0=ALU.add,
                                                         op1=ALU.mult), r=["Af", "nf"], w=["Af"])
            for rb in range(OB):
                s = rb % 2
                P.dma("sp", xt[s][:], xs[rb * 128:(rb + 1) * 128, :], w=[("xt", s)], sem=("xtl", s))
                P.op("dve", lambda h, s=s: h.memset(ss[s][:], 0.0), w=[("ss", s)])
                P.op("act", lambda h, s=s: h.activation(out=junk[:], in_=xt[s][:], func=AF.Square, accum_out=ss[s][:]),
                     r=[("xt", s)], w=["junk", ("ss", s)])
                P.op("dve", lambda h, s=s: h.tensor_scalar(out=rs[s][:], in0=ss[s][:], scalar1=1.0 / D, scalar2=EPS,
                                                           op0=ALU.mult, op1=ALU.add), r=[("ss", s)], w=[("rs", s)])
                P.op("act", lambda h, s=s: h.sqrt(out=rs[s][:], in_=rs[s][:]), r=[("rs", s)], w=[("rs", s)])
                P.op("dve", lambda h, s=s: h.reciprocal(out=rs[s][:], in_=rs[s][:]), r=[("rs", s)], w=[("rs", s)])
                P.op("dve", lambda h, s=s: h.scalar_tensor_tensor(out=xt[s][:], in0=xt[s][:], scalar=rs[s][:, 0:1],
                                                                  in1=Af[:], op0=ALU.mult, op1=ALU.mult),
                     r=[("xt", s), ("rs", s), "Af"], w=[("xt", s)])
                P.op("dve", lambda h, s=s: h.tensor_tensor(out=xt[s][:], in0=xt[s][:], in1=Bf[:], op=ALU.add),
                     r=[("xt", s), "Bf"], w=[("xt", s)])
                P.dma("sp", out[rb * 128:(rb + 1) * 128, :], xt[s][:], r=[("xt", s)], sem=("xts", s))
            P.barrier()
            P.emit()
    return nc


def make_in_maps(cfg, inputs):
    D, S, B = cfg["D"], cfg["S"], cfg["B"]
    OWN = S // 2
    g = lambda k: np.asarray(inputs[k])
    cbh, cfh = host_consts()
    shared = {
        "constb": cbh, "constf": cfh,
        "ada_w": g("ada_w")[0], "ada_b": g("ada_b")[0][None, :],
        "final_ada_w": g("final_ada_w"), "final_ada_b": g("final_ada_b")[None, :],
        "norms": np.stack([g("norm_ffn1")[0], g("norm_mix")[0], g("norm_ffn2")[0], g("norm_final")]),
        "ffn1_w_in": g("ffn1_w_in")[0], "ffn1_w_out": g("ffn1_w_out")[0],
        "ffn2_w_in": g("ffn2_w_in")[0], "ffn2_w_out": g("ffn2_w_out")[0],
        "w_in": g("w_in")[0], "b_forget": g("b_forget")[0][:, None], "b_gate": g("b_gate")[0][None, :],
        "diff_lambda": g("diff_lambda")[0].reshape(1, 512), "diff_subln": g("diff_subln")[0][None, :],
        "w_o_fox": g("w_o_fox")[0], "w_o_diff": g("w_o_diff")[0], "w_out": g("w_out")[0],
    }
    shared = {k: np.ascontiguousarray(v) for k, v in shared.items()}
    x, c, pos = g("x"), g("c"), g("positions")
    maps = []
    for core in range(2 * B):
        b, r = core // 2, core % 2
        order = np.concatenate([np.arange(r * OWN, (r + 1) * OWN), np.arange((1 - r) * OWN, (2 - r) * OWN)])
        m = dict(shared)
        m["x"] = np.ascontiguousarray(x[b][order])
        m["c"] = np.ascontiguousarray(c[b][None, :])
        m["pos"] = np.ascontiguousarray(pos[b][order][None, :].astype(np.int32))
        m["flag"] = np.full((128, 1), 0.0 if r == 1 else -30000.0, np.float32)
        maps.append(m)
    return maps


def run(cfg, inputs):
    nc = build(cfg)
    maps = make_in_maps(cfg, inputs)
    ncores = 2 * cfg["B"]
    res = run_bass_kernel_spmd(nc, maps, core_ids=list(range(ncores)))
    if cfg.get("dbgout"):
        return res.results
    D, S, B = cfg["D"], cfg["S"], cfg["B"]
    OWN = S // 2
    outp = np.empty((B, S, D), np.float32)
    for core in range(ncores):
        b, r = core // 2, core % 2
        outp[b, r * OWN:(r + 1) * OWN] = res.results[core]["out"]
    return outp


def kernel(**inputs):
    return run(FULL, inputs)
```
